# Optimizing a Trainium2 kernel written in Bass

```python
import math
import jax
import jax.numpy as jnp
from jax import lax
import numpy as np

D_MODEL = 1024
BATCH = 8
SEQ = 2048
DEPTH = 2

GRID_W = 64
CTX_LEN = 256
EPS = 1e-6
F32 = jnp.float32

S5_WIDTH = D_MODEL
S5_GROUP = 16
S5_GROUPS = S5_WIDTH // S5_GROUP
S5_STATE = 64

DN_HEAD_DIM = 128
DN_HEADS = D_MODEL // DN_HEAD_DIM
DN_WIDTH = DN_HEADS * DN_HEAD_DIM
DN_CONV = 3
DN_CHUNK = 64

CV_WIDTH = D_MODEL
CV_TAPS = 31

FFN_HIDDEN = ((8 * D_MODEL // 3 + 255) // 256) * 256
FFN_CONV = 3

N_BRANCH = 3
N_MOD = 6

COL_S5 = 0
COL_QKV = COL_S5 + S5_WIDTH
COL_BETA = COL_QKV + 3 * DN_WIDTH
COL_DECAY = COL_BETA + 2 * DN_HEADS
N_STATE_COLS = COL_DECAY + 2 * DN_HEADS
COL_Z = N_STATE_COLS
COL_CV = COL_Z + DN_WIDTH
COL_GATE = COL_CV + 2 * CV_WIDTH
N_IN_COLS = COL_GATE + N_BRANCH * D_MODEL

kernel_name = "hybrid_s5_deltanet_conformer_prefix_dit"


def rmsnorm(x, w):
    xf = x.astype(F32)
    xf = xf * lax.rsqrt(jnp.mean(xf * xf, axis=-1, keepdims=True) + EPS)
    return (xf * w.astype(F32)).astype(x.dtype)


def layernorm(x, w, b):
    xf = x.astype(F32)
    mu = jnp.mean(xf, axis=-1, keepdims=True)
    var = jnp.mean(jnp.square(xf - mu), axis=-1, keepdims=True)
    y = (xf - mu) * lax.rsqrt(var + EPS)
    return (y * w.astype(F32) + b.astype(F32)).astype(x.dtype)


def dwconv1d(x, w):
    k = w.shape[0]
    return lax.conv_general_dilated(
        x, w[:, None, :].astype(x.dtype), (1,), [(k // 2, k // 2)],
        dimension_numbers=("NWC", "WIO", "NWC"), feature_group_count=x.shape[-1])


def dwconv2d(x, w):
    kh, kw = w.shape[:2]
    return lax.conv_general_dilated(
        x, w[:, :, None, :].astype(x.dtype), (1, 1), [(kh // 2, kh // 2), (kw // 2, kw // 2)],
        dimension_numbers=("NHWC", "HWIO", "NHWC"), feature_group_count=x.shape[-1])


def l2norm(x):
    return x * lax.rsqrt(jnp.sum(x * x, axis=-1, keepdims=True) + EPS)


def s5_discretise(a_re, a_im, log_dt, b_re, b_im):
    lam = lax.complex(a_re.astype(F32), a_im.astype(F32))
    dt = jnp.exp(log_dt.astype(F32))[:, None]
    a_bar = jnp.exp(lam * dt)
    b = lax.complex(b_re.astype(F32), b_im.astype(F32))
    b_bar = ((a_bar - 1.0) / lam)[..., None] * b
    return a_bar, b_bar


def _linear_combine(e1, e2):
    a1, b1 = e1
    a2, b2 = e2
    return a1 * a2, a2 * b1 + b2


def s5_scan(a_bar, bu, h0, reverse):
    if h0 is not None:
        start = -1 if reverse else 0
        bu = bu.at[:, start].add(a_bar * h0)
    a = jnp.broadcast_to(a_bar, (1,) + bu.shape[1:])
    _, h = lax.associative_scan(_linear_combine, (a, bu), reverse=reverse, axis=1)
    return h


def s5_scans(u, p, init):
    bsz, n, _ = u.shape
    ug = u.astype(F32).reshape(bsz, n, S5_GROUPS, S5_GROUP).astype(jnp.complex64)
    hs = []
    for d in range(2):
        a_bar, b_bar = s5_discretise(p["s5_a_re"][d], p["s5_a_im"][d], p["s5_log_dt"][d],
                                     p["s5_b_re"][d], p["s5_b_im"][d])
        bu = jnp.einsum("blgc,gpc->blgp", ug, b_bar)
        hs.append(s5_scan(a_bar, bu, None if init is None else init[d], reverse=(d == 1)))
    return hs[0], hs[1]


def s5_readout(h, u, p):
    bsz, n, _ = u.shape
    y = (jnp.einsum("blgp,gcp->blgc", h.real, p["s5_c_re"].astype(F32))
         - jnp.einsum("blgp,gcp->blgc", h.imag, p["s5_c_im"].astype(F32)))
    y = y.reshape(bsz, n, S5_WIDTH) + p["s5_d"].astype(F32) * u.astype(F32)
    y = jax.nn.gelu(y).astype(u.dtype)
    return y * jax.nn.sigmoid(y @ p["s5_w_glu"])


def dn_inputs(qkv, beta_logits, decay_logits, p):
    bsz, n, _ = qkv.shape
    qkv = jax.nn.silu(dwconv1d(qkv, p["dn_conv_w"])).astype(F32)
    qkv = qkv.reshape(bsz, n, 3, DN_HEADS, DN_HEAD_DIM)
    q = l2norm(qkv[:, :, 0]) * (DN_HEAD_DIM ** -0.5)
    k = l2norm(qkv[:, :, 1])
    v = qkv[:, :, 2]
    beta = jax.nn.sigmoid(beta_logits.astype(F32)).reshape(bsz, n, 2, DN_HEADS)
    g = -jnp.exp(p["dn_a_log"].astype(F32)) * jax.nn.softplus(
        decay_logits.astype(F32).reshape(bsz, n, 2, DN_HEADS) + p["dn_dt_bias"].astype(F32))
    return q, k, v, beta, g


def gated_delta_chunked(q, k, v, beta, g, s0, with_output):
    bsz, n, h, dk = k.shape
    dv = v.shape[-1]
    nc = n // DN_CHUNK

    def chunks(t):
        return jnp.transpose(t.reshape(bsz, nc, DN_CHUNK, h, -1), (1, 0, 3, 2, 4))

    q, k, v = chunks(q), chunks(k), chunks(v)
    beta = chunks(beta[..., None])[..., 0]
    g = jnp.cumsum(chunks(g[..., None])[..., 0], axis=-1)
    pos = jnp.arange(DN_CHUNK)
    incl = pos[:, None] >= pos[None, :]
    strict = pos[:, None] > pos[None, :]
    decay = jnp.exp(jnp.where(incl, g[..., :, None] - g[..., None, :], -jnp.inf))
    kb = k * beta[..., None]
    m = jnp.where(strict, jnp.einsum("nbhid,nbhjd->nbhij", kb, k) * decay, 0.0)
    eye = jnp.eye(DN_CHUNK, dtype=F32)
    t = lax.linalg.triangular_solve(m + eye, jnp.broadcast_to(eye, m.shape),
                                    left_side=True, lower=True, unit_diagonal=True)
    u = t @ (v * beta[..., None])
    w = t @ (kb * jnp.exp(g)[..., None])
    g_last = g[..., -1:]
    k_tail = k * jnp.exp(g_last - g)[..., None]
    state_decay = jnp.exp(g_last)[..., None]
    s0 = jnp.zeros((bsz, h, dk, dv), F32) if s0 is None else s0

    def step(s, xs):
        u_i, w_i, kt_i, sd_i = xs[:4]
        v_new = u_i - w_i @ s
        s_new = s * sd_i + jnp.swapaxes(kt_i, -1, -2) @ v_new
        if not with_output:
            return s_new, None
        qg_i, a_i = xs[4:]
        return s_new, qg_i @ s + a_i @ v_new

    xs = (u, w, k_tail, state_decay)
    if with_output:
        xs = xs + (q * jnp.exp(g)[..., None], jnp.einsum("nbhid,nbhjd->nbhij", q, k) * decay)
    s_final, o = lax.scan(step, s0, xs)
    if not with_output:
        return None, s_final
    o = jnp.transpose(o, (1, 0, 3, 2, 4)).reshape(bsz, n, h, dv)
    return o, s_final


def dn_bidirectional(q, k, v, beta, g, init, with_output):
    s_f0, s_b0 = (None, None) if init is None else init
    o_f, s_f = gated_delta_chunked(q, k, v, beta[:, :, 0], g[:, :, 0], s_f0, with_output)
    flip = lambda t: jnp.flip(t, axis=1)
    o_b, s_b = gated_delta_chunked(flip(q), flip(k), flip(v), flip(beta[:, :, 1]),
                                   flip(g[:, :, 1]), s_b0, with_output)
    o = o_f + flip(o_b) if with_output else None
    return o, (s_f, s_b)


def dn_readout(o, z, p):
    bsz, n = o.shape[:2]
    o = o * lax.rsqrt(jnp.mean(o * o, axis=-1, keepdims=True) + EPS) * p["dn_norm_w"].astype(F32)
    o = o.reshape(bsz, n, DN_WIDTH) * jax.nn.silu(z.astype(F32))
    return o.astype(z.dtype)


def conv_module(glu_in, p):
    a, gate = jnp.split(glu_in, 2, axis=-1)
    y = a * jax.nn.sigmoid(gate)
    y = dwconv1d(y, p["cv_dw_w"]) + p["cv_dw_b"]
    return jax.nn.silu(layernorm(y, p["cv_ln_w"], p["cv_ln_b"]))


def recurrent_mixers(ps, p, init, with_output):
    u = ps[..., COL_S5:COL_QKV]
    s5_init, dn_init = (None, None) if init is None else init
    h_f, h_b = s5_scans(u, p, s5_init)
    s5_final = (h_f[:, -1], h_b[:, 0])
    q, k, v, beta, g = dn_inputs(ps[..., COL_QKV:COL_BETA], ps[..., COL_BETA:COL_DECAY],
                                 ps[..., COL_DECAY:N_STATE_COLS], p)
    o, dn_final = dn_bidirectional(q, k, v, beta, g, dn_init, with_output)
    s5_out = s5_readout(h_f + h_b, u, p) if with_output else None
    return s5_out, o, (s5_final, dn_final)


def token_mixer(h, p, init):
    proj = h @ p["w_in"]
    s5_out, o, final = recurrent_mixers(proj[..., :N_STATE_COLS], p, init, True)
    dn_out = dn_readout(o, proj[..., COL_Z:COL_CV], p)
    cv_out = conv_module(proj[..., COL_CV:COL_GATE], p)
    gates = jax.nn.sigmoid(proj[..., COL_GATE:].astype(F32)).astype(h.dtype)
    gates = gates.reshape(h.shape[:2] + (N_BRANCH, D_MODEL))
    merged = (gates[..., 0, :] * (s5_out @ p["w_br_s5"])
              + gates[..., 1, :] * (dn_out @ p["w_br_dn"])
              + gates[..., 2, :] * (cv_out @ p["w_br_cv"]))
    return merged @ p["w_out"], final


def context_states(hc, p):
    proj = hc @ p["w_in"][:, :N_STATE_COLS]
    _, _, final = recurrent_mixers(proj, p, None, False)
    return final


def conv_ffn(h, p, grid):
    a, v = jnp.split(h @ p["ffn_w_up"], 2, axis=-1)
    if grid:
        bsz, n, f = a.shape
        rows = n // GRID_W
        a = dwconv2d(a.reshape(bsz, rows, GRID_W, f), p["ffn_dw_w"]).reshape(bsz, n, f)
    else:
        a = dwconv1d(a, p["ffn_dw_w"][FFN_CONV // 2])
    return (jax.nn.silu(a + p["ffn_dw_b"]) * v) @ p["ffn_w_down"]


def modulation(cvec, p, n_chunks):
    m = jax.nn.silu(cvec) @ p["ada_w"][:, :n_chunks * D_MODEL] + p["ada_b"][:n_chunks * D_MODEL]
    return [m[:, None, i * D_MODEL:(i + 1) * D_MODEL] for i in range(n_chunks)]


def layer(x, xc, c, c_ctx, p, last):
    csh1, csc1, *ctx_rest = modulation(c_ctx[None, :], p, 2 if last else N_MOD)
    hc = rmsnorm(xc, p["norm1_w"]) * (1 + csc1) + csh1
    if last:
        ctx_state = context_states(hc, p)
    else:
        cg1, csh2, csc2, cg2 = ctx_rest
        yc, ctx_state = token_mixer(hc, p, None)
        xc = xc + cg1 * yc
        hc = rmsnorm(xc, p["norm2_w"]) * (1 + csc2) + csh2
        xc = xc + cg2 * conv_ffn(hc, p, grid=False)
    sh1, sc1, g1, sh2, sc2, g2 = modulation(c, p, N_MOD)
    h = rmsnorm(x, p["norm1_w"]) * (1 + sc1) + sh1
    y, _ = token_mixer(h, p, ctx_state)
    x = x + g1 * y
    h = rmsnorm(x, p["norm2_w"]) * (1 + sc2) + sh2
    x = x + g2 * conv_ffn(h, p, grid=True)
    return x, xc


def setup_inputs(seed: int = 0) -> dict:
    key = jax.random.key(seed)
    keys = iter(jax.random.split(key, 48))

    def nrm(shape, scale):
        return jax.random.normal(next(keys), shape, F32) * scale

    def unif(shape, lo, hi):
        return jax.random.uniform(next(keys), shape, F32, lo, hi)

    L, G, P = DEPTH, S5_GROUPS, S5_STATE
    n_idx = jnp.arange(P, dtype=F32)
    dn_dt = jnp.exp(unif((L, 2, DN_HEADS), math.log(1e-3), math.log(1e-1)))
    return {
        "x": nrm((BATCH, SEQ, D_MODEL), 1.0),
        "c": nrm((BATCH, D_MODEL), 1.0),
        "ctx": nrm((BATCH, CTX_LEN, D_MODEL), 1.0),
        "c_ctx": nrm((D_MODEL,), 1.0),
        "ada_w": nrm((L, D_MODEL, N_MOD * D_MODEL), 0.5 * D_MODEL ** -0.5),
        "ada_b": nrm((L, N_MOD * D_MODEL), 0.02),
        "norm1_w": 1.0 + nrm((L, D_MODEL), 0.02),
        "norm2_w": 1.0 + nrm((L, D_MODEL), 0.02),
        "w_in": nrm((L, D_MODEL, N_IN_COLS), D_MODEL ** -0.5),
        "s5_a_re": -0.5 + nrm((L, 2, G, P), 0.01),
        "s5_a_im": math.pi * n_idx + nrm((L, 2, G, P), 0.01),
        "s5_log_dt": unif((L, 2, G), math.log(1e-3), math.log(1e-1)),
        "s5_b_re": nrm((L, 2, G, P, S5_GROUP), (2 * S5_GROUP) ** -0.5),
        "s5_b_im": nrm((L, 2, G, P, S5_GROUP), (2 * S5_GROUP) ** -0.5),
        "s5_c_re": nrm((L, G, S5_GROUP, P), P ** -0.5),
        "s5_c_im": nrm((L, G, S5_GROUP, P), P ** -0.5),
        "s5_d": nrm((L, S5_WIDTH), 1.0),
        "s5_w_glu": nrm((L, S5_WIDTH, S5_WIDTH), S5_WIDTH ** -0.5),
        "dn_conv_w": nrm((L, DN_CONV, 3 * DN_WIDTH), DN_CONV ** -0.5),
        "dn_a_log": jnp.log(unif((L, 2, DN_HEADS), 1.0, 16.0)),
        "dn_dt_bias": dn_dt + jnp.log(-jnp.expm1(-dn_dt)),
        "dn_norm_w": 1.0 + nrm((L, DN_HEAD_DIM), 0.02),
        "cv_dw_w": nrm((L, CV_TAPS, CV_WIDTH), CV_TAPS ** -0.5),
        "cv_dw_b": nrm((L, CV_WIDTH), 0.02),
        "cv_ln_w": 1.0 + nrm((L, CV_WIDTH), 0.02),
        "cv_ln_b": nrm((L, CV_WIDTH), 0.02),
        "w_br_s5": nrm((L, S5_WIDTH, D_MODEL), S5_WIDTH ** -0.5),
        "w_br_dn": nrm((L, DN_WIDTH, D_MODEL), DN_WIDTH ** -0.5),
        "w_br_cv": nrm((L, CV_WIDTH, D_MODEL), CV_WIDTH ** -0.5),
        "w_out": nrm((L, D_MODEL, D_MODEL), D_MODEL ** -0.5),
        "ffn_w_up": nrm((L, D_MODEL, 2 * FFN_HIDDEN), D_MODEL ** -0.5),
        "ffn_dw_w": nrm((L, FFN_CONV, FFN_CONV, FFN_HIDDEN), 1.0 / FFN_CONV),
        "ffn_dw_b": nrm((L, FFN_HIDDEN), 0.02),
        "ffn_w_down": nrm((L, FFN_HIDDEN, D_MODEL), FFN_HIDDEN ** -0.5),
        "final_norm_w": 1.0 + nrm((D_MODEL,), 0.02),
    }


def reference(x, c, ctx, c_ctx, ada_w, ada_b, norm1_w, norm2_w, w_in,
              s5_a_re, s5_a_im, s5_log_dt, s5_b_re, s5_b_im, s5_c_re, s5_c_im, s5_d, s5_w_glu,
              dn_conv_w, dn_a_log, dn_dt_bias, dn_norm_w,
              cv_dw_w, cv_dw_b, cv_ln_w, cv_ln_b,
              w_br_s5, w_br_dn, w_br_cv, w_out,
              ffn_w_up, ffn_dw_w, ffn_dw_b, ffn_w_down, final_norm_w):
    xc = ctx
    for i in range(DEPTH):
        p = {
            "ada_w": ada_w[i], "ada_b": ada_b[i], "norm1_w": norm1_w[i], "norm2_w": norm2_w[i],
            "w_in": w_in[i],
            "s5_a_re": s5_a_re[i], "s5_a_im": s5_a_im[i], "s5_log_dt": s5_log_dt[i],
            "s5_b_re": s5_b_re[i], "s5_b_im": s5_b_im[i], "s5_c_re": s5_c_re[i], "s5_c_im": s5_c_im[i],
            "s5_d": s5_d[i], "s5_w_glu": s5_w_glu[i],
            "dn_conv_w": dn_conv_w[i], "dn_a_log": dn_a_log[i], "dn_dt_bias": dn_dt_bias[i],
            "dn_norm_w": dn_norm_w[i],
            "cv_dw_w": cv_dw_w[i], "cv_dw_b": cv_dw_b[i], "cv_ln_w": cv_ln_w[i], "cv_ln_b": cv_ln_b[i],
            "w_br_s5": w_br_s5[i], "w_br_dn": w_br_dn[i], "w_br_cv": w_br_cv[i], "w_out": w_out[i],
            "ffn_w_up": ffn_w_up[i], "ffn_dw_w": ffn_dw_w[i], "ffn_dw_b": ffn_dw_b[i],
            "ffn_w_down": ffn_w_down[i],
        }
        x, xc = layer(x, xc, c, c_ctx, p, last=(i == DEPTH - 1))
    return rmsnorm(x, final_norm_w)
```

```python
import numpy as np
import concourse.bass as bass
import concourse.mybir as mybir
from concourse.ap import AP
from concourse.bass_utils import run_bass_kernel_spmd

F32 = mybir.dt.float32
BF16 = mybir.dt.bfloat16
I32 = mybir.dt.int32
AF = mybir.ActivationFunctionType
ALU = mybir.AluOpType

ENGS = ("pe", "dve", "act", "pool", "sp")
DMA_RING = {"sp": 12, "act": 6, "pool": 10}


class Prog:
    def __init__(self, nc):
        self.nc = nc
        self.q = {e: [] for e in ENGS}
        self.cnt = {e: 0 for e in ENGS}
        self.last_w = {}
        self.readers = {}
        self.waited = {e: {} for e in ENGS}
        self.dma_n = {k: 0 for k in DMA_RING}
        self.dma_uses = {}
        self.sems = {}
        self._ctx = []
        self.final_tokens = []
        for e in ENGS:
            self.sem("E:" + e)
        for qn, n in DMA_RING.items():
            for i in range(n):
                self.sem("D:%s:%d" % (qn, i))

    def sem(self, key):
        if key not in self.sems:
            g = self.nc.semaphore(key.replace(":", "_"))
            h = g.__enter__()
            self._ctx.append(g)
            self.sems[key] = h
        return self.sems[key]

    def sbuf(self, name, shape, dtype):
        self.uid = getattr(self, "uid", 0) + 1
        g = self.nc.sbuf_tensor("%s_u%d" % (name, self.uid), list(shape), dtype)
        t = g.__enter__()
        self._ctx.append(g)
        return t

    def psum(self, name, shape, dtype=F32):
        self.uid = getattr(self, "uid", 0) + 1
        g = self.nc.psum_tensor("%s_u%d" % (name, self.uid), list(shape), dtype)
        t = g.__enter__()
        self._ctx.append(g)
        return t

    def _deps(self, eng, reads, writes):
        need = {}
        for r in reads:
            t = self.last_w.get(r)
            if t is not None:
                need[t[0]] = max(need.get(t[0], 0), t[1])
        for w in writes:
            t = self.last_w.get(w)
            if t is not None:
                need[t[0]] = max(need.get(t[0], 0), t[1])
            for t in self.readers.get(w, ()):
                need[t[0]] = max(need.get(t[0], 0), t[1])
        out = []
        wd = self.waited[eng]
        for k, v in need.items():
            if wd.get(k, 0) >= v:
                continue
            wd[k] = v
            out.append((k, v))
        return out

    def _commit(self, tok, reads, writes):
        for r in reads:
            self.readers.setdefault(r, []).append(tok)
        for w in writes:
            self.last_w[w] = tok
            self.readers[w] = []

    def op(self, eng, fn, reads=(), writes=()):
        reads = list(reads)
        writes = list(writes)
        waits = self._deps(eng, reads, writes)
        self.cnt[eng] += 1
        tok = ("E:" + eng, self.cnt[eng])
        self._commit(tok, reads, writes)
        self.q[eng].append((fn, waits, ("E:" + eng, 1)))
        return tok

    def dma(self, queue, out, in_, reads=(), writes=(), **kw):
        reads = list(reads)
        writes = list(writes)
        n = self.dma_n[queue]
        self.dma_n[queue] += 1
        slot = n % DMA_RING[queue]
        key = "D:%s:%d" % (queue, slot)
        u = self.dma_uses.get(key, 0) + 1
        self.dma_uses[key] = u
        waits = self._deps(queue, reads, writes)
        if u > 1:
            wd = self.waited[queue]
            if wd.get(key, 0) < 16 * (u - 1):
                wd[key] = 16 * (u - 1)
                waits.append((key, 16 * (u - 1)))
        tok = (key, 16 * u)
        self._commit(tok, reads, writes)

        def fn(e, out=out, in_=in_, kw=kw):
            return e.dma_start(out=out, in_=in_, **kw)

        self.q[queue].append((fn, waits, (key, 16)))
        return tok

    def finish(self, tokens):
        self.final_tokens.extend(tokens)

    def barrier(self):
        allk = {}
        for e in ENGS:
            allk["E:" + e] = self.cnt[e]
        for key, u in self.dma_uses.items():
            allk[key] = 16 * u
        for e in ENGS:
            waits = []
            for kk, v in allk.items():
                if v > 0 and self.waited[e].get(kk, 0) < v:
                    self.waited[e][kk] = v
                    waits.append((kk, v))
            self.q[e].append((None, waits, None))

    def scope_begin(self):
        return len(self._ctx)

    def scope_end(self, mark):
        self.barrier()
        while len(self._ctx) > mark:
            g = self._ctx.pop()
            g.__exit__(None, None, None)

    def emit(self):
        nc = self.nc
        for e in ENGS:
            self.sem("E:" + e)
        for k in list(self.dma_uses):
            self.sem(k)
        engmap = {"pe": "tensor", "dve": "vector", "act": "scalar", "pool": "gpsimd", "sp": "sync"}
        fin = {}
        for t in self.final_tokens:
            fin[t[0]] = max(fin.get(t[0], 0), t[1])
        with nc.Block() as block:
            for e in ENGS:
                items = self.q[e]
                extra = list(fin.items()) if e == "sp" else []

                def body(eng, items=items, extra=extra):
                    for fn, waits, inc in items:
                        for k, v in waits:
                            eng.wait_ge(self.sems[k], v)
                        if fn is None:
                            continue
                        ins = fn(eng)
                        ins.then_inc(self.sems[inc[0]], inc[1])
                    for k, v in extra:
                        eng.wait_ge(self.sems[k], v)

                getattr(block, engmap[e])(body)

    def close(self):
        for g in reversed(self._ctx):
            g.__exit__(None, None, None)
        self._ctx = []


D = 1024
KT = 8
NT = 2304
NX = 2048
NCX = 256
TT = [(0, 512), (512, 512), (1024, 512), (1536, 512), (2048, 256)]
SEGS = [(0, 2048), (2048, 256)]
DEPTH = 2
N_IN = 10272
COL_QKV = 1024
COL_BETA = 4096
COL_DECAY = 4112
COL_Z = 4128
COL_CV = 5152
COL_GATE = 7200
FH = 2816
FKT = 22
EPS = 1e-6
TWO_PI_LO = 6.283185


def sub(ap, p0, pn, dims, off=0):
    a = ap.ap
    pstep = a[0][0]
    return AP(ap.tensor, ap.offset + p0 * pstep + off, [[pstep, pn]] + [list(d) for d in dims])


class K:
    pass


def build(depth=DEPTH, debug=(), stop_after=None, inject=()):
    nc = bass.Bass("TRN2", target_bir_lowering=False)
    P = Prog(nc)
    k = K()
    k.P = P
    k.nc = nc

    def din(name, shape, dtype=F32):
        return nc.dram_tensor(name, list(shape), dtype, kind="ExternalInput").ap()

    def dscr(name, shape, dtype=F32):
        kind = "ExternalOutput" if name in debug else ("ExternalInput" if name in inject else "Internal")
        return nc.dram_tensor(name, list(shape), dtype, kind=kind).ap()

    xT = din("xT", [D, NT])
    cvec = din("cvec", [128, 16])
    ada_w = din("ada_w", [DEPTH, D, 6 * D])
    ada_bT = din("ada_bT", [DEPTH, 128, 48])
    vec8 = din("vec8", [DEPTH, 128, 6, 8])
    fnw = din("fnw", [128, 8])
    w_in = din("w_in", [DEPTH, D, N_IN])
    iota_in = din("iota", [128, NT])
    cv_dw_wT = din("cv_dw_wT", [DEPTH, 128, 8, 31])
    s5_aT = din("s5_aT", [DEPTH, 3, 128, 64])
    s5_kp = din("s5_kp", [DEPTH, 8, 3, 128, 128])
    s5_bT = din("s5_bT", [DEPTH, 8, 2, 128, 128])
    s5_cT = din("s5_cT", [DEPTH, 2, 128, D])
    s5_w_glu = din("s5_w_glu", [DEPTH, D, D])
    gmask_in = din("gmask", [128, 8])
    dn_conv_wT = din("dn_conv_wT", [DEPTH, 128, 24, 3])
    dnp40 = din("dnp40", [DEPTH, 40, 2])
    dn_nw = din("dn_nw", [DEPTH, 128, 1])
    sel2_in = din("sel2", [40, 16, 128])
    cmask_in = din("cmask", [40, NT])
    tri_in = din("tri", [128, 8, 128])
    negm_in = din("negm", [128, 2, 128])
    ident_in = din("ident", [128, 128])
    w_br = [din("w_br_s5", [DEPTH, D, D]), din("w_br_dn", [DEPTH, D, D]), din("w_br_cv", [DEPTH, D, D])]
    w_out = din("w_out", [DEPTH, D, D])
    ffn_w_up = din("ffn_w_up", [DEPTH, D, 2 * FH])
    ffn_w_down = din("ffn_w_down", [DEPTH, FH, D])
    ffn_dw_wT = din("ffn_dw_wT", [DEPTH, 128, FKT, 9])
    ffn_dw_bT = din("ffn_dw_bT", [DEPTH, 128, FKT])
    outT = nc.dram_tensor("outT", [D, NX], F32, kind="ExternalOutput").ap()

    S_X = dscr("S_X", [D, NT])
    S_H = dscr("S_H", [D, NT], BF16)
    S_P = dscr("S_P", [N_IN, NT])
    S_BR = [dscr("S_BR%d" % i, [D, NT], BF16) for i in range(3)]
    S_F = dscr("S_F", [2 * FH, NT])
    S_O = dscr("S_O", [D, NT])
    S_A = dscr("S_A", [FH, NT], BF16)

    ones32 = P.sbuf("ones32", [128, 128], F32)
    P.op("pool", lambda e: e.memset(ones32[:], 1.0), writes=["ones32"])
    modS = P.sbuf("modS", [128, 2, 48], F32)
    scl1 = P.sbuf("scl1", [128, 2, 8], F32)
    scl2 = P.sbuf("scl2", [128, 2, 8], F32)
    v8 = P.sbuf("v8", [128, 6, 8], F32)
    fnw_t = P.sbuf("fnw_t", [128, 2, 8], F32)
    sc_t = P.sbuf("sc_t", [128, 16], F32)
    ps_lin = [P.psum("ps_lin%d" % i, [128, 512], F32) for i in range(3)]
    k.ps_rr = 0

    def next_ps():
        i = k.ps_rr % 3
        k.ps_rr += 1
        return ps_lin[i], "ps_lin%d" % i

    for kt in range(8):
        P.dma("sp", S_X[kt * 128:(kt + 1) * 128, :], xT[kt * 128:(kt + 1) * 128, :], writes=["S_X"])
    P.dma("sp", sc_t[:], cvec, writes=["sc_t"])
    P.op("act", lambda e: e.activation(out=sc_t[:], in_=sc_t[:], func=AF.Silu), reads=["sc_t"], writes=["sc_t"])
    P.dma("sp", fnw_t[:, 0, :], fnw, writes=["fnw_t"])
    P.dma("sp", fnw_t[:, 1, :], fnw, writes=["fnw_t"])

    def modulation(l):
        mark = P.scope_begin()
        wa = [P.sbuf("adaw%d_%d" % (l, i), [128, 8, 512], BF16) for i in range(2)]
        scb = P.sbuf("scb%d" % l, [128, 16], BF16)
        P.op("act", lambda e: e.copy(out=scb[:], in_=sc_t[:]), reads=["sc_t"], writes=["scb"])
        bT = P.sbuf("adab%d" % l, [128, 48], F32)
        P.dma("sp", bT[:], ada_bT[l], writes=["adab"])
        P.dma("sp", v8[:], vec8[l], writes=["v8"])
        ps, psn = next_ps()
        for cg in range(12):
            w = wa[cg % 2]
            wn = "adaw%d" % (cg % 2)
            P.dma("pool", w[:], ada_w[l][:, cg * 512:(cg + 1) * 512].rearrange("(kt p) c -> p kt c", p=128), writes=[wn])
            for ci in range(4):
                ct = cg * 4 + ci

                def mm(e, w=w, ci=ci, ct=ct, ps=ps):
                    for kt in range(8):
                        ins = e.matmul(ps[:, ct * 2:ct * 2 + 2], lhsT=w[:, kt, ci * 128:(ci + 1) * 128],
                                       rhs=scb[:, kt * 2:kt * 2 + 2], start=(kt == 0), stop=(kt == 7))
                    return ins
                P.op("pe", mm, reads=[wn, "scb"], writes=[psn])
        for s in range(2):
            P.op("dve", lambda e, s=s: e.tensor_tensor(out=modS[:, s, :], in0=sub(ps[:], 0, 128, [(2, 48)], off=s), in1=bT[:], op=ALU.add),
                 reads=[psn, "adab"], writes=["modS"])
        for s in range(2):
            P.op("dve", lambda e, s=s: e.scalar_tensor_tensor(out=scl1[:, s, :], in0=modS[:, s, 8:16], scalar=1.0, in1=v8[:, 0, :], op0=ALU.add, op1=ALU.mult),
                 reads=["modS", "v8"], writes=["scl1"])
            P.op("dve", lambda e, s=s: e.scalar_tensor_tensor(out=scl2[:, s, :], in0=modS[:, s, 32:40], scalar=1.0, in1=v8[:, 1, :], op0=ALU.add, op1=ALU.mult),
                 reads=["modS", "v8"], writes=["scl2"])
        P.scope_end(mark)

    def norm_phase(tag, src, srcname, scale_tab, scale_name, shift_ap_fn, dst, dstname, out_bf16, tts):
        mark = P.scope_begin()
        xt = [P.sbuf("nx_%s%d" % (tag, i), [128, 8, 512], F32) for i in range(2)]
        sq = P.sbuf("nsq_" + tag, [128, 8, 512], F32)
        rs = P.sbuf("nrs_" + tag, [128, 512], F32)
        tmp = [P.sbuf("ntmp_%s%d" % (tag, i), [128, 512], F32) for i in range(2)]
        ho = [P.sbuf("nho_%s%d" % (tag, i), [128, 8, 512], BF16 if out_bf16 else F32) for i in range(2)]
        toks = []

        def n_load(ti):
            t0, tsz = tts[ti]
            P.dma("sp", xt[ti % 2][:, :, :tsz], src[:, t0:t0 + tsz].rearrange("(kt p) t -> p kt t", p=128), reads=[srcname], writes=["nx%d" % (ti % 2)])

        for ti, (t0, tsz) in enumerate(tts):
            seg = 0 if t0 < NX else 1
            x_ = xt[ti % 2]
            xn = "nx%d" % (ti % 2)
            h_ = ho[ti % 2]
            hn = "nho%d" % (ti % 2)
            if ti == 0:
                n_load(0)
            P.op("act", lambda e, x_=x_, tsz=tsz: e.activation(out=sq[:, :, :tsz], in_=x_[:, :, :tsz], func=AF.Square), reads=[xn], writes=["nsq"])
            ps, psn = next_ps()

            def mm(e, ps=ps, tsz=tsz):
                for kt in range(8):
                    ins = e.matmul(ps[:, :tsz], lhsT=ones32[:], rhs=sq[:, kt, :tsz], start=(kt == 0), stop=(kt == 7))
                return ins
            P.op("pe", mm, reads=["nsq", "ones32"], writes=[psn])
            P.op("act", lambda e, ps=ps, tsz=tsz: e.activation(out=rs[:, :tsz], in_=ps[:, :tsz], func=AF.Sqrt, scale=1.0 / D, bias=k.eps_t[:, 0:1]), reads=[psn, "eps_t"], writes=["nrs"])
            P.op("dve", lambda e, tsz=tsz: e.reciprocal(out=rs[:, :tsz], in_=rs[:, :tsz]), reads=["nrs"], writes=["nrs"])
            for kt in range(8):
                t_ = tmp[kt % 2]
                tn = "ntmp%d" % (kt % 2)
                P.op("dve", lambda e, t_=t_, x_=x_, kt=kt, tsz=tsz: e.tensor_tensor(out=t_[:, :tsz], in0=x_[:, kt, :tsz], in1=rs[:, :tsz], op=ALU.mult), reads=[xn, "nrs"], writes=[tn])
                if shift_ap_fn is not None:
                    P.op("act", lambda e, t_=t_, h_=h_, kt=kt, tsz=tsz, seg=seg: e.activation(out=h_[:, kt, :tsz], in_=t_[:, :tsz], func=AF.Identity, scale=scale_tab[:, seg, kt:kt + 1], bias=shift_ap_fn(seg, kt)),
                         reads=[tn, scale_name, "modS"], writes=[hn])
                else:
                    P.op("act", lambda e, t_=t_, h_=h_, kt=kt, tsz=tsz, seg=seg: e.activation(out=h_[:, kt, :tsz], in_=t_[:, :tsz], func=AF.Copy, scale=scale_tab[:, seg, kt:kt + 1]),
                         reads=[tn, scale_name], writes=[hn])
            if ti + 1 < len(tts):
                n_load(ti + 1)
            toks.append(P.dma("sp", dst[:, t0:t0 + tsz].rearrange("(kt p) t -> p kt t", p=128), h_[:, :, :tsz], reads=[hn], writes=[dstname]))
        P.scope_end(mark)
        return toks

    def linear_to_dram(tag, W, ncols, Ktiles, src, srcname, dst, dstname, tts=TT, ctx_skip_from=None):
        mark = P.scope_begin()
        hT = P.sbuf("lin_h_" + tag, [128, Ktiles, NT], BF16)
        for kt in range(Ktiles):
            P.dma("sp", hT[:, kt, :], src[kt * 128:(kt + 1) * 128, :], reads=[srcname], writes=["lin_h"])
        wb = [P.sbuf("lin_w_%s%d" % (tag, i), [128, Ktiles, 512], BF16) for i in range(2)]
        st = [P.sbuf("lin_st_%s%d" % (tag, i), [128, NT], F32) for i in range(2)]
        if len(tts) < len(TT):
            for i in range(2):
                P.op("pool", lambda e, i=i: e.memset(st[i][:], 0.0), writes=["lin_st%d" % i])
        ngrp = (ncols + 511) // 512
        cti = 0
        for g in range(ngrp):
            c0 = g * 512
            gsz = min(512, ncols - c0)
            w = wb[g % 2]
            wn = "lin_w%d" % (g % 2)
            P.dma("pool", w[:, :, :gsz], W[:, c0:c0 + gsz].rearrange("(kt p) c -> p kt c", p=128), writes=[wn])
            for ci in range((gsz + 127) // 128):
                csz = min(128, gsz - ci * 128)
                s_ = st[cti % 2]
                sn = "lin_st%d" % (cti % 2)
                tiles_ = tts
                if ctx_skip_from is not None and c0 + ci * 128 >= ctx_skip_from:
                    tiles_ = [t for t in tts if t[0] < NX]
                for ti, (t0, tsz) in enumerate(tiles_):
                    ps, psn = next_ps()

                    def mm(e, ps=ps, w=w, ci=ci, csz=csz, t0=t0, tsz=tsz):
                        for kt in range(Ktiles):
                            ins = e.matmul(ps[:csz, :tsz], lhsT=w[:, kt, ci * 128:ci * 128 + csz], rhs=hT[:, kt, t0:t0 + tsz], start=(kt == 0), stop=(kt == Ktiles - 1))
                        return ins
                    P.op("pe", mm, reads=[wn, "lin_h"], writes=[psn])
                    if ti % 2 == 0:
                        P.op("act", lambda e, ps=ps, s_=s_, csz=csz, t0=t0, tsz=tsz: e.copy(out=s_[:csz, t0:t0 + tsz], in_=ps[:csz, :tsz]), reads=[psn], writes=[sn])
                    else:
                        P.op("dve", lambda e, ps=ps, s_=s_, csz=csz, t0=t0, tsz=tsz: e.tensor_copy(out=s_[:csz, t0:t0 + tsz], in_=ps[:csz, :tsz]), reads=[psn], writes=[sn])
                r0 = c0 + ci * 128
                P.dma("sp", dst[r0:r0 + csz, :], s_[:csz, :], reads=[sn], writes=[dstname])
                cti += 1
        P.scope_end(mark)


    TT256 = [(i * 256, 256) for i in range(9)]

    def shift_mac(eng, acc, src, wap, sh, lo, hi, reads, writes):
        a = max(lo, lo - sh)
        b = min(hi, hi - sh)
        P.op(eng, lambda e: e.scalar_tensor_tensor(out=acc[:, a:b], in0=src[:, a + sh:b + sh], scalar=wap, in1=acc[:, a:b], op0=ALU.mult, op1=ALU.add),
             reads=reads, writes=writes)

    def cv_phase(l):
        mark = P.scope_begin()
        ycv = P.sbuf("cv_y", [128, 8, NT], F32)
        a_t = [P.sbuf("cv_a%d" % i, [128, NT], F32) for i in range(2)]
        g_t = [P.sbuf("cv_g%d" % i, [128, NT], F32) for i in range(2)]
        wt = P.sbuf("cv_w", [128, 8, 31], F32)
        P.dma("sp", wt[:], cv_dw_wT[l], writes=["cv_w"])
        OX, OC, XPW = 15, 15 + 2048 + 30, 2364
        xps = [P.sbuf("cv_xp%d" % i, [128, XPW], BF16) for i in range(2)]
        dgs = [P.sbuf("cv_dg%d" % i, [128, 31, 128], BF16) for i in range(2)]
        idf = P.sbuf("cv_idf", [128, 128], F32)
        P.dma("sp", idf[:], ident_in, writes=["cv_idf"])
        for i in range(2):
            P.op("pool", lambda e, i=i: e.memset(xps[i][:], 0.0), writes=["cv_xp%d" % i])
        for j in range(8):
            a_ = a_t[j % 2]; an = "cv_a%d" % (j % 2)
            g_ = g_t[j % 2]; gn = "cv_g%d" % (j % 2)
            xp = xps[j % 2]; xpn = "cv_xp%d" % (j % 2)
            dg = dgs[j % 2]; dgn = "cv_dg%d" % (j % 2)
            P.dma("sp", a_[:], S_P[COL_CV + 128 * j:COL_CV + 128 * (j + 1), :], reads=["S_P"], writes=[an])
            P.dma("sp", g_[:], S_P[COL_CV + D + 128 * j:COL_CV + D + 128 * (j + 1), :], reads=["S_P"], writes=[gn])
            P.op("act", lambda e, g_=g_: e.activation(out=g_[:], in_=g_[:], func=AF.Sigmoid), reads=[gn], writes=[gn])
            P.op("pool", lambda e, a_=a_, g_=g_, xp=xp: e.tensor_tensor(out=xp[:, OX:OX + NX], in0=a_[:, 0:NX], in1=g_[:, 0:NX], op=ALU.mult), reads=[an, gn], writes=[xpn])
            P.op("dve", lambda e, a_=a_, g_=g_, xp=xp: e.tensor_tensor(out=xp[:, OC:OC + NCX], in0=a_[:, NX:NT], in1=g_[:, NX:NT], op=ALU.mult), reads=[an, gn], writes=[xpn])

            def mkdiag(e, dg=dg, j=j):
                for kk in range(31):
                    ins = e.activation(out=dg[:, kk, :], in_=idf[:], func=AF.Copy, scale=wt[:, j, kk:kk + 1])
                return ins
            P.op("act", mkdiag, reads=["cv_idf", "cv_w"], writes=[dgn])
            yn = "cv_y%d" % j
            for (t0, tsz) in (TT[:4] if l == depth - 1 else TT):
                base = (OX + t0) if t0 < NX else OC
                ps, psn = next_ps()

                def cmm(e, ps=ps, dg=dg, xp=xp, base=base, tsz=tsz):
                    for kk in range(31):
                        ins = e.matmul(ps[:, :tsz], lhsT=dg[:, kk, :], rhs=xp[:, base + kk - 15:base + kk - 15 + tsz], start=(kk == 0), stop=(kk == 30))
                    return ins
                P.op("pe", cmm, reads=[dgn, xpn], writes=[psn])
                P.op("act", lambda e, ps=ps, j=j, t0=t0, tsz=tsz: e.activation(out=ycv[:, j, t0:t0 + tsz], in_=ps[:, :tsz], func=AF.Identity, bias=v8[:, 3, j:j + 1]), reads=[psn, "v8"], writes=[yn])
        sq = P.sbuf("cv_sq", [128, 8, 512], F32)
        mean = P.sbuf("cv_mean", [128, 512], F32)
        rstd = P.sbuf("cv_rstd", [128, 512], F32)
        msq = P.sbuf("cv_msq", [128, 512], F32)
        tmp = [P.sbuf("cv_tmp%d" % i, [128, 512], F32) for i in range(2)]
        ob = [P.sbuf("cv_ob%d" % i, [128, 8, 512], BF16) for i in range(2)]
        yall = ["cv_y%d" % j for j in range(8)]
        for ti, (t0, tsz) in enumerate(TT[:4] if l == depth - 1 else TT):
            P.op("act", lambda e, t0=t0, tsz=tsz: e.activation(out=sq[:, :, :tsz], in_=ycv[:, :, t0:t0 + tsz], func=AF.Square), reads=yall, writes=["cv_sq"])
            ps1, ps1n = next_ps()
            ps2, ps2n = next_ps()

            def mm(e, ps1=ps1, ps2=ps2, t0=t0, tsz=tsz):
                for kt in range(8):
                    e.matmul(ps1[:, :tsz], lhsT=ones32[:], rhs=ycv[:, kt, t0:t0 + tsz], start=(kt == 0), stop=(kt == 7))
                for kt in range(8):
                    ins = e.matmul(ps2[:, :tsz], lhsT=ones32[:], rhs=sq[:, kt, :tsz], start=(kt == 0), stop=(kt == 7))
                return ins
            P.op("pe", mm, reads=yall + ["cv_sq", "ones32"], writes=[ps1n, ps2n])
            P.op("act", lambda e, ps1=ps1, tsz=tsz: e.activation(out=mean[:, :tsz], in_=ps1[:, :tsz], func=AF.Copy, scale=1.0 / D), reads=[ps1n], writes=["cv_mean"])
            P.op("dve", lambda e, tsz=tsz: e.tensor_tensor(out=msq[:, :tsz], in0=mean[:, :tsz], in1=mean[:, :tsz], op=ALU.mult), reads=["cv_mean"], writes=["cv_msq"])
            P.op("dve", lambda e, ps2=ps2, tsz=tsz: e.scalar_tensor_tensor(out=rstd[:, :tsz], in0=ps2[:, :tsz], scalar=1.0 / D, in1=msq[:, :tsz], op0=ALU.mult, op1=ALU.subtract), reads=[ps2n, "cv_msq"], writes=["cv_rstd"])
            P.op("act", lambda e, tsz=tsz: e.activation(out=rstd[:, :tsz], in_=rstd[:, :tsz], func=AF.Sqrt, bias=k.eps_t[:, 0:1]), reads=["cv_rstd", "eps_t"], writes=["cv_rstd"])
            P.op("dve", lambda e, tsz=tsz: e.reciprocal(out=rstd[:, :tsz], in_=rstd[:, :tsz]), reads=["cv_rstd"], writes=["cv_rstd"])
            o_ = ob[ti % 2]; on = "cv_ob%d" % (ti % 2)
            for kt in range(8):
                t_ = tmp[kt % 2]; tn = "cv_tmp%d" % (kt % 2)
                P.op("dve", lambda e, t_=t_, kt=kt, t0=t0, tsz=tsz: e.tensor_tensor(out=t_[:, :tsz], in0=ycv[:, kt, t0:t0 + tsz], in1=mean[:, :tsz], op=ALU.subtract), reads=["cv_y%d" % kt, "cv_mean"], writes=[tn])
                P.op("pool", lambda e, t_=t_, tsz=tsz: e.tensor_tensor(out=t_[:, :tsz], in0=t_[:, :tsz], in1=rstd[:, :tsz], op=ALU.mult), reads=[tn, "cv_rstd"], writes=[tn])
                P.op("act", lambda e, t_=t_, o_=o_, kt=kt, tsz=tsz: e.activation(out=o_[:, kt, :tsz], in_=t_[:, :tsz], func=AF.Silu, scale=v8[:, 4, kt:kt + 1], bias=v8[:, 5, kt:kt + 1]), reads=[tn, "v8"], writes=[on])
            P.dma("sp", S_BR[2][:, t0:t0 + tsz].rearrange("(kt p) t -> p kt t", p=128), o_[:, :, :tsz], reads=[on], writes=["S_BR2"])
        P.scope_end(mark)

    def load_w_bf16(dst, dname, W, Ktiles, ncols):
        for c0 in range(0, ncols, 512):
            cs = min(512, ncols - c0)
            for k0 in range(0, Ktiles, 8):
                k1 = min(Ktiles, k0 + 8)
                P.dma("pool", dst[:, k0:k1, c0:c0 + cs], W[k0 * 128:k1 * 128, c0:c0 + cs].rearrange("(kt p) c -> p kt c", p=128), writes=[dname])

    def merge_phase(l):
        mark = P.scope_begin()
        wb = [P.sbuf("mg_w%d" % i, [128, 8, D], BF16) for i in range(4)]
        for i in range(3):
            load_w_bf16(wb[i], "mg_w%d" % i, w_br[i][l], 8, D)
        load_w_bf16(wb[3], "mg_w3", w_out[l], 8, D)
        brs = [[P.sbuf("mg_br%d_%d" % (i, p_), [128, 8, 256], BF16) for i in range(3)] for p_ in range(2)]
        gt = [P.sbuf("mg_g%d" % i, [128, 8, 256], F32) for i in range(2)]
        mg = P.sbuf("mg_m", [128, 8, 256], F32)
        mgb = P.sbuf("mg_mb", [128, 8, 256], BF16)
        xts = [P.sbuf("mg_x%d" % p_, [128, 8, 256], F32) for p_ in range(2)]
        tmp = [P.sbuf("mg_t%d" % i, [128, 256], F32) for i in range(2)]
        gi = 0

        def mg_loads(ti):
            t0, tsz = TT256[ti]
            P.dma("sp", xts[ti % 2][:], S_X[:, t0:t0 + tsz].rearrange("(kt p) t -> p kt t", p=128), reads=["S_X"], writes=["mg_x%d" % (ti % 2)])
            for b in range(3):
                P.dma("sp", brs[ti % 2][b][:], S_BR[b][:, t0:t0 + tsz].rearrange("(kt p) t -> p kt t", p=128), reads=["S_BR%d" % b], writes=["mg_br%d_%d" % (b, ti % 2)])

        mtiles = TT256[:8] if l == depth - 1 else TT256
        for ti, (t0, tsz) in enumerate(mtiles):
            seg = 0 if t0 < NX else 1
            if ti == 0:
                mg_loads(0)
            xt = xts[ti % 2]; xtn = "mg_x%d" % (ti % 2)
            br = brs[ti % 2]
            brn = ["mg_br%d_%d" % (b, ti % 2) for b in range(3)]
            for b in range(3):
                g_ = gt[gi % 2]; gn = "mg_g%d" % (gi % 2); gi += 1
                r0 = COL_GATE + b * D
                P.dma("sp", g_[:], S_P[r0:r0 + D, t0:t0 + tsz].rearrange("(kt p) t -> p kt t", p=128), reads=["S_P"], writes=[gn])
                P.op("act", lambda e, g_=g_: e.activation(out=g_[:], in_=g_[:], func=AF.Sigmoid), reads=[gn], writes=[gn])
                for ct in range(8):
                    ps, psn = next_ps()

                    def mm(e, ps=ps, b=b, ct=ct, br=br):
                        for kt in range(8):
                            ins = e.matmul(ps[:, :256], lhsT=wb[b][:, kt, ct * 128:(ct + 1) * 128], rhs=br[b][:, kt, :], start=(kt == 0), stop=(kt == 7))
                        return ins
                    P.op("pe", mm, reads=["mg_w%d" % b, brn[b]], writes=[psn])
                    mn = "mg_m%d" % ct
                    if b == 0:
                        P.op("dve", lambda e, ps=ps, g_=g_, ct=ct: e.tensor_tensor(out=mg[:, ct, :], in0=ps[:, :256], in1=g_[:, ct, :], op=ALU.mult), reads=[psn, gn], writes=[mn])
                    else:
                        t_ = tmp[ct % 2]; tn = "mg_t%d" % (ct % 2)
                        P.op("dve", lambda e, ps=ps, g_=g_, ct=ct, t_=t_: e.tensor_tensor(out=t_[:], in0=ps[:, :256], in1=g_[:, ct, :], op=ALU.mult), reads=[psn, gn], writes=[tn])
                        P.op("pool", lambda e, ct=ct, t_=t_: e.tensor_tensor(out=mg[:, ct, :], in0=mg[:, ct, :], in1=t_[:], op=ALU.add), reads=[tn, mn], writes=[mn])
            mall = ["mg_m%d" % ct for ct in range(8)]
            P.op("act", lambda e: e.copy(out=mgb[:], in_=mg[:]), reads=mall, writes=["mg_mb"])
            for ct in range(8):
                ps, psn = next_ps()

                def mm2(e, ps=ps, ct=ct):
                    for kt in range(8):
                        ins = e.matmul(ps[:, :256], lhsT=wb[3][:, kt, ct * 128:(ct + 1) * 128], rhs=mgb[:, kt, :], start=(kt == 0), stop=(kt == 7))
                    return ins
                P.op("pe", mm2, reads=["mg_w3", "mg_mb"], writes=[psn])
                P.op("dve", lambda e, ps=ps, ct=ct, seg=seg, xt=xt: e.scalar_tensor_tensor(out=xt[:, ct, :], in0=ps[:, :256], scalar=modS[:, seg, 16 + ct:17 + ct], in1=xt[:, ct, :], op0=ALU.mult, op1=ALU.add),
                     reads=[psn, "modS", xtn], writes=[xtn])
            if ti + 1 < len(mtiles):
                mg_loads(ti + 1)
            P.dma("sp", S_X[:, t0:t0 + tsz].rearrange("(kt p) t -> p kt t", p=128), xt[:], reads=[xtn], writes=["S_X"])
        P.scope_end(mark)

    def ffn_act_phase(l):
        mark = P.scope_begin()
        wt = P.sbuf("ff_w", [128, FKT, 9], F32)
        bt = P.sbuf("ff_b", [128, FKT], F32)
        idf = P.sbuf("ff_idf", [128, 128], F32)
        P.dma("sp", wt[:], ffn_dw_wT[l], writes=["ff_w"])
        P.dma("sp", bt[:], ffn_dw_bT[l], writes=["ff_b"])
        P.dma("sp", idf[:], ident_in, writes=["ff_idf"])
        PADX = 65
        OX, OC = PADX, PADX + NX + PADX + 1
        XPW = OC + NCX + 1
        a_t = [P.sbuf("ff_a%d" % i, [128, NT], F32) for i in range(2)]
        v_t = [P.sbuf("ff_v%d" % i, [128, NT], F32) for i in range(2)]
        xv = [[P.sbuf("ff_xp%d_%d" % (i, m), [128, XPW], BF16) for m in range(3)] for i in range(2)]
        dgs = [P.sbuf("ff_dg%d" % i, [128, 9, 128], BF16) for i in range(2)]
        acc = [P.sbuf("ff_acc%d" % i, [128, NT], F32) for i in range(2)]
        ob = [P.sbuf("ff_ob%d" % i, [128, NT], BF16) for i in range(2)]
        for i in range(2):
            for m in range(3):
                P.op("pool", lambda e, i=i, m=m: e.memset(xv[i][m][:], 0.0), writes=["ff_xp%d_%d" % (i, m)])
        def ff_loads(c):
            P.dma("sp", a_t[c % 2][:], S_F[128 * c:128 * (c + 1), :], reads=["S_F"], writes=["ff_a%d" % (c % 2)])
            P.dma("sp", v_t[c % 2][:], S_F[FH + 128 * c:FH + 128 * (c + 1), :], reads=["S_F"], writes=["ff_v%d" % (c % 2)])

        for c in range(FKT):
            a_ = a_t[c % 2]; an = "ff_a%d" % (c % 2)
            v_ = v_t[c % 2]; vn = "ff_v%d" % (c % 2)
            ac = acc[c % 2]; acn = "ff_acc%d" % (c % 2)
            o_ = ob[c % 2]; on = "ff_ob%d" % (c % 2)
            X = xv[c % 2]; xn = "ff_xp%d" % (c % 2)
            dg = dgs[c % 2]; dgn = "ff_dg%d" % (c % 2)
            if c == 0:
                ff_loads(0)
            xn0, xn1, xn2 = xn + "_0", xn + "_1", xn + "_2"
            P.op("act", lambda e, a_=a_, X=X: e.copy(out=X[0][:, OX:OX + NX], in_=a_[:, 0:NX]), reads=[an], writes=[xn0])
            P.op("act", lambda e, a_=a_, X=X: e.copy(out=X[0][:, OC:OC + NCX], in_=a_[:, NX:NT]), reads=[an], writes=[xn0])
            P.op("act", lambda e, a_=a_, X=X: e.copy(out=X[1][:, OX:OX + NX], in_=a_[:, 0:NX]), reads=[an], writes=[xn1])
            P.op("dve", lambda e, a_=a_, X=X: e.tensor_copy(out=X[2][:, OX:OX + NX], in_=a_[:, 0:NX]), reads=[an], writes=[xn2])
            P.op("pool", lambda e, X=X: e.memset(sub(X[1][:], 0, 128, [(64, 32)], off=OX + 63), 0.0), reads=[], writes=[xn1])
            P.op("dve", lambda e, X=X: e.memset(sub(X[2][:], 0, 128, [(64, 32)], off=OX), 0.0), reads=[], writes=[xn2])

            def mkdiag(e, dg=dg, c=c):
                for kk in range(9):
                    ins = e.activation(out=dg[:, kk, :], in_=idf[:], func=AF.Copy, scale=wt[:, c, kk:kk + 1])
                return ins
            P.op("act", mkdiag, reads=["ff_idf", "ff_w"], writes=[dgn])
            for (t0, tsz) in TT:
                ps, psn = next_ps()
                if t0 < NX:
                    def cmm(e, ps=ps, dg=dg, X=X, t0=t0, tsz=tsz):
                        i_ = 0
                        for dr in (-1, 0, 1):
                            for dw in (-1, 0, 1):
                                src = X[0] if dw == 0 else (X[1] if dw == -1 else X[2])
                                o0 = OX + t0 + 64 * dr + dw
                                ins = e.matmul(ps[:, :tsz], lhsT=dg[:, (dr + 1) * 3 + (dw + 1), :], rhs=src[:, o0:o0 + tsz], start=(i_ == 0), stop=(i_ == 8))
                                i_ += 1
                        return ins
                else:
                    def cmm(e, ps=ps, dg=dg, X=X, t0=t0, tsz=tsz):
                        for i_, dw in enumerate((-1, 0, 1)):
                            ins = e.matmul(ps[:, :tsz], lhsT=dg[:, 3 + (dw + 1), :], rhs=X[0][:, OC + dw:OC + dw + tsz], start=(i_ == 0), stop=(i_ == 2))
                        return ins
                P.op("pe", cmm, reads=[dgn, xn0, xn1, xn2], writes=[psn])
                P.op("act", lambda e, ps=ps, ac=ac, c=c, t0=t0, tsz=tsz: e.activation(out=ac[:, t0:t0 + tsz], in_=ps[:, :tsz], func=AF.Silu, bias=bt[:, c:c + 1]), reads=[psn, "ff_b"], writes=[acn])
                P.op("dve", lambda e, ac=ac, v_=v_, o_=o_, t0=t0, tsz=tsz: e.tensor_tensor(out=o_[:, t0:t0 + tsz], in0=ac[:, t0:t0 + tsz], in1=v_[:, t0:t0 + tsz], op=ALU.mult), reads=[acn, vn], writes=[on])
            if c + 1 < FKT:
                ff_loads(c + 1)
            P.dma("sp", S_A[128 * c:128 * (c + 1), :], o_[:], reads=[on], writes=["S_A"])
        P.scope_end(mark)

    def ffn_down_phase(l, wd):
        mark = P.scope_begin()
        at = [P.sbuf("fd_a%d" % i, [128, FKT, 256], BF16) for i in range(2)]
        xt = [P.sbuf("fd_x%d" % i, [128, 8, 256], F32) for i in range(2)]
        def fd_loads(ti):
            t0, tsz = TT256[ti]
            P.dma("sp", xt[ti % 2][:], S_X[:, t0:t0 + tsz].rearrange("(kt p) t -> p kt t", p=128), reads=["S_X"], writes=["fd_x%d" % (ti % 2)])
            P.dma("sp", at[ti % 2][:], S_A[:, t0:t0 + tsz].rearrange("(kt p) t -> p kt t", p=128), reads=["S_A"], writes=["fd_a%d" % (ti % 2)])

        dtiles = TT256[:8] if l == depth - 1 else TT256
        for ti, (t0, tsz) in enumerate(dtiles):
            seg = 0 if t0 < NX else 1
            a_ = at[ti % 2]; an = "fd_a%d" % (ti % 2)
            x_ = xt[ti % 2]; xn = "fd_x%d" % (ti % 2)
            if ti == 0:
                fd_loads(0)
            for ct in range(8):
                ps, psn = next_ps()

                def mm(e, ps=ps, ct=ct, a_=a_):
                    for kt in range(FKT):
                        ins = e.matmul(ps[:, :256], lhsT=wd[:, kt, ct * 128:(ct + 1) * 128], rhs=a_[:, kt, :], start=(kt == 0), stop=(kt == FKT - 1))
                    return ins
                P.op("pe", mm, reads=["fd_w", an], writes=[psn])
                P.op("dve", lambda e, ps=ps, ct=ct, seg=seg, x_=x_: e.scalar_tensor_tensor(out=x_[:, ct, :], in0=ps[:, :256], scalar=modS[:, seg, 40 + ct:41 + ct], in1=x_[:, ct, :], op0=ALU.mult, op1=ALU.add),
                     reads=[psn, "modS", xn], writes=[xn])
            if ti + 1 < len(dtiles):
                fd_loads(ti + 1)
            P.dma("sp", S_X[:, t0:t0 + tsz].rearrange("(kt p) t -> p kt t", p=128), x_[:], reads=[xn], writes=["S_X"])
        P.scope_end(mark)

    def s5_phase(l):
        mark0 = P.scope_begin()
        Y = P.sbuf("s5_Y", [128, 8, NT], BF16)
        mark = P.scope_begin()
        INV2PI = 1.0 / (2.0 * np.pi)
        PI_LO = TWO_PI_LO / 2.0
        iot = P.sbuf("s5_iota", [128, NT], F32)
        P.dma("sp", iot[:], iota_in, writes=["s5_iota"])
        gm = P.sbuf("s5_gm", [128, 8], F32)
        P.dma("sp", gm[:], gmask_in, writes=["s5_gm"])
        one_t = P.sbuf("s5_one", [128, 1], F32)
        P.op("pool", lambda e: e.memset(one_t[:], 1.0), writes=["s5_one"])
        hpi_t = P.sbuf("s5_hpi", [128, 1], F32)
        P.op("pool", lambda e: e.memset(hpi_t[:], float(np.pi / 2.0)), writes=["s5_hpi"])
        aT = P.sbuf("s5_aT", [128, 3, 64], F32)
        for i in range(3):
            P.dma("sp", aT[:, i, :], s5_aT[l, i], writes=["s5_aTt"])
        rho = P.sbuf("s5_rho", [128, 64], F32)
        thn = P.sbuf("s5_thn", [128, 64], F32)
        dtt = P.sbuf("s5_dtt", [128, 64], F32)
        P.op("act", lambda e: e.activation(out=dtt[:], in_=aT[:, 2, :], func=AF.Exp), reads=["s5_aTt"], writes=["s5_dtt"])
        P.op("dve", lambda e: e.tensor_tensor(out=rho[:], in0=aT[:, 0, :], in1=dtt[:], op=ALU.mult), reads=["s5_aTt", "s5_dtt"], writes=["s5_rho"])
        P.op("act", lambda e: e.activation(out=rho[:], in_=rho[:], func=AF.Exp), reads=["s5_rho"], writes=["s5_rho"])
        P.op("dve", lambda e: e.scalar_tensor_tensor(out=thn[:], in0=aT[:, 1, :], scalar=INV2PI, in1=dtt[:], op0=ALU.mult, op1=ALU.mult), reads=["s5_aTt", "s5_dtt"], writes=["s5_thn"])
        cT = P.sbuf("s5_cT", [128, 2, D], F32)
        for i in range(2):
            P.dma("sp", cT[:, i, :], s5_cT[l, i], writes=["s5_cTt"])
        BL = P.sbuf("s5_BL", [128, 8, 4, 128], BF16)
        CZ = P.sbuf("s5_CZ", [128, 8, 6, 128], BF16)
        P.op("pool", lambda e: e.memset(BL[:], 0.0), writes=["s5_BL"])
        P.op("pool", lambda e: e.memset(CZ[:], 0.0), writes=["s5_CZ"])
        kp = P.sbuf("s5_kp", [128, 3, 128], F32)
        bT = P.sbuf("s5_bT", [128, 2, 128], F32)
        sm = [P.sbuf("s5_sm%d" % i, [128, 128], F32) for i in range(10)]
        smi = P.sbuf("s5_smi", [128, 128], I32)
        f_t = P.sbuf("s5_f", [128, NT], F32)
        fi_t = P.sbuf("s5_fi", [128, NT], F32)
        sn_b = [P.sbuf("s5_sn%d" % i, [128, NT], BF16) for i in range(2)]
        cs_b = [P.sbuf("s5_cs%d" % i, [128, NT], BF16) for i in range(2)]
        bsb2 = [P.sbuf("s5_bsb%d" % i, [128, 2, NT], BF16) for i in range(2)]
        tfull = [P.sbuf("s5_tf%d" % i, [128, NT], BF16) for i in range(4)]
        negC = P.sbuf("s5_negC", [128, 1], F32)
        posC = P.sbuf("s5_posC", [128, 1], F32)
        P.op("pool", lambda e: e.memset(negC[:], -12582912.0), writes=["s5_C"])
        P.op("pool", lambda e: e.memset(posC[:], 12582912.0), writes=["s5_C"])
        wr_t = P.sbuf("s5_wr", [128, NT], BF16)
        wi_t = P.sbuf("s5_wi", [128, NT], BF16)
        tpf = [P.sbuf("s5_tpf%d" % i, [128, 512], F32) for i in range(2)]
        pr_t = [P.sbuf("s5_pr%d" % i, [128, NT], BF16) for i in range(4)]
        u_t = P.sbuf("s5_u", [128, NT], F32)
        ub_t = P.sbuf("s5_ub", [128, NT], BF16)
        yps = [P.psum("s5_yps%d" % i, [128, 512], F32) for i in range(5)]
        ypn = ["s5_yps%d" % i for i in range(5)]

        def sincos(eng, f_ap, fi_ap, sn_ap, cs_ap, fn, fin, snn, csn):
            P.op(eng, lambda e: e.tensor_copy(out=fi_ap, in_=f_ap), reads=[fn], writes=[fin])
            P.op(eng, lambda e: e.tensor_copy(out=cs_ap, in_=fi_ap), reads=[fin], writes=[csn])
            P.op(eng, lambda e: e.tensor_tensor(out=f_ap, in0=f_ap, in1=cs_ap, op=ALU.subtract), reads=[fn, csn], writes=[fn])
            P.op("act", lambda e: e.activation(out=sn_ap, in_=f_ap, func=AF.Sin, scale=TWO_PI_LO), reads=[fn], writes=[snn])
            P.op("act", lambda e: e.activation(out=cs_ap, in_=f_ap, func=AF.Sin, scale=PI_LO), reads=[fn], writes=[csn])
            P.op("act", lambda e: e.activation(out=cs_ap, in_=cs_ap, func=AF.Square, scale=float(np.sqrt(2.0))), reads=[csn], writes=[csn])
            P.op("act", lambda e: e.activation(out=cs_ap, in_=cs_ap, func=AF.Identity, scale=-1.0, bias=one_t[:, 0:1]), reads=[csn, "s5_one"], writes=[csn])

        def tt(eng, out, a, b, op, r, w):
            P.op(eng, lambda e: e.tensor_tensor(out=out, in0=a, in1=b, op=op), reads=r, writes=w)

        REG = [(0, 256, 2048, 2303)] + [(256 + 256 * i, 256, 256 * i, 2047 - 256 * i) for i in range(8)]

        def gen_tables(g):
            sn_t = sn_b[g % 2]; cs_t = cs_b[g % 2]
            snn = "s5_sn%d" % (g % 2); csn = "s5_cs%d" % (g % 2)
            P.op("act", lambda e, g=g: e.activation(out=f_t[:], in_=iot[:], func=AF.Copy, scale=thn[:, g:g + 1]), reads=["s5_iota", "s5_thn"], writes=["s5_f"])
            P.op("act", lambda e: e.activation(out=fi_t[:], in_=f_t[:], func=AF.Identity, bias=posC[:, 0:1]), reads=["s5_f", "s5_C"], writes=["s5_fi"])
            P.op("act", lambda e: e.activation(out=fi_t[:], in_=fi_t[:], func=AF.Identity, bias=negC[:, 0:1]), reads=["s5_fi", "s5_C"], writes=["s5_fi"])
            P.op("pool", lambda e: e.tensor_tensor(out=f_t[:], in0=f_t[:], in1=fi_t[:], op=ALU.subtract), reads=["s5_f", "s5_fi"], writes=["s5_f"])
            P.op("act", lambda e, sn_t=sn_t: e.activation(out=sn_t[:], in_=f_t[:], func=AF.Sin, scale=TWO_PI_LO), reads=["s5_f"], writes=[snn])
            P.op("act", lambda e: e.activation(out=fi_t[:], in_=f_t[:], func=AF.Abs), reads=["s5_f"], writes=["s5_fi"])
            P.op("act", lambda e, cs_t=cs_t: e.activation(out=cs_t[:], in_=fi_t[:], func=AF.Sin, scale=-TWO_PI_LO, bias=hpi_t[:, 0:1]), reads=["s5_fi", "s5_hpi"], writes=[csn])

        k.s5_pending = None
        for j in range(8):
            for i in range(3):
                P.dma("sp", kp[:, i, :], s5_kp[l, j, i], writes=["s5_kp"])
            for i in range(2):
                P.dma("sp", bT[:, i, :], s5_bT[l, j, i], writes=["s5_bTt"])
            are, aim, ldt = kp[:, 0, :], kp[:, 1, :], kp[:, 2, :]
            n = lambda i: "s5_sm%d" % i
            S = [t[:] for t in sm]
            P.op("act", lambda e: e.activation(out=S[0], in_=ldt, func=AF.Exp), reads=["s5_kp"], writes=[n(0)])
            tt("dve", S[1], are, S[0], ALU.mult, ["s5_kp", n(0)], [n(1)])
            P.op("act", lambda e: e.activation(out=S[1], in_=S[1], func=AF.Exp), reads=[n(1)], writes=[n(1)])
            P.op("dve", lambda e: e.scalar_tensor_tensor(out=S[2], in0=aim, scalar=INV2PI, in1=S[0], op0=ALU.mult, op1=ALU.mult), reads=["s5_kp", n(0)], writes=[n(2)])
            sincos("dve", S[2], smi[:], S[3], S[4], n(2), "s5_smi", n(3), n(4))
            tt("dve", S[5], S[1], S[4], ALU.mult, [n(1), n(4)], [n(5)])
            P.op("dve", lambda e: e.tensor_scalar_add(out=S[5], in0=S[5], scalar1=-1.0), reads=[n(5)], writes=[n(5)])
            tt("dve", S[6], S[1], S[3], ALU.mult, [n(1), n(3)], [n(6)])
            tt("dve", S[7], are, are, ALU.mult, ["s5_kp"], [n(7)])
            tt("dve", S[8], aim, aim, ALU.mult, ["s5_kp"], [n(8)])
            tt("dve", S[7], S[7], S[8], ALU.add, [n(7), n(8)], [n(7)])
            P.op("dve", lambda e: e.reciprocal(out=S[7], in_=S[7]), reads=[n(7)], writes=[n(7)])
            tt("dve", S[8], S[5], are, ALU.mult, [n(5), "s5_kp"], [n(8)])
            tt("dve", S[9], S[6], aim, ALU.mult, [n(6), "s5_kp"], [n(9)])
            tt("dve", S[8], S[8], S[9], ALU.add, [n(8), n(9)], [n(8)])
            tt("dve", S[8], S[8], S[7], ALU.mult, [n(8), n(7)], [n(8)])
            tt("dve", S[9], S[6], are, ALU.mult, [n(6), "s5_kp"], [n(9)])
            tt("dve", S[0], S[5], aim, ALU.mult, [n(5), "s5_kp"], [n(0)])
            tt("dve", S[9], S[9], S[0], ALU.subtract, [n(9), n(0)], [n(9)])
            tt("dve", S[9], S[9], S[7], ALU.mult, [n(9), n(7)], [n(9)])
            br_, bi_ = bT[:, 0, :], bT[:, 1, :]
            tt("dve", S[0], S[8], br_, ALU.mult, [n(8), "s5_bTt"], [n(0)])
            tt("dve", S[1], S[9], bi_, ALU.mult, [n(9), "s5_bTt"], [n(1)])
            tt("dve", S[0], S[0], S[1], ALU.subtract, [n(0), n(1)], [n(0)])
            tt("dve", S[2], S[8], bi_, ALU.mult, [n(8), "s5_bTt"], [n(2)])
            tt("dve", S[3], S[9], br_, ALU.mult, [n(9), "s5_bTt"], [n(3)])
            tt("dve", S[2], S[2], S[3], ALU.add, [n(2), n(3)], [n(2)])
            for gl in range(8):
                g = 8 * j + gl
                for ri, src in ((0, sm[0]), (1, sm[2])):
                    P.op("dve", lambda e, gl=gl, ri=ri, src=src: e.tensor_scalar(out=BL[:, gl, 2 * ri, 0:64], in0=src[:, 0:64], scalar1=gm[:, gl:gl + 1], scalar2=None, op0=ALU.mult),
                         reads=[n(0), n(2), "s5_gm"], writes=["s5_BL"])
                    P.op("dve", lambda e, gl=gl, ri=ri, src=src: e.tensor_scalar(out=BL[:, gl, 2 * ri + 1, 64:128], in0=src[:, 64:128], scalar1=gm[:, gl:gl + 1], scalar2=None, op0=ALU.mult),
                         reads=[n(0), n(2), "s5_gm"], writes=["s5_BL"])
                for hh in range(2):
                    pp = slice(64 * hh, 64 * hh + 64)
                    P.op("pool", lambda e, gl=gl, g=g, hh=hh, pp=pp: e.tensor_copy(out=CZ[pp, gl, 3 * hh, 16 * gl:16 * gl + 16], in_=cT[pp, 0, 16 * g:16 * g + 16]), reads=["s5_cTt"], writes=["s5_CZ"])
                    P.op("pool", lambda e, gl=gl, g=g, hh=hh, pp=pp: e.tensor_scalar(out=CZ[pp, gl, 3 * hh + 1, 16 * gl:16 * gl + 16], in0=cT[pp, 0, 16 * g:16 * g + 16], scalar1=-1.0, scalar2=None, op0=ALU.mult), reads=["s5_cTt"], writes=["s5_CZ"])
                    P.op("pool", lambda e, gl=gl, g=g, hh=hh, pp=pp: e.tensor_scalar(out=CZ[pp, gl, 3 * hh + 2, 16 * gl:16 * gl + 16], in0=cT[pp, 1, 16 * g:16 * g + 16], scalar1=-1.0, scalar2=None, op0=ALU.mult), reads=["s5_cTt"], writes=["s5_CZ"])
            P.dma("sp", u_t[:], S_P[128 * j:128 * (j + 1), :], reads=["S_P"], writes=["s5_u"])
            P.op("act", lambda e: e.copy(out=ub_t[:], in_=u_t[:]), reads=["s5_u"], writes=["s5_ub"])

            def bu_evac(gl_, j=j):
                g_ = 8 * j + gl_
                bs_ = bsb2[g_ % 2]; bsn = "s5_bsb%d" % (g_ % 2)
                for (tau0, nn, fc, bc) in REG:
                    k.s5_rr = getattr(k, "s5_rr", 0) + 1
                    bk = ps_lin[k.s5_rr % 3]; bkn = "ps_lin%d" % (k.s5_rr % 3)
                    bre = bk[:, 0:256]; bim = bk[:, 256:512]

                    def mm(e, gl_=gl_, nn=nn, fc=fc, bc=bc, bre=bre, bim=bim):
                        rev = sub(ub_t[:], 0, 128, [(-1, nn)], off=bc)
                        e.matmul(bre[:, :nn], lhsT=BL[:, gl_, 0, :], rhs=ub_t[:, fc:fc + nn], start=True, stop=False)
                        e.matmul(bre[:, :nn], lhsT=BL[:, gl_, 1, :], rhs=rev, start=False, stop=True)
                        e.matmul(bim[:, :nn], lhsT=BL[:, gl_, 2, :], rhs=ub_t[:, fc:fc + nn], start=True, stop=False)
                        return e.matmul(bim[:, :nn], lhsT=BL[:, gl_, 3, :], rhs=rev, start=False, stop=True)
                    P.op("pe", mm, reads=["s5_BL", "s5_ub"], writes=[bkn])
                    P.op("act", lambda e, bk=bk, bs_=bs_, tau0=tau0: e.copy(out=bs_[:, :, tau0:tau0 + 256], in_=bk[:].rearrange("p (c n) -> p c n", c=2)), reads=[bkn], writes=[bsn])

            for gl in range(8):
                g = 8 * j + gl
                gi_ = 8 * j + gl
                sn_t = sn_b[gi_ % 2]; cs_t = cs_b[gi_ % 2]
                snn = "s5_sn%d" % (gi_ % 2); csn = "s5_cs%d" % (gi_ % 2)
                if gi_ == 0:
                    gen_tables(0)
                if gi_ + 1 < 64:
                    gen_tables(gi_ + 1)
                if gl == 0:
                    bu_evac(gl)
                if gl < 7:
                    bu_evac(gl + 1)
                bs_ = bsb2[gi_ % 2]; bsn = "s5_bsb%d" % (gi_ % 2)
                tt("dve", tfull[0][:], bs_[:, 0, :], cs_t[:], ALU.mult, [bsn, csn], ["s5_tf0"])
                tt("dve", tfull[1][:], bs_[:, 1, :], sn_t[:], ALU.mult, [bsn, snn], ["s5_tf1"])
                tt("dve", tfull[2][:], bs_[:, 1, :], cs_t[:], ALU.mult, [bsn, csn], ["s5_tf2"])
                tt("dve", tfull[3][:], bs_[:, 0, :], sn_t[:], ALU.mult, [bsn, snn], ["s5_tf3"])
                tt("dve", wr_t[:], tfull[0][:], tfull[1][:], ALU.add, ["s5_tf0", "s5_tf1"], ["s5_wr"])
                tt("dve", wi_t[:], tfull[2][:], tfull[3][:], ALU.subtract, ["s5_tf2", "s5_tf3"], ["s5_wi"])
                rho_b = sub(rho[:], 0, 128, [(0, NT)], off=g)
                P.op("dve", lambda e, rho_b=rho_b: e.tensor_tensor_scan(out=wr_t[:], data0=rho_b, data1=wr_t[:], initial=0.0, op0=ALU.mult, op1=ALU.add), reads=["s5_rho", "s5_wr"], writes=["s5_wr"])
                P.op("dve", lambda e, rho_b=rho_b: e.tensor_tensor_scan(out=wi_t[:], data0=rho_b, data1=wi_t[:], initial=0.0, op0=ALU.mult, op1=ALU.add), reads=["s5_rho", "s5_wi"], writes=["s5_wi"])
                if k.s5_pending is not None:
                    k.s5_pending()
                    k.s5_pending = None
                tt("dve", pr_t[0][:], wr_t[:], cs_t[:], ALU.mult, ["s5_wr", csn], ["s5_pr0"])
                tt("dve", pr_t[1][:], wi_t[:], sn_t[:], ALU.mult, ["s5_wi", snn], ["s5_pr1"])
                tt("dve", pr_t[2][:], wr_t[:], sn_t[:], ALU.mult, ["s5_wr", snn], ["s5_pr2"])
                tt("dve", pr_t[3][:], wi_t[:], cs_t[:], ALU.mult, ["s5_wi", csn], ["s5_pr3"])
                def emit_readout(gl=gl):
                    for ti, (t0, tsz) in enumerate(TT):
                        if t0 < NX:
                            ftau = 256 + t0
                            btau = 2303 - t0
                        else:
                            ftau = 0
                            btau = 255

                        def rd(e, gl=gl, ti=ti, tsz=tsz, ftau=ftau, btau=btau):
                            first = (gl == 0)
                            last = (gl == 7)
                            lt = (0, 1, 2, 2)
                            for pi in range(4):
                                e.matmul(yps[ti][:, :tsz], lhsT=CZ[:, gl, lt[pi], :], rhs=pr_t[pi][:, ftau:ftau + tsz], start=(first and pi == 0), stop=False)
                            for pi in range(4):
                                ins = e.matmul(yps[ti][:, :tsz], lhsT=CZ[:, gl, 3 + lt[pi], :], rhs=sub(pr_t[pi][:], 0, 128, [(-1, tsz)], off=btau), start=False, stop=(last and pi == 3))
                            return ins
                        P.op("pe", rd, reads=["s5_CZ", "s5_pr0", "s5_pr1", "s5_pr2", "s5_pr3"], writes=[ypn[ti]])

                k.s5_pending = emit_readout
            k.s5_pending()
            k.s5_pending = None
            for ti, (t0, tsz) in enumerate(TT):
                t_ = tpf[ti % 2]; tn = "s5_tpf%d" % (ti % 2)
                P.op("dve", lambda e, t_=t_, ti=ti, t0=t0, tsz=tsz, j=j: e.scalar_tensor_tensor(out=t_[:, :tsz], in0=u_t[:, t0:t0 + tsz], scalar=v8[:, 2, j:j + 1], in1=yps[ti][:, :tsz], op0=ALU.mult, op1=ALU.add),
                     reads=["s5_u", "v8", ypn[ti]], writes=[tn])
                P.op("act", lambda e, t_=t_, t0=t0, tsz=tsz, j=j: e.activation(out=Y[:, j, t0:t0 + tsz], in_=t_[:, :tsz], func=AF.Gelu_apprx_tanh), reads=[tn], writes=["s5_Y"])
        P.scope_end(mark)
        mark = P.scope_begin()
        wg = P.sbuf("s5_wg", [128, 8, D], BF16)
        load_w_bf16(wg, "s5_wg", s5_w_glu[l], 8, D)
        sg = [P.sbuf("s5_sg%d" % i, [128, 512], BF16) for i in range(2)]
        ob = [P.sbuf("s5_ob%d" % i, [128, 8, 512], BF16) for i in range(2)]
        for ti, (t0, tsz) in enumerate(TT[:4] if l == depth - 1 else TT):
            o_ = ob[ti % 2]; on = "s5_ob%d" % (ti % 2)
            for ct in range(8):
                ps, psn = next_ps()

                def mm(e, ps=ps, ct=ct, t0=t0, tsz=tsz):
                    for kt in range(8):
                        ins = e.matmul(ps[:, :tsz], lhsT=wg[:, kt, ct * 128:(ct + 1) * 128], rhs=Y[:, kt, t0:t0 + tsz], start=(kt == 0), stop=(kt == 7))
                    return ins
                P.op("pe", mm, reads=["s5_wg", "s5_Y"], writes=[psn])
                s_ = sg[ct % 2]; sn_ = "s5_sg%d" % (ct % 2)
                P.op("act", lambda e, ps=ps, s_=s_, tsz=tsz: e.activation(out=s_[:, :tsz], in_=ps[:, :tsz], func=AF.Sigmoid), reads=[psn], writes=[sn_])
                P.op("dve", lambda e, s_=s_, o_=o_, ct=ct, t0=t0, tsz=tsz: e.tensor_tensor(out=o_[:, ct, :tsz], in0=Y[:, ct, t0:t0 + tsz], in1=s_[:, :tsz], op=ALU.mult), reads=[sn_, "s5_Y"], writes=[on])
            P.dma("sp", S_BR[0][:, t0:t0 + tsz].rearrange("(kt p) t -> p kt t", p=128), o_[:, :, :tsz], reads=[on], writes=["S_BR0"])
        P.scope_end(mark)
        P.scope_end(mark0)

    def dn_phase(l):
        mark = P.scope_begin()
        CH = 128
        NCH = NT // CH
        BQ = 4
        act = lambda out, in_, func, r, w, **kw: P.op("act", lambda e: e.activation(out=out, in_=in_, func=func, **kw), reads=r, writes=w)
        tt = lambda eng, out, a, b, op, r, w: P.op(eng, lambda e: e.tensor_tensor(out=out, in0=a, in1=b, op=op), reads=r, writes=w)
        sel = P.sbuf("dn_sel", [128, 16, 128], F32)
        nsel = P.sbuf("dn_nsel", [128, 16, 128], F32)
        tri = P.sbuf("dn_tri", [128, 8, 128], BF16)
        negm = P.sbuf("dn_negm", [128, 2, 128], F32)
        identf = P.sbuf("dn_identf", [128, 128], F32)
        P.dma("sp", negm[:], negm_in, writes=["dn_negm"])
        P.dma("sp", identf[:], ident_in, writes=["dn_identf"])
        P.op("pool", lambda e: e.memset(sel[:], 0.0), writes=["dn_sel"])
        identb = P.sbuf("dn_ident", [128, 128], BF16)
        cw = P.sbuf("dn_cw", [128, 24, 3], F32)
        pp = P.sbuf("dn_pp", [40, 2], F32)
        nA = P.sbuf("dn_nA", [40, 1], F32)
        one40 = P.sbuf("dn_one", [128, 1], F32)
        nw = P.sbuf("dn_nw", [128, 1], F32)
        P.dma("sp", sel[0:40], sel2_in, writes=["dn_sel"])
        P.dma("pool", tri[:], tri_in, writes=["dn_tri"])
        P.dma("pool", identb[:], ident_in, writes=["dn_ident"])
        P.dma("sp", cw[:], dn_conv_wT[l], writes=["dn_cw"])
        P.dma("sp", pp[:], dnp40[l], writes=["dn_pp"])
        P.dma("sp", nw[:], dn_nw[l], writes=["dn_nw"])
        P.op("pool", lambda e: e.memset(one40[:], 1.0), writes=["dn_one"])
        P.op("dve", lambda e: e.tensor_scalar(out=nsel[:], in0=sel[:], scalar1=-1.0, scalar2=None, op0=ALU.mult), reads=["dn_sel"], writes=["dn_nsel"])
        act(nA[:], pp[:, 0:1], AF.Exp, ["dn_pp"], ["dn_nA"])
        P.op("dve", lambda e: e.tensor_scalar(out=nA[:], in0=nA[:], scalar1=-1.0, scalar2=None, op0=ALU.mult), reads=["dn_nA"], writes=["dn_nA"])
        beta = P.sbuf("dn_beta", [128, NT], F32)
        G = P.sbuf("dn_G", [128, NT], F32)
        EG = P.sbuf("dn_EG", [128, NT], F32)
        BEG = P.sbuf("dn_BEG", [128, NT], F32)
        ET = P.sbuf("dn_ET", [128, NT], F32)
        mk_g = P.scope_begin()
        cm = P.sbuf("dn_cm", [40, NT], F32)
        P.dma("sp", cm[:], cmask_in, writes=["dn_cm"])
        P.op("pool", lambda e: e.memset(beta[:], 0.0), writes=["dn_beta"])
        P.op("pool", lambda e: e.memset(ET[:], 0.0), writes=["dn_ET"])
        P.op("pool", lambda e: e.memset(G[:], 0.0), writes=["dn_G"])
        P.op("pool", lambda e: e.memset(EG[:], 0.0), writes=["dn_EG"])
        P.op("pool", lambda e: e.memset(BEG[:], 0.0), writes=["dn_BEG"])
        for d in range(2):
            P.dma("sp", beta[32 * d:32 * d + 8, :], S_P[COL_BETA + 8 * d:COL_BETA + 8 * d + 8, :], reads=["S_P"], writes=["dn_beta"])
            P.dma("sp", ET[32 * d:32 * d + 8, :], S_P[COL_DECAY + 8 * d:COL_DECAY + 8 * d + 8, :], reads=["S_P"], writes=["dn_ET"])
        act(beta[0:40], beta[0:40], AF.Sigmoid, ["dn_beta"], ["dn_beta"])
        act(ET[0:40], ET[0:40], AF.Exp, ["dn_ET", "dn_pp"], ["dn_ET"], bias=pp[:, 1:2])
        act(ET[0:40], ET[0:40], AF.Ln, ["dn_ET", "dn_one"], ["dn_ET"], bias=one40[0:40, 0:1])
        P.op("dve", lambda e: e.tensor_scalar(out=ET[0:40], in0=ET[0:40], scalar1=nA[:, 0:1], scalar2=None, op0=ALU.mult), reads=["dn_ET", "dn_nA"], writes=["dn_ET"])
        P.op("dve", lambda e: e.tensor_tensor_scan(out=G[0:8, :], data0=cm[0:8, :], data1=ET[0:8, :], initial=0.0, op0=ALU.mult, op1=ALU.add), reads=["dn_cm", "dn_ET"], writes=["dn_G"])
        rv = lambda t: sub(t[:], 32, 8, [(-1, NT)], off=NT - 1)
        P.op("dve", lambda e: e.tensor_tensor_scan(out=rv(G), data0=rv(cm), data1=rv(ET), initial=0.0, op0=ALU.mult, op1=ALU.add), reads=["dn_cm", "dn_ET"], writes=["dn_G"])
        act(EG[0:40], G[0:40], AF.Exp, ["dn_G"], ["dn_EG"])
        tt("dve", BEG[0:40], beta[0:40], EG[0:40], ALU.mult, ["dn_beta", "dn_EG"], ["dn_BEG"])
        for d in range(2):
            lastoff = CH - 1 if d == 0 else 0
            P.op("dve", lambda e, d=d, lastoff=lastoff: e.tensor_tensor(out=sub(ET[:], 32 * d, 8, [(CH, NCH), (1, CH)]), in0=sub(G[:], 32 * d, 8, [(CH, NCH), (0, CH)], off=lastoff),
                                                                      in1=sub(G[:], 32 * d, 8, [(CH, NCH), (1, CH)]), op=ALU.subtract), reads=["dn_G"], writes=["dn_ET"])
        act(ET[0:40], ET[0:40], AF.Exp, ["dn_ET"], ["dn_ET"])
        P.scope_end(mk_g)
        qn = P.sbuf("dn_qn", [128, NT], BF16)
        kn = P.sbuf("dn_kn", [128, NT], BF16)
        vn = P.sbuf("dn_vn", [128, NT], BF16)
        rin = P.sbuf("dn_rin", [128, 512], F32)
        sqt = P.sbuf("dn_sq", [128, 512], F32)
        O = P.sbuf("dn_O", [128, NT], F32)
        ob = P.sbuf("dn_ob", [128, NT], BF16)
        pd = [P.psum("dn_pd%d" % i, [128, 512], F32) for i in range(5)]
        pdn = ["dn_pd%d" % i for i in range(5)]

        def tri_b(m, nb):
            return sub(tri[:], 0, 128, [(0, nb), (1, CH)], off=m * CH)

        for h in range(8):
            mk1 = P.scope_begin()
            raw = [P.sbuf("dn_raw%d" % i, [128, NT], F32) for i in range(2)]
            cv_ = [P.sbuf("dn_cv%d" % i, [128, NT], F32) for i in range(3)]
            HOX, HOC, HXW = 1, 1 + NX + 2, 1 + NX + 2 + NCX + 1
            hxp = [P.sbuf("dn_hxp%d" % i, [128, HXW], BF16) for i in range(2)]
            hdg = [P.sbuf("dn_hdg%d" % i, [128, 3, 128], BF16) for i in range(2)]
            for i in range(2):
                P.op("pool", lambda e, i=i: e.memset(hxp[i][:], 0.0), writes=["dn_hxp%d" % i])
            for wi_ in range(3):
                r_ = raw[wi_ % 2]; rn = "dn_raw%d" % (wi_ % 2)
                c_ = cv_[wi_]; cn = "dn_cv%d" % wi_
                xp = hxp[wi_ % 2]; xpn = "dn_hxp%d" % (wi_ % 2)
                dg = hdg[wi_ % 2]; dgn = "dn_hdg%d" % (wi_ % 2)
                ch = wi_ * 8 + h
                r0 = COL_QKV + wi_ * D + 128 * h
                P.dma("sp", r_[:], S_P[r0:r0 + 128, :], reads=["S_P"], writes=[rn])
                P.op("act", lambda e, r_=r_, xp=xp: e.copy(out=xp[:, HOX:HOX + NX], in_=r_[:, 0:NX]), reads=[rn], writes=[xpn])
                P.op("dve", lambda e, r_=r_, xp=xp: e.tensor_copy(out=xp[:, HOC:HOC + NCX], in_=r_[:, NX:NT]), reads=[rn], writes=[xpn])

                def mkdiag(e, dg=dg, ch=ch):
                    for kk in range(3):
                        ins = e.activation(out=dg[:, kk, :], in_=identf[:], func=AF.Copy, scale=cw[:, ch, kk:kk + 1])
                    return ins
                P.op("act", mkdiag, reads=["dn_identf", "dn_cw"], writes=[dgn])
                for (t0, tsz) in TT:
                    base = (HOX + t0) if t0 < NX else HOC
                    ps, psn = next_ps()

                    def cmm(e, ps=ps, dg=dg, xp=xp, base=base, tsz=tsz):
                        for kk in range(3):
                            ins = e.matmul(ps[:, :tsz], lhsT=dg[:, kk, :], rhs=xp[:, base + kk - 1:base + kk - 1 + tsz], start=(kk == 0), stop=(kk == 2))
                        return ins
                    P.op("pe", cmm, reads=[dgn, xpn], writes=[psn])
                    act(c_[:, t0:t0 + tsz], ps[:, :tsz], AF.Silu, [psn], [cn])
            for wi_, dst, dn_, scl in ((0, qn, "dn_qn", 128.0 ** -0.5), (1, kn, "dn_kn", 1.0)):
                c_ = cv_[wi_]; cn = "dn_cv%d" % wi_
                for (t0, tsz) in TT:
                    act(sqt[:, :tsz], c_[:, t0:t0 + tsz], AF.Square, [cn], ["dn_sq"])
                    ps, psn = next_ps()
                    P.op("pe", lambda e, ps=ps, tsz=tsz: e.matmul(ps[:, :tsz], lhsT=ones32[:], rhs=sqt[:, :tsz], start=True, stop=True), reads=["dn_sq", "ones32"], writes=[psn])
                    act(rin[:, :tsz], ps[:, :tsz], AF.Sqrt, [psn, "eps_t"], ["dn_rin"], bias=k.eps_t[:, 0:1])
                    P.op("dve", lambda e, tsz=tsz: e.reciprocal(out=rin[:, :tsz], in_=rin[:, :tsz]), reads=["dn_rin"], writes=["dn_rin"])
                    P.op("dve", lambda e, c_=c_, dst=dst, scl=scl, t0=t0, tsz=tsz: e.scalar_tensor_tensor(out=dst[:, t0:t0 + tsz], in0=c_[:, t0:t0 + tsz], scalar=scl, in1=rin[:, :tsz], op0=ALU.mult, op1=ALU.mult),
                         reads=[cn, "dn_rin"], writes=[dn_])
            P.op("act", lambda e: e.copy(out=vn[:], in_=cv_[2][:]), reads=["dn_cv2"], writes=["dn_vn"])
            P.scope_end(mk1)
            mk2 = P.scope_begin()
            tokv = [P.sbuf("dn_tokv%d" % d, [128, NCH, 128], BF16) for d in range(2)]
            tokt = [P.sbuf("dn_tokt%d" % d, [128, NCH, 128], BF16) for d in range(2)]
            tokkb = [P.sbuf("dn_tokk%d" % d, [128, BQ, 128], BF16) for d in range(2)]
            TtAll = [P.sbuf("dn_TtAll%d" % d, [128, NT], BF16) for d in range(2)]
            Aqk = [P.sbuf("dn_Aqk%d" % d, [128, NT], BF16) for d in range(2)]
            wTn = [P.sbuf("dn_wTn%d" % d, [128, NT], BF16) for d in range(2)]
            qg = [P.sbuf("dn_qg%d" % d, [128, NT], BF16) for d in range(2)]
            sdcol = [P.sbuf("dn_sd%d" % d, [128, NCH], F32) for d in range(2)]
            tmpbs = [[P.sbuf("dn_tmpb%d_%d" % (d, i), [128, 512], BF16) for i in range(3)] for d in range(2)]
            W1s = [P.sbuf("dn_W1_%d" % d, [128, 512], F32) for d in range(2)]
            Mxs = [P.sbuf("dn_Mx_%d" % d, [128, 512], F32) for d in range(2)]
            matss = [{nm: [P.sbuf("dn_%s%d_%d" % (nm, i, d), [128, 512], BF16) for i in range(2)] for nm in ("A", "Bt", "T", "Tt")} for d in range(2)]
            A0fs = [P.sbuf("dn_A0f%d" % d, [128, 512], BF16) for d in range(2)]
            Bt0fs = [P.sbuf("dn_Bt0f%d" % d, [128, 512], BF16) for d in range(2)]
            Sf = [P.sbuf("dn_S%d" % d, [128, 128], F32) for d in range(2)]
            Sb = [P.sbuf("dn_Sb%d" % d, [128, 128], BF16) for d in range(2)]
            vnew = [P.sbuf("dn_vnew%d" % d, [128, 128], BF16) for d in range(2)]
            P.op("pool", lambda e: e.memset(O[:], 0.0), writes=["dn_O"])
            banks = [[pd[0], pd[1], pd[2], pd[3]], [pd[4], ps_lin[0], ps_lin[1], ps_lin[2]]]
            bankn = [[pdn[0], pdn[1], pdn[2], pdn[3]], [pdn[4], "ps_lin0", "ps_lin1", "ps_lin2"]]

            def batch_gen(d, t0, tsz):
                dh = 8 * d + h
                m_nstrT = 1 if d == 0 else 3
                lastoff = CH - 1 if d == 0 else 0
                B = banks[d]; Bn = bankn[d]
                nb = tsz // CH
                n0 = t0 // CH
                W = tsz
                S_ = "_%d" % d
                tmpb = tmpbs[d]; W1 = W1s[d]; Mx = Mxs[d]; mats = matss[d]; A0f = A0fs[d]; Bt0f = Bt0fs[d]; tokk = tokkb[d]
                mn = lambda nm, i: "dn_%s%d_%d" % (nm, i, d)
                v3 = lambda t: t[:, :W].rearrange("p (c i) -> p c i", c=nb)
                dsts = (tokv[d], tokk, tokt[d])
                dstn = ("dn_tokv%d" % d, "dn_tokk%d" % d, "dn_tokt%d" % d)
                for xi, (rows, srcT, sname) in enumerate(((beta, vn, "dn_vn"), (BEG, kn, "dn_kn"), (ET, kn, "dn_kn"))):
                    P.op("pe", lambda e, xi=xi, rows=rows: e.matmul(B[xi][:, :tsz], lhsT=sel[:, dh, :], rhs=rows[:, t0:t0 + tsz], start=True, stop=True),
                         reads=["dn_sel", "dn_beta", "dn_BEG", "dn_ET"], writes=[Bn[xi]])
                P.op("pe", lambda e: e.matmul(B[3][:, :tsz], lhsT=sel[:, dh, :], rhs=EG[:, t0:t0 + tsz], start=True, stop=True), reads=["dn_sel", "dn_EG"], writes=[Bn[3]])
                yield
                for xi, (rows, srcT, sname) in enumerate(((beta, vn, "dn_vn"), (BEG, kn, "dn_kn"), (ET, kn, "dn_kn"))):
                    tt("dve", tmpb[xi][:, :tsz], srcT[:, t0:t0 + tsz], B[xi][:, :tsz], ALU.mult, [sname, Bn[xi]], ["dn_tmpb%d_%d" % (d, xi)])
                tt("dve", qg[d][:, t0:t0 + tsz], qn[:, t0:t0 + tsz], B[3][:, :tsz], ALU.mult, ["dn_qn", Bn[3]], ["dn_qg%d" % d])
                P.op("dve", lambda e: e.tensor_copy(out=sdcol[d][:, n0:n0 + nb], in_=sub(B[3][:], 0, 128, [(CH, nb)], off=lastoff)), reads=[Bn[3]], writes=["dn_sd%d" % d])
                yield
                for xi in range(3):
                    def tr(e, xi=xi):
                        for c in range(nb):
                            ins = e.matmul(B[xi][:, c * 128:(c + 1) * 128], lhsT=tmpb[xi][:, c * CH:(c + 1) * CH], rhs=identb[:], start=True, stop=True)
                        return ins
                    P.op("pe", tr, reads=["dn_tmpb%d_%d" % (d, xi), "dn_ident"], writes=[Bn[xi]])
                yield
                for xi in range(3):
                    o_ = dsts[xi][:, n0:n0 + nb, :] if xi != 1 else dsts[xi][:, 0:nb, :]
                    i_ = B[xi][:, :nb * 128].rearrange("p (c d) -> p c d", c=nb)
                    if xi != 1:
                        P.op("act", lambda e, o_=o_, i_=i_: e.copy(out=o_, in_=i_), reads=[Bn[xi]], writes=[dstn[xi]])
                    else:
                        P.op("dve", lambda e, o_=o_, i_=i_: e.tensor_copy(out=o_, in_=i_), reads=[Bn[xi]], writes=[dstn[xi]])
                yield

                def mats_mm(e):
                    for c in range(nb):
                        cs = slice((n0 + c) * CH, (n0 + c + 1) * CH)
                        os_ = slice(c * CH, (c + 1) * CH)
                        e.matmul(B[0][:, os_], lhsT=sel[:, dh, :], rhs=G[:, cs], start=True, stop=False)
                        e.matmul(B[0][:, os_], lhsT=G[:, cs], rhs=nsel[:, dh, :], start=False, stop=False)
                        e.matmul(B[0][:, os_], lhsT=identf[:], rhs=negm[:, d, :], start=False, stop=True)
                        e.matmul(B[1][:, os_], lhsT=kn[:, cs], rhs=kn[:, cs], start=True, stop=True)
                        e.matmul(B[2][:, os_], lhsT=kn[:, cs], rhs=qn[:, cs], start=True, stop=True)
                        ins = e.matmul(B[3][:, os_], lhsT=sel[:, dh, :], rhs=beta[:, cs], start=True, stop=True)
                    return ins
                P.op("pe", mats_mm, reads=["dn_sel", "dn_nsel", "dn_G", "dn_kn", "dn_qn", "dn_beta", "dn_negm", "dn_identf"], writes=Bn)
                yield
                act(W1[:, :W], B[0][:, :W], AF.Exp, [Bn[0]], ["dn_W1" + S_])
                yield
                tt("dve", Mx[:, :W], B[1][:, :W], W1[:, :W], ALU.mult, [Bn[1], "dn_W1" + S_], ["dn_Mx" + S_])
                tt("dve", Mx[:, :W], B[3][:, :W], Mx[:, :W], ALU.mult, [Bn[3], "dn_Mx" + S_], ["dn_Mx" + S_])
                tt("dve", v3(Bt0f), v3(Mx), tri_b(m_nstrT, nb), ALU.mult, ["dn_Mx" + S_, "dn_tri"], ["dn_Bt0f" + S_])
                tt("dve", Aqk[d][:, n0 * CH:n0 * CH + W], B[2][:, :W], W1[:, :W], ALU.mult, [Bn[2], "dn_W1" + S_], ["dn_Aqk%d" % d])
                yield

                def trA(e):
                    for c in range(nb):
                        os_ = slice(c * CH, (c + 1) * CH)
                        ins = e.matmul(B[0][:, os_], lhsT=Bt0f[:, os_], rhs=identb[:], start=True, stop=True)
                    return ins
                P.op("pe", trA, reads=["dn_Bt0f" + S_, "dn_ident"], writes=[Bn[0]])
                tt("dve", v3(mats["Bt"][0]), v3(Bt0f), tri_b(5, nb), ALU.mult, ["dn_Bt0f" + S_, "dn_tri"], [mn("Bt", 0)])
                yield
                P.op("act", lambda e: e.copy(out=A0f[:, :W], in_=B[0][:, :W]), reads=[Bn[0]], writes=["dn_A0f" + S_])
                tt("pool", v3(mats["Tt"][0]), v3(mats["Bt"][0]), tri_b(4, nb), ALU.add, [mn("Bt", 0), "dn_tri"], [mn("Tt", 0)])
                yield
                tt("dve", v3(mats["A"][0]), v3(A0f), tri_b(5, nb), ALU.mult, ["dn_A0f" + S_, "dn_tri"], [mn("A", 0)])
                yield
                tt("pool", v3(mats["T"][0]), v3(mats["A"][0]), tri_b(4, nb), ALU.add, [mn("A", 0), "dn_tri"], [mn("T", 0)])
                yield
                NLV = 4
                for lv in range(NLV):
                    a_, b_ = lv % 2, (lv + 1) % 2
                    A_, Bt_, T_, Tt_ = mats["A"][a_], mats["Bt"][a_], mats["T"][a_], mats["Tt"][a_]
                    An, Btn, Tn, Ttn = mats["A"][b_], mats["Bt"][b_], mats["T"][b_], mats["Tt"][b_]

                    def sq(e, A_=A_, Bt_=Bt_):
                        for c in range(nb):
                            os_ = slice(c * CH, (c + 1) * CH)
                            e.matmul(B[0][:, os_], lhsT=Bt_[:, os_], rhs=A_[:, os_], start=True, stop=True)
                            ins = e.matmul(B[1][:, os_], lhsT=A_[:, os_], rhs=Bt_[:, os_], start=True, stop=True)
                        return ins
                    P.op("pe", sq, reads=[mn("A", a_), mn("Bt", a_)], writes=[Bn[0], Bn[1]])
                    yield
                    P.op("act", lambda e, An=An: e.copy(out=An[:, :W], in_=B[0][:, :W]), reads=[Bn[0]], writes=[mn("A", b_)])
                    P.op("dve", lambda e, Btn=Btn: e.tensor_copy(out=Btn[:, :W], in_=B[1][:, :W]), reads=[Bn[1]], writes=[mn("Bt", b_)])
                    yield

                    def pr(e, T_=T_, Tt_=Tt_, An=An, Btn=Btn):
                        for c in range(nb):
                            os_ = slice(c * CH, (c + 1) * CH)
                            e.matmul(B[2][:, os_], lhsT=T_[:, os_], rhs=Btn[:, os_], start=True, stop=False)
                            e.matmul(B[2][:, os_], lhsT=identb[:], rhs=Tt_[:, os_], start=False, stop=True)
                            e.matmul(B[3][:, os_], lhsT=Tt_[:, os_], rhs=An[:, os_], start=True, stop=False)
                            ins = e.matmul(B[3][:, os_], lhsT=identb[:], rhs=T_[:, os_], start=False, stop=True)
                        return ins
                    P.op("pe", pr, reads=[mn("T", a_), mn("Tt", a_), mn("A", b_), mn("Bt", b_), "dn_ident"], writes=[Bn[2], Bn[3]])
                    yield
                    P.op("act", lambda e, Ttn=Ttn: e.copy(out=Ttn[:, :W], in_=B[2][:, :W]), reads=[Bn[2]], writes=[mn("Tt", b_)])
                    P.op("act", lambda e, Tn=Tn: e.copy(out=Tn[:, :W], in_=B[3][:, :W]), reads=[Bn[3]], writes=[mn("T", b_)])
                    yield
                cur = NLV % 2
                for bi, mk_ in enumerate((6, 7)):
                    oth = 1 - cur
                    T_, Tt_ = mats["T"][cur], mats["Tt"][cur]
                    Tn, Ttn = mats["T"][oth], mats["Tt"][oth]
                    Ao, Bo, Pm, Rm = mats["A"][0], mats["Bt"][0], mats["A"][1], mats["Bt"][1]
                    tt("dve", v3(Ao), v3(A0f), tri_b(mk_, nb), ALU.mult, ["dn_A0f" + S_, "dn_tri"], [mn("A", 0)])
                    tt("pool", v3(Bo), v3(Bt0f), tri_b(mk_, nb), ALU.mult, ["dn_Bt0f" + S_, "dn_tri"], [mn("Bt", 0)])
                    yield

                    def b1(e, T_=T_, Tt_=Tt_, Ao=Ao, Bo=Bo):
                        for c in range(nb):
                            os_ = slice(c * CH, (c + 1) * CH)
                            e.matmul(B[0][:, os_], lhsT=Bo[:, os_], rhs=T_[:, os_], start=True, stop=True)
                            ins = e.matmul(B[1][:, os_], lhsT=Ao[:, os_], rhs=Tt_[:, os_], start=True, stop=True)
                        return ins
                    P.op("pe", b1, reads=[mn("A", 0), mn("Bt", 0), mn("T", cur), mn("Tt", cur)], writes=[Bn[0], Bn[1]])
                    yield
                    P.op("act", lambda e, Pm=Pm: e.copy(out=Pm[:, :W], in_=B[0][:, :W]), reads=[Bn[0]], writes=[mn("A", 1)])
                    P.op("dve", lambda e, Rm=Rm: e.tensor_copy(out=Rm[:, :W], in_=B[1][:, :W]), reads=[Bn[1]], writes=[mn("Bt", 1)])
                    yield

                    def b2(e, T_=T_, Tt_=Tt_, Pm=Pm, Rm=Rm, bi=bi):
                        for c in range(nb):
                            os_ = slice(c * CH, (c + 1) * CH)
                            e.matmul(B[3][:, os_], lhsT=T_[:, os_], rhs=Rm[:, os_], start=True, stop=False)
                            ins = e.matmul(B[3][:, os_], lhsT=identb[:], rhs=Tt_[:, os_], start=False, stop=True)
                            if bi == 0:
                                e.matmul(B[2][:, os_], lhsT=Tt_[:, os_], rhs=Pm[:, os_], start=True, stop=False)
                                ins = e.matmul(B[2][:, os_], lhsT=identb[:], rhs=T_[:, os_], start=False, stop=True)
                        return ins
                    P.op("pe", b2, reads=[mn("A", 1), mn("Bt", 1), mn("T", cur), mn("Tt", cur), "dn_ident"], writes=[Bn[2], Bn[3]])
                    yield
                    if bi == 0:
                        P.op("act", lambda e, Ttn=Ttn: e.copy(out=Ttn[:, :W], in_=B[3][:, :W]), reads=[Bn[3]], writes=[mn("Tt", oth)])
                        P.op("act", lambda e, Tn=Tn: e.copy(out=Tn[:, :W], in_=B[2][:, :W]), reads=[Bn[2]], writes=[mn("T", oth)])
                    else:
                        P.op("act", lambda e: e.copy(out=TtAll[d][:, n0 * CH:n0 * CH + W], in_=B[3][:, :W]), reads=[Bn[3]], writes=["dn_TtAll%d" % d])
                    yield
                    cur = oth

                def wmm(e):
                    for c in range(nb):
                        cs = slice((n0 + c) * CH, (n0 + c + 1) * CH)
                        ins = e.matmul(B[0][:, c * CH:(c + 1) * CH], lhsT=tokk[:, c, :], rhs=TtAll[d][:, cs], start=True, stop=True)
                    return ins
                P.op("pe", wmm, reads=["dn_tokk%d" % d, "dn_TtAll%d" % d], writes=[Bn[0]])
                yield
                P.op("act", lambda e: e.activation(out=wTn[d][:, n0 * CH:n0 * CH + W], in_=B[0][:, :W], func=AF.Copy, scale=-1.0), reads=[Bn[0]], writes=["dn_wTn%d" % d])
                yield

            for (t0, tsz) in TT:
                gens = [batch_gen(0, t0, tsz), batch_gen(1, t0, tsz)]
                alive = [True, True]
                while any(alive):
                    for gi_ in range(2):
                        if alive[gi_]:
                            try:
                                next(gens[gi_])
                            except StopIteration:
                                alive[gi_] = False
            orders = [[16, 17] + list(range(16)), [17, 16] + list(range(15, -1, -1))]
            for d in range(2):
                P.op("pool", lambda e, d=d: e.memset(Sf[d][:], 0.0), writes=["dn_S%d" % d])
                P.op("pool", lambda e, d=d: e.memset(Sb[d][:], 0.0), writes=["dn_Sb%d" % d])
            for si in range(NCH):
                for d in range(2):
                    n = orders[d][si]
                    cs = slice(n * CH, (n + 1) * CH)
                    bank = pd[d * 2 + si % 2]
                    bn = pdn[d * 2 + si % 2]
                    p1, p2, p3 = bank[:, 0:128], bank[:, 128:256], bank[:, 256:384]
                    rS, rSb, rv_ = "dn_S%d" % d, "dn_Sb%d" % d, "dn_vnew%d" % d

                    def m1(e, p1=p1, n=n, cs=cs, d=d):
                        e.matmul(p1, lhsT=TtAll[d][:, cs], rhs=tokv[d][:, n, :], start=True, stop=False)
                        return e.matmul(p1, lhsT=wTn[d][:, cs], rhs=Sb[d][:], start=False, stop=True)
                    P.op("pe", m1, reads=["dn_TtAll%d" % d, "dn_tokv%d" % d, "dn_wTn%d" % d, rSb], writes=[bn + "a"])
                    P.op("act", lambda e, p1=p1, d=d: e.copy(out=vnew[d][:], in_=p1), reads=[bn + "a"], writes=[rv_])

                    def m2(e, p2=p2, p3=p3, n=n, cs=cs, d=d):
                        e.matmul(p2, lhsT=tokt[d][:, n, :], rhs=vnew[d][:], start=True, stop=True)
                        e.matmul(p3, lhsT=Sb[d][:], rhs=qg[d][:, cs], start=True, stop=False)
                        return e.matmul(p3, lhsT=vnew[d][:], rhs=Aqk[d][:, cs], start=False, stop=True)
                    P.op("pe", m2, reads=["dn_tokt%d" % d, rv_, rSb, "dn_qg%d" % d, "dn_Aqk%d" % d], writes=[bn + "b"])
                    P.op("dve", lambda e, p2=p2, n=n, d=d: e.scalar_tensor_tensor(out=Sb[d][:], in0=Sf[d][:], scalar=sdcol[d][:, n:n + 1], in1=p2, op0=ALU.mult, op1=ALU.add), reads=[bn + "b", "dn_sd%d" % d, rS], writes=[rSb])
                    P.op("dve", lambda e, p2=p2, n=n, d=d: e.scalar_tensor_tensor(out=Sf[d][:], in0=Sf[d][:], scalar=sdcol[d][:, n:n + 1], in1=p2, op0=ALU.mult, op1=ALU.add), reads=[bn + "b", "dn_sd%d" % d, rS], writes=[rS])
                    tt("dve", O[:, cs], p3, O[:, cs], ALU.add, [bn + "b", "dn_O"], ["dn_O"])
            P.scope_end(mk2)
            mk3 = P.scope_begin()
            zt = P.sbuf("dn_z", [128, NT], F32)
            P.dma("sp", zt[:], S_P[COL_Z + 128 * h:COL_Z + 128 * (h + 1), :], reads=["S_P"], writes=["dn_z"])
            act(zt[:], zt[:], AF.Silu, ["dn_z"], ["dn_z"])
            if "S_O" in debug:
                P.dma("sp", S_O[128 * h:128 * (h + 1), :], O[:], reads=["dn_O"], writes=["S_O"])
            for (t0, tsz) in TT:
                act(sqt[:, :tsz], O[:, t0:t0 + tsz], AF.Square, ["dn_O"], ["dn_sq"])
                ps, psn = next_ps()
                P.op("pe", lambda e, ps=ps, tsz=tsz: e.matmul(ps[:, :tsz], lhsT=ones32[:], rhs=sqt[:, :tsz], start=True, stop=True), reads=["dn_sq", "ones32"], writes=[psn])
                act(rin[:, :tsz], ps[:, :tsz], AF.Sqrt, [psn, "eps_t"], ["dn_rin"], scale=1.0 / 128.0, bias=k.eps_t[:, 0:1])
                P.op("dve", lambda e, tsz=tsz: e.reciprocal(out=rin[:, :tsz], in_=rin[:, :tsz]), reads=["dn_rin"], writes=["dn_rin"])
                P.op("dve", lambda e, t0=t0, tsz=tsz: e.scalar_tensor_tensor(out=rin[:, :tsz], in0=O[:, t0:t0 + tsz], scalar=nw[:, 0:1], in1=rin[:, :tsz], op0=ALU.mult, op1=ALU.mult), reads=["dn_O", "dn_nw", "dn_rin"], writes=["dn_rin"])
                tt("pool", ob[:, t0:t0 + tsz], rin[:, :tsz], zt[:, t0:t0 + tsz], ALU.mult, ["dn_rin", "dn_z"], ["dn_ob"])
            P.dma("sp", S_BR[1][128 * h:128 * (h + 1), :], ob[:], reads=["dn_ob"], writes=["S_BR1"])
            P.scope_end(mk3)
        P.scope_end(mark)

    k.eps_t = P.sbuf("eps_t", [128, 1], F32)
    P.op("pool", lambda e: e.memset(k.eps_t[:], EPS), writes=["eps_t"])

    final = []
    for l in range(depth):
        modulation(l)
        if stop_after == "mod":
            break
        norm_phase("n1", S_X, "S_X", scl1, "scl1", lambda seg, kt: modS[:, seg, kt:kt + 1], S_H, "S_H", True, TT)
        if stop_after == "norm1":
            break
        if "S_P" not in inject:
            linear_to_dram("pin", w_in[l], N_IN, KT, S_H, "S_H", S_P, "S_P", ctx_skip_from=(4224 if l == depth - 1 else None))
        if stop_after == "proj":
            break
        if "S_BR0" not in inject:
            s5_phase(l)
        if stop_after == "s5":
            break
        if "S_BR1" not in inject:
            dn_phase(l)
        if stop_after == "dn":
            break
        if "S_BR2" not in inject:
            cv_phase(l)
        if stop_after == "cv":
            break
        merge_phase(l)
        if stop_after == "merge":
            break
        norm_phase("n2", S_X, "S_X", scl2, "scl2", lambda seg, kt: modS[:, seg, 24 + kt:25 + kt], S_H, "S_H", True, TT[:4] if l == depth - 1 else TT)
        linear_to_dram("fup", ffn_w_up[l], 2 * FH, KT, S_H, "S_H", S_F, "S_F", tts=(TT[:4] if l == depth - 1 else TT))
        mk_wd = P.scope_begin()
        wd_t = P.sbuf("fd_w", [128, FKT, D], BF16)
        load_w_bf16(wd_t, "fd_w", ffn_w_down[l], FKT, D)
        ffn_act_phase(l)
        ffn_down_phase(l, wd_t)
        P.scope_end(mk_wd)
        if stop_after == "ffn":
            break
    else:
        final += norm_phase("nf", S_X, "S_X", fnw_t, "fnw_t", None, outT, "outT", False, TT[:4])

    if "modS" in debug:
        o = nc.dram_tensor("modS_o", [128, 96], F32, kind="ExternalOutput").ap()
        final.append(P.dma("sp", o, modS[:].rearrange("p s c -> p (s c)"), reads=["modS"]))
    for nm in ("S_X", "S_H", "S_P", "S_BR0", "S_BR1", "S_BR2", "S_F", "S_A", "S_O", "outT"):
        t = P.last_w.get(nm)
        if t is not None:
            final.append(t)
    P.finish(final)
    P.emit()
    P.close()
    return nc


def prep_inputs(inputs):
    f = lambda a: np.ascontiguousarray(a, dtype=np.float32)
    x = inputs["x"]
    ctx = inputs["ctx"]
    shared = {}
    shared["ada_w"] = f(inputs["ada_w"])
    shared["ada_bT"] = f(inputs["ada_b"].reshape(DEPTH, 48, 128).transpose(0, 2, 1))
    v = np.stack([inputs[n] for n in ("norm1_w", "norm2_w", "s5_d", "cv_dw_b", "cv_ln_w", "cv_ln_b")], axis=1)
    shared["vec8"] = f(v.reshape(DEPTH, 6, 8, 128).transpose(0, 3, 1, 2))
    shared["fnw"] = f(inputs["final_norm_w"].reshape(8, 128).T)
    shared["w_in"] = f(inputs["w_in"])
    shared["cv_dw_wT"] = f(inputs["cv_dw_w"].reshape(DEPTH, 31, 8, 128).transpose(0, 3, 2, 1))
    for n in ("w_br_s5", "w_br_dn", "w_br_cv", "w_out", "ffn_w_up", "ffn_w_down"):
        shared[n] = f(inputs[n])
    shared["ffn_dw_wT"] = f(inputs["ffn_dw_w"].reshape(DEPTH, 9, FKT, 128).transpose(0, 3, 2, 1))
    shared["ffn_dw_bT"] = f(inputs["ffn_dw_b"].reshape(DEPTH, FKT, 128).transpose(0, 2, 1))
    L = DEPTH
    are, aim, ldt = inputs["s5_a_re"], inputs["s5_a_im"], inputs["s5_log_dt"]
    ldt_b = np.broadcast_to(ldt[:, :, :, None], are.shape)
    aT = np.stack([a.transpose(0, 1, 3, 2).reshape(L, 128, 64) for a in (are, aim, ldt_b)], axis=1)
    shared["s5_aT"] = f(aT)
    def kp_layout(a):
        a5 = a.reshape(L, 2, 8, 8, 64).transpose(0, 2, 3, 1, 4)
        a6 = np.broadcast_to(a5[:, :, :, None, :, :], (L, 8, 8, 16, 2, 64))
        return a6.reshape(L, 8, 128, 128)
    shared["s5_kp"] = f(np.stack([kp_layout(a) for a in (are, aim, ldt_b)], axis=2))
    def b_layout(b):
        b6 = b.reshape(L, 2, 8, 8, 64, 16).transpose(0, 2, 3, 5, 1, 4)
        return b6.reshape(L, 8, 128, 128)
    shared["s5_bT"] = f(np.stack([b_layout(inputs["s5_b_re"]), b_layout(inputs["s5_b_im"])], axis=2))
    def c_layout(c):
        c3 = c.transpose(0, 3, 1, 2).reshape(L, 64, 1024)
        return np.concatenate([c3, c3], axis=1)
    shared["s5_cT"] = f(np.stack([c_layout(inputs["s5_c_re"]), c_layout(inputs["s5_c_im"])], axis=1))
    shared["s5_w_glu"] = f(inputs["s5_w_glu"])
    shared["gmask"] = f((np.arange(128)[:, None] // 16 == np.arange(8)[None, :]).astype(np.float32))
    shared["dn_conv_wT"] = f(inputs["dn_conv_w"].reshape(L, 3, 24, 128).transpose(0, 3, 2, 1))
    dnp = np.zeros((L, 40, 2), np.float32)
    for d_ in range(2):
        dnp[:, 32 * d_:32 * d_ + 8, 0] = inputs["dn_a_log"][:, d_, :]
        dnp[:, 32 * d_:32 * d_ + 8, 1] = inputs["dn_dt_bias"][:, d_, :]
    shared["dnp40"] = dnp
    shared["dn_nw"] = f(inputs["dn_norm_w"].reshape(L, 128, 1))
    sel2 = np.zeros((40, 16, 128), np.float32)
    for d_ in range(2):
        for h_ in range(8):
            sel2[32 * d_ + h_, 8 * d_ + h_, :] = 1.0
    shared["sel2"] = sel2
    cmk = np.ones((40, NT), np.float32)
    cmk[0:8, 0::128] = 0.0
    cmk[32:40, 127::128] = 0.0
    shared["cmask"] = cmk
    pi_ = np.arange(128)[:, None]; fi_ = np.arange(128)[None, :]
    tri_ = np.stack([fi_ >= pi_, fi_ > pi_, fi_ <= pi_, fi_ < pi_, fi_ == pi_,
                     pi_ // 32 == fi_ // 32, (pi_ // 64 == fi_ // 64) & (pi_ // 32 != fi_ // 32), pi_ // 64 != fi_ // 64], axis=1).astype(np.float32)
    tri_[:, 1, :] *= -1.0
    tri_[:, 3, :] *= -1.0
    shared["tri"] = f(tri_)
    shared["ident"] = f(np.eye(128, dtype=np.float32))
    shared["negm"] = f(np.stack([np.where(fi_ >= pi_, 0.0, -30000.0), np.where(fi_ <= pi_, 0.0, -30000.0)], axis=1))
    shared["iota"] = f(np.broadcast_to(np.arange(NT, dtype=np.float32), (128, NT)))
    in_maps = []
    for b in range(8):
        m = dict(shared)
        m["xT"] = f(np.concatenate([x[b].T, ctx[b].T], axis=1))
        cv = np.stack([inputs["c"][b].reshape(8, 128).T, inputs["c_ctx"].reshape(8, 128).T], axis=2)
        m["cvec"] = f(cv.reshape(128, 16))
        in_maps.append(m)
    return in_maps


def kernel(**inputs):
    in_maps = prep_inputs(inputs)
    nc = build()
    res = run_bass_kernel_spmd(nc, in_maps, core_ids=list(range(8)))
    out = np.stack([r["outT"].T for r in res.results], axis=0)
    return np.ascontiguousarray(out, dtype=np.float32)
```

```python
import numpy as np
import concourse.bass as bass
import concourse.mybir as mybir
from concourse.ap import AP
from concourse.bass_utils import run_bass_kernel_spmd

F32 = mybir.dt.float32
BF16 = mybir.dt.bfloat16
I32 = mybir.dt.int32
AF = mybir.ActivationFunctionType
ALU = mybir.AluOpType

ENGS = ("pe", "dve", "act", "pool", "sp")
DMA_RING = {"sp": 12, "act": 6, "pool": 10}


class Prog:
    def __init__(self, nc):
        self.nc = nc
        self.q = {e: [] for e in ENGS}
        self.cnt = {e: 0 for e in ENGS}
        self.last_w = {}
        self.readers = {}
        self.waited = {e: {} for e in ENGS}
        self.dma_n = {k: 0 for k in DMA_RING}
        self.dma_uses = {}
        self.sems = {}
        self._ctx = []
        self.final_tokens = []
        for e in ENGS:
            self.sem("E:" + e)
        for qn, n in DMA_RING.items():
            for i in range(n):
                self.sem("D:%s:%d" % (qn, i))

    def sem(self, key):
        if key not in self.sems:
            g = self.nc.semaphore(key.replace(":", "_"))
            h = g.__enter__()
            self._ctx.append(g)
            self.sems[key] = h
        return self.sems[key]

    def sbuf(self, name, shape, dtype):
        self.uid = getattr(self, "uid", 0) + 1
        g = self.nc.sbuf_tensor("%s_u%d" % (name, self.uid), list(shape), dtype)
        t = g.__enter__()
        self._ctx.append(g)
        return t

    def psum(self, name, shape, dtype=F32):
        self.uid = getattr(self, "uid", 0) + 1
        g = self.nc.psum_tensor("%s_u%d" % (name, self.uid), list(shape), dtype)
        t = g.__enter__()
        self._ctx.append(g)
        return t

    def _deps(self, eng, reads, writes):
        need = {}
        for r in reads:
            t = self.last_w.get(r)
            if t is not None:
                need[t[0]] = max(need.get(t[0], 0), t[1])
        for w in writes:
            t = self.last_w.get(w)
            if t is not None:
                need[t[0]] = max(need.get(t[0], 0), t[1])
            for t in self.readers.get(w, ()):
                need[t[0]] = max(need.get(t[0], 0), t[1])
        out = []
        wd = self.waited[eng]
        for k, v in need.items():
            if wd.get(k, 0) >= v:
                continue
            wd[k] = v
            out.append((k, v))
        return out

    def _commit(self, tok, reads, writes):
        for r in reads:
            self.readers.setdefault(r, []).append(tok)
        for w in writes:
            self.last_w[w] = tok
            self.readers[w] = []

    def op(self, eng, fn, reads=(), writes=()):
        reads = list(reads)
        writes = list(writes)
        waits = self._deps(eng, reads, writes)
        self.cnt[eng] += 1
        tok = ("E:" + eng, self.cnt[eng])
        self._commit(tok, reads, writes)
        self.q[eng].append((fn, waits, ("E:" + eng, 1)))
        return tok

    def dma(self, queue, out, in_, reads=(), writes=(), **kw):
        reads = list(reads)
        writes = list(writes)
        n = self.dma_n[queue]
        self.dma_n[queue] += 1
        slot = n % DMA_RING[queue]
        key = "D:%s:%d" % (queue, slot)
        u = self.dma_uses.get(key, 0) + 1
        self.dma_uses[key] = u
        waits = self._deps(queue, reads, writes)
        if u > 1:
            wd = self.waited[queue]
            if wd.get(key, 0) < 16 * (u - 1):
                wd[key] = 16 * (u - 1)
                waits.append((key, 16 * (u - 1)))
        tok = (key, 16 * u)
        self._commit(tok, reads, writes)

        def fn(e, out=out, in_=in_, kw=kw):
            return e.dma_start(out=out, in_=in_, **kw)

        self.q[queue].append((fn, waits, (key, 16)))
        return tok

    def finish(self, tokens):
        self.final_tokens.extend(tokens)

    def barrier(self):
        allk = {}
        for e in ENGS:
            allk["E:" + e] = self.cnt[e]
        for key, u in self.dma_uses.items():
            allk[key] = 16 * u
        for e in ENGS:
            waits = []
            for kk, v in allk.items():
                if v > 0 and self.waited[e].get(kk, 0) < v:
                    self.waited[e][kk] = v
                    waits.append((kk, v))
            self.q[e].append((None, waits, None))

    def scope_begin(self):
        return len(self._ctx)

    def scope_end(self, mark):
        self.barrier()
        while len(self._ctx) > mark:
            g = self._ctx.pop()
            g.__exit__(None, None, None)

    def emit(self):
        nc = self.nc
        for e in ENGS:
            self.sem("E:" + e)
        for k in list(self.dma_uses):
            self.sem(k)
        engmap = {"pe": "tensor", "dve": "vector", "act": "scalar", "pool": "gpsimd", "sp": "sync"}
        fin = {}
        for t in self.final_tokens:
            fin[t[0]] = max(fin.get(t[0], 0), t[1])
        with nc.Block() as block:
            for e in ENGS:
                items = self.q[e]
                extra = list(fin.items()) if e == "sp" else []

                def body(eng, items=items, extra=extra):
                    for fn, waits, inc in items:
                        for k, v in waits:
                            eng.wait_ge(self.sems[k], v)
                        if fn is None:
                            continue
                        ins = fn(eng)
                        ins.then_inc(self.sems[inc[0]], inc[1])
                    for k, v in extra:
                        eng.wait_ge(self.sems[k], v)

                getattr(block, engmap[e])(body)

    def close(self):
        for g in reversed(self._ctx):
            g.__exit__(None, None, None)
        self._ctx = []


D = 1024
KT = 8
NT = 2304
NX = 2048
NCX = 256
TT = [(0, 512), (512, 512), (1024, 512), (1536, 512), (2048, 256)]
SEGS = [(0, 2048), (2048, 256)]
DEPTH = 2
N_IN = 10272
COL_QKV = 1024
COL_BETA = 4096
COL_DECAY = 4112
COL_Z = 4128
COL_CV = 5152
COL_GATE = 7200
FH = 2816
FKT = 22
EPS = 1e-6
TWO_PI_LO = 6.283185


def sub(ap, p0, pn, dims, off=0):
    a = ap.ap
    pstep = a[0][0]
    return AP(ap.tensor, ap.offset + p0 * pstep + off, [[pstep, pn]] + [list(d) for d in dims])


class K:
    pass


def build(depth=DEPTH, debug=(), stop_after=None, inject=()):
    nc = bass.Bass("TRN2", target_bir_lowering=False)
    P = Prog(nc)
    k = K()
    k.P = P
    k.nc = nc

    def din(name, shape, dtype=F32):
        return nc.dram_tensor(name, list(shape), dtype, kind="ExternalInput").ap()

    def dscr(name, shape, dtype=F32):
        kind = "ExternalOutput" if name in debug else ("ExternalInput" if name in inject else "Internal")
        return nc.dram_tensor(name, list(shape), dtype, kind=kind).ap()

    xT = din("xT", [D, NT])
    cvec = din("cvec", [128, 16])
    ada_w = din("ada_w", [DEPTH, D, 6 * D])
    ada_bT = din("ada_bT", [DEPTH, 128, 48])
    vec8 = din("vec8", [DEPTH, 128, 6, 8])
    fnw = din("fnw", [128, 8])
    w_in = din("w_in", [DEPTH, D, N_IN])
    iota_in = din("iota", [128, NT])
    cv_dw_wT = din("cv_dw_wT", [DEPTH, 128, 8, 31])
    s5_aT = din("s5_aT", [DEPTH, 3, 128, 64])
    s5_kp = din("s5_kp", [DEPTH, 8, 3, 128, 128])
    s5_bT = din("s5_bT", [DEPTH, 8, 2, 128, 128])
    s5_cT = din("s5_cT", [DEPTH, 2, 128, D])
    s5_w_glu = din("s5_w_glu", [DEPTH, D, D])
    gmask_in = din("gmask", [128, 8])
    dn_conv_wT = din("dn_conv_wT", [DEPTH, 128, 24, 3])
    dnp40 = din("dnp40", [DEPTH, 40, 2])
    dn_nw = din("dn_nw", [DEPTH, 128, 1])
    sel2_in = din("sel2", [40, 16, 128])
    cmask_in = din("cmask", [40, NT])
    tri_in = din("tri", [128, 8, 128])
    negm_in = din("negm", [128, 2, 128])
    ident_in = din("ident", [128, 128])
    w_br = [din("w_br_s5", [DEPTH, D, D]), din("w_br_dn", [DEPTH, D, D]), din("w_br_cv", [DEPTH, D, D])]
    w_out = din("w_out", [DEPTH, D, D])
    ffn_w_up = din("ffn_w_up", [DEPTH, D, 2 * FH])
    ffn_w_down = din("ffn_w_down", [DEPTH, FH, D])
    ffn_dw_wT = din("ffn_dw_wT", [DEPTH, 128, FKT, 9])
    ffn_dw_bT = din("ffn_dw_bT", [DEPTH, 128, FKT])
    outT = nc.dram_tensor("outT", [D, NX], F32, kind="ExternalOutput").ap()

    S_X = dscr("S_X", [D, NT])
    S_H = dscr("S_H", [D, NT], BF16)
    S_P = dscr("S_P", [N_IN, NT])
    S_BR = [dscr("S_BR%d" % i, [D, NT], BF16) for i in range(3)]
    S_F = dscr("S_F", [2 * FH, NT])
    S_O = dscr("S_O", [D, NT])
    S_A = dscr("S_A", [FH, NT], BF16)

    ones32 = P.sbuf("ones32", [128, 128], F32)
    P.op("pool", lambda e: e.memset(ones32[:], 1.0), writes=["ones32"])
    modS = P.sbuf("modS", [128, 2, 48], F32)
    scl1 = P.sbuf("scl1", [128, 2, 8], F32)
    scl2 = P.sbuf("scl2", [128, 2, 8], F32)
    v8 = P.sbuf("v8", [128, 6, 8], F32)
    fnw_t = P.sbuf("fnw_t", [128, 2, 8], F32)
    sc_t = P.sbuf("sc_t", [128, 16], F32)
    ps_lin = [P.psum("ps_lin%d" % i, [128, 512], F32) for i in range(3)]
    k.ps_rr = 0

    def next_ps():
        i = k.ps_rr % 3
        k.ps_rr += 1
        return ps_lin[i], "ps_lin%d" % i

    P.dma("sp", sc_t[:], cvec, writes=["sc_t"])
    P.op("act", lambda e: e.activation(out=sc_t[:], in_=sc_t[:], func=AF.Silu), reads=["sc_t"], writes=["sc_t"])
    P.dma("sp", fnw_t[:, 0, :], fnw, writes=["fnw_t"])
    P.dma("sp", fnw_t[:, 1, :], fnw, writes=["fnw_t"])

    def modulation(l):
        mark = P.scope_begin()
        wa = [P.sbuf("adaw%d_%d" % (l, i), [128, 8, 512], BF16) for i in range(2)]
        scb = P.sbuf("scb%d" % l, [128, 16], BF16)
        P.op("act", lambda e: e.copy(out=scb[:], in_=sc_t[:]), reads=["sc_t"], writes=["scb"])
        bT = P.sbuf("adab%d" % l, [128, 48], F32)
        P.dma("sp", bT[:], ada_bT[l], writes=["adab"])
        P.dma("sp", v8[:], vec8[l], writes=["v8"])
        ps, psn = next_ps()
        for cg in range(12):
            w = wa[cg % 2]
            wn = "adaw%d" % (cg % 2)
            P.dma("pool", w[:], ada_w[l][:, cg * 512:(cg + 1) * 512].rearrange("(kt p) c -> p kt c", p=128), writes=[wn])
            for ci in range(4):
                ct = cg * 4 + ci

                def mm(e, w=w, ci=ci, ct=ct, ps=ps):
                    for kt in range(8):
                        ins = e.matmul(ps[:, ct * 2:ct * 2 + 2], lhsT=w[:, kt, ci * 128:(ci + 1) * 128],
                                       rhs=scb[:, kt * 2:kt * 2 + 2], start=(kt == 0), stop=(kt == 7))
                    return ins
                P.op("pe", mm, reads=[wn, "scb"], writes=[psn])
        for s in range(2):
            P.op("dve", lambda e, s=s: e.tensor_tensor(out=modS[:, s, :], in0=sub(ps[:], 0, 128, [(2, 48)], off=s), in1=bT[:], op=ALU.add),
                 reads=[psn, "adab"], writes=["modS"])
        for s in range(2):
            P.op("dve", lambda e, s=s: e.scalar_tensor_tensor(out=scl1[:, s, :], in0=modS[:, s, 8:16], scalar=1.0, in1=v8[:, 0, :], op0=ALU.add, op1=ALU.mult),
                 reads=["modS", "v8"], writes=["scl1"])
            P.op("dve", lambda e, s=s: e.scalar_tensor_tensor(out=scl2[:, s, :], in0=modS[:, s, 32:40], scalar=1.0, in1=v8[:, 1, :], op0=ALU.add, op1=ALU.mult),
                 reads=["modS", "v8"], writes=["scl2"])
        P.scope_end(mark)

    def norm_phase(tag, src, srcname, scale_tab, scale_name, shift_ap_fn, dst, dstname, out_bf16, tts):
        mark = P.scope_begin()
        xt = [P.sbuf("nx_%s%d" % (tag, i), [128, 8, 512], F32) for i in range(2)]
        sq = P.sbuf("nsq_" + tag, [128, 8, 512], F32)
        rs = P.sbuf("nrs_" + tag, [128, 512], F32)
        tmp = [P.sbuf("ntmp_%s%d" % (tag, i), [128, 512], F32) for i in range(2)]
        ho = [P.sbuf("nho_%s%d" % (tag, i), [128, 8, 512], BF16 if out_bf16 else F32) for i in range(2)]
        toks = []

        def n_load(ti):
            t0, tsz = tts[ti]
            P.dma("sp", xt[ti % 2][:, :, :tsz], src[:, t0:t0 + tsz].rearrange("(kt p) t -> p kt t", p=128), reads=[srcname], writes=["nx%d" % (ti % 2)])

        for ti, (t0, tsz) in enumerate(tts):
            seg = 0 if t0 < NX else 1
            x_ = xt[ti % 2]
            xn = "nx%d" % (ti % 2)
            h_ = ho[ti % 2]
            hn = "nho%d" % (ti % 2)
            if ti == 0:
                n_load(0)
            P.op("act", lambda e, x_=x_, tsz=tsz: e.activation(out=sq[:, :, :tsz], in_=x_[:, :, :tsz], func=AF.Square), reads=[xn], writes=["nsq"])
            ps, psn = next_ps()

            def mm(e, ps=ps, tsz=tsz):
                for kt in range(8):
                    ins = e.matmul(ps[:, :tsz], lhsT=ones32[:], rhs=sq[:, kt, :tsz], start=(kt == 0), stop=(kt == 7))
                return ins
            P.op("pe", mm, reads=["nsq", "ones32"], writes=[psn])
            P.op("act", lambda e, ps=ps, tsz=tsz: e.activation(out=rs[:, :tsz], in_=ps[:, :tsz], func=AF.Sqrt, scale=1.0 / D, bias=k.eps_t[:, 0:1]), reads=[psn, "eps_t"], writes=["nrs"])
            P.op("dve", lambda e, tsz=tsz: e.reciprocal(out=rs[:, :tsz], in_=rs[:, :tsz]), reads=["nrs"], writes=["nrs"])
            for kt in range(8):
                t_ = tmp[kt % 2]
                tn = "ntmp%d" % (kt % 2)
                P.op("dve", lambda e, t_=t_, x_=x_, kt=kt, tsz=tsz: e.tensor_tensor(out=t_[:, :tsz], in0=x_[:, kt, :tsz], in1=rs[:, :tsz], op=ALU.mult), reads=[xn, "nrs"], writes=[tn])
                if shift_ap_fn is not None:
                    P.op("act", lambda e, t_=t_, h_=h_, kt=kt, tsz=tsz, seg=seg: e.activation(out=h_[:, kt, :tsz], in_=t_[:, :tsz], func=AF.Identity, scale=scale_tab[:, seg, kt:kt + 1], bias=shift_ap_fn(seg, kt)),
                         reads=[tn, scale_name, "modS"], writes=[hn])
                else:
                    P.op("act", lambda e, t_=t_, h_=h_, kt=kt, tsz=tsz, seg=seg: e.activation(out=h_[:, kt, :tsz], in_=t_[:, :tsz], func=AF.Copy, scale=scale_tab[:, seg, kt:kt + 1]),
                         reads=[tn, scale_name], writes=[hn])
            if ti + 1 < len(tts):
                n_load(ti + 1)
            toks.append(P.dma("sp", dst[:, t0:t0 + tsz].rearrange("(kt p) t -> p kt t", p=128), h_[:, :, :tsz], reads=[hn], writes=[dstname]))
        P.scope_end(mark)
        return toks

    def linear_to_dram(tag, W, ncols, Ktiles, src, srcname, dst, dstname, tts=TT, ctx_skip_from=None):
        mark = P.scope_begin()
        hT = P.sbuf("lin_h_" + tag, [128, Ktiles, NT], BF16)
        for kt in range(Ktiles):
            P.dma("sp", hT[:, kt, :], src[kt * 128:(kt + 1) * 128, :], reads=[srcname], writes=["lin_h"])
        wb = [P.sbuf("lin_w_%s%d" % (tag, i), [128, Ktiles, 512], BF16) for i in range(2)]
        st = [P.sbuf("lin_st_%s%d" % (tag, i), [128, NT], F32) for i in range(2)]
        if len(tts) < len(TT):
            for i in range(2):
                P.op("pool", lambda e, i=i: e.memset(st[i][:], 0.0), writes=["lin_st%d" % i])
        ngrp = (ncols + 511) // 512
        cti = 0
        for g in range(ngrp):
            c0 = g * 512
            gsz = min(512, ncols - c0)
            w = wb[g % 2]
            wn = "lin_w%d" % (g % 2)
            P.dma("pool", w[:, :, :gsz], W[:, c0:c0 + gsz].rearrange("(kt p) c -> p kt c", p=128), writes=[wn])
            for ci in range((gsz + 127) // 128):
                csz = min(128, gsz - ci * 128)
                s_ = st[cti % 2]
                sn = "lin_st%d" % (cti % 2)
                tiles_ = tts
                if ctx_skip_from is not None and c0 + ci * 128 >= ctx_skip_from:
                    tiles_ = [t for t in tts if t[0] < NX]
                for ti, (t0, tsz) in enumerate(tiles_):
                    ps, psn = next_ps()

                    def mm(e, ps=ps, w=w, ci=ci, csz=csz, t0=t0, tsz=tsz):
                        for kt in range(Ktiles):
                            ins = e.matmul(ps[:csz, :tsz], lhsT=w[:, kt, ci * 128:ci * 128 + csz], rhs=hT[:, kt, t0:t0 + tsz], start=(kt == 0), stop=(kt == Ktiles - 1))
                        return ins
                    P.op("pe", mm, reads=[wn, "lin_h"], writes=[psn])
                    if ti % 2 == 0:
                        P.op("act", lambda e, ps=ps, s_=s_, csz=csz, t0=t0, tsz=tsz: e.copy(out=s_[:csz, t0:t0 + tsz], in_=ps[:csz, :tsz]), reads=[psn], writes=[sn])
                    else:
                        P.op("dve", lambda e, ps=ps, s_=s_, csz=csz, t0=t0, tsz=tsz: e.tensor_copy(out=s_[:csz, t0:t0 + tsz], in_=ps[:csz, :tsz]), reads=[psn], writes=[sn])
                r0 = c0 + ci * 128
                P.dma("sp", dst[r0:r0 + csz, :], s_[:csz, :], reads=[sn], writes=[dstname])
                cti += 1
        P.scope_end(mark)


    TT256 = [(i * 256, 256) for i in range(9)]

    def shift_mac(eng, acc, src, wap, sh, lo, hi, reads, writes):
        a = max(lo, lo - sh)
        b = min(hi, hi - sh)
        P.op(eng, lambda e: e.scalar_tensor_tensor(out=acc[:, a:b], in0=src[:, a + sh:b + sh], scalar=wap, in1=acc[:, a:b], op0=ALU.mult, op1=ALU.add),
             reads=reads, writes=writes)

    def cv_phase(l):
        mark = P.scope_begin()
        ycv = P.sbuf("cv_y", [128, 8, NT], F32)
        a_t = [P.sbuf("cv_a%d" % i, [128, NT], F32) for i in range(2)]
        g_t = [P.sbuf("cv_g%d" % i, [128, NT], F32) for i in range(2)]
        wt = P.sbuf("cv_w", [128, 8, 31], F32)
        P.dma("sp", wt[:], cv_dw_wT[l], writes=["cv_w"])
        OX, OC, XPW = 15, 15 + 2048 + 30, 2364
        xps = [P.sbuf("cv_xp%d" % i, [128, XPW], BF16) for i in range(2)]
        dgs = [P.sbuf("cv_dg%d" % i, [128, 31, 128], BF16) for i in range(2)]
        idf = P.sbuf("cv_idf", [128, 128], F32)
        P.dma("sp", idf[:], ident_in, writes=["cv_idf"])
        for i in range(2):
            P.op("pool", lambda e, i=i: e.memset(xps[i][:], 0.0), writes=["cv_xp%d" % i])
        for j in range(8):
            a_ = a_t[j % 2]; an = "cv_a%d" % (j % 2)
            g_ = g_t[j % 2]; gn = "cv_g%d" % (j % 2)
            xp = xps[j % 2]; xpn = "cv_xp%d" % (j % 2)
            dg = dgs[j % 2]; dgn = "cv_dg%d" % (j % 2)
            P.dma("sp", a_[:], S_P[COL_CV + 128 * j:COL_CV + 128 * (j + 1), :], reads=["S_P"], writes=[an])
            P.dma("sp", g_[:], S_P[COL_CV + D + 128 * j:COL_CV + D + 128 * (j + 1), :], reads=["S_P"], writes=[gn])
            P.op("act", lambda e, g_=g_: e.activation(out=g_[:], in_=g_[:], func=AF.Sigmoid), reads=[gn], writes=[gn])
            P.op("pool", lambda e, a_=a_, g_=g_, xp=xp: e.tensor_tensor(out=xp[:, OX:OX + NX], in0=a_[:, 0:NX], in1=g_[:, 0:NX], op=ALU.mult), reads=[an, gn], writes=[xpn])
            P.op("dve", lambda e, a_=a_, g_=g_, xp=xp: e.tensor_tensor(out=xp[:, OC:OC + NCX], in0=a_[:, NX:NT], in1=g_[:, NX:NT], op=ALU.mult), reads=[an, gn], writes=[xpn])

            def mkdiag(e, dg=dg, j=j):
                for kk in range(31):
                    ins = e.activation(out=dg[:, kk, :], in_=idf[:], func=AF.Copy, scale=wt[:, j, kk:kk + 1])
                return ins
            P.op("act", mkdiag, reads=["cv_idf", "cv_w"], writes=[dgn])
            yn = "cv_y%d" % j
            for (t0, tsz) in (TT[:4] if l == depth - 1 else TT):
                base = (OX + t0) if t0 < NX else OC
                ps, psn = next_ps()

                def cmm(e, ps=ps, dg=dg, xp=xp, base=base, tsz=tsz):
                    for kk in range(31):
                        ins = e.matmul(ps[:, :tsz], lhsT=dg[:, kk, :], rhs=xp[:, base + kk - 15:base + kk - 15 + tsz], start=(kk == 0), stop=(kk == 30))
                    return ins
                P.op("pe", cmm, reads=[dgn, xpn], writes=[psn])
                P.op("act", lambda e, ps=ps, j=j, t0=t0, tsz=tsz: e.activation(out=ycv[:, j, t0:t0 + tsz], in_=ps[:, :tsz], func=AF.Identity, bias=v8[:, 3, j:j + 1]), reads=[psn, "v8"], writes=[yn])
        sq = P.sbuf("cv_sq", [128, 8, 512], F32)
        mean = P.sbuf("cv_mean", [128, 512], F32)
        rstd = P.sbuf("cv_rstd", [128, 512], F32)
        msq = P.sbuf("cv_msq", [128, 512], F32)
        tmp = [P.sbuf("cv_tmp%d" % i, [128, 512], F32) for i in range(2)]
        ob = [P.sbuf("cv_ob%d" % i, [128, 8, 512], BF16) for i in range(2)]
        yall = ["cv_y%d" % j for j in range(8)]
        for ti, (t0, tsz) in enumerate(TT[:4] if l == depth - 1 else TT):
            P.op("act", lambda e, t0=t0, tsz=tsz: e.activation(out=sq[:, :, :tsz], in_=ycv[:, :, t0:t0 + tsz], func=AF.Square), reads=yall, writes=["cv_sq"])
            ps1, ps1n = next_ps()
            ps2, ps2n = next_ps()

            def mm(e, ps1=ps1, ps2=ps2, t0=t0, tsz=tsz):
                for kt in range(8):
                    e.matmul(ps1[:, :tsz], lhsT=ones32[:], rhs=ycv[:, kt, t0:t0 + tsz], start=(kt == 0), stop=(kt == 7))
                for kt in range(8):
                    ins = e.matmul(ps2[:, :tsz], lhsT=ones32[:], rhs=sq[:, kt, :tsz], start=(kt == 0), stop=(kt == 7))
                return ins
            P.op("pe", mm, reads=yall + ["cv_sq", "ones32"], writes=[ps1n, ps2n])
            P.op("act", lambda e, ps1=ps1, tsz=tsz: e.activation(out=mean[:, :tsz], in_=ps1[:, :tsz], func=AF.Copy, scale=1.0 / D), reads=[ps1n], writes=["cv_mean"])
            P.op("dve", lambda e, tsz=tsz: e.tensor_tensor(out=msq[:, :tsz], in0=mean[:, :tsz], in1=mean[:, :tsz], op=ALU.mult), reads=["cv_mean"], writes=["cv_msq"])
            P.op("dve", lambda e, ps2=ps2, tsz=tsz: e.scalar_tensor_tensor(out=rstd[:, :tsz], in0=ps2[:, :tsz], scalar=1.0 / D, in1=msq[:, :tsz], op0=ALU.mult, op1=ALU.subtract), reads=[ps2n, "cv_msq"], writes=["cv_rstd"])
            P.op("act", lambda e, tsz=tsz: e.activation(out=rstd[:, :tsz], in_=rstd[:, :tsz], func=AF.Sqrt, bias=k.eps_t[:, 0:1]), reads=["cv_rstd", "eps_t"], writes=["cv_rstd"])
            P.op("dve", lambda e, tsz=tsz: e.reciprocal(out=rstd[:, :tsz], in_=rstd[:, :tsz]), reads=["cv_rstd"], writes=["cv_rstd"])
            o_ = ob[ti % 2]; on = "cv_ob%d" % (ti % 2)
            for kt in range(8):
                t_ = tmp[kt % 2]; tn = "cv_tmp%d" % (kt % 2)
                P.op("dve", lambda e, t_=t_, kt=kt, t0=t0, tsz=tsz: e.tensor_tensor(out=t_[:, :tsz], in0=ycv[:, kt, t0:t0 + tsz], in1=mean[:, :tsz], op=ALU.subtract), reads=["cv_y%d" % kt, "cv_mean"], writes=[tn])
                P.op("pool", lambda e, t_=t_, tsz=tsz: e.tensor_tensor(out=t_[:, :tsz], in0=t_[:, :tsz], in1=rstd[:, :tsz], op=ALU.mult), reads=[tn, "cv_rstd"], writes=[tn])
                P.op("act", lambda e, t_=t_, o_=o_, kt=kt, tsz=tsz: e.activation(out=o_[:, kt, :tsz], in_=t_[:, :tsz], func=AF.Silu, scale=v8[:, 4, kt:kt + 1], bias=v8[:, 5, kt:kt + 1]), reads=[tn, "v8"], writes=[on])
            P.dma("sp", S_BR[2][:, t0:t0 + tsz].rearrange("(kt p) t -> p kt t", p=128), o_[:, :, :tsz], reads=[on], writes=["S_BR2"])
        P.scope_end(mark)

    def load_w_bf16(dst, dname, W, Ktiles, ncols):
        for c0 in range(0, ncols, 512):
            cs = min(512, ncols - c0)
            for k0 in range(0, Ktiles, 8):
                k1 = min(Ktiles, k0 + 8)
                P.dma("pool", dst[:, k0:k1, c0:c0 + cs], W[k0 * 128:k1 * 128, c0:c0 + cs].rearrange("(kt p) c -> p kt c", p=128), writes=[dname])

    def merge_phase(l):
        mark = P.scope_begin()
        wb = [P.sbuf("mg_w%d" % i, [128, 8, D], BF16) for i in range(4)]
        for i in range(3):
            load_w_bf16(wb[i], "mg_w%d" % i, w_br[i][l], 8, D)
        load_w_bf16(wb[3], "mg_w3", w_out[l], 8, D)
        brs = [[P.sbuf("mg_br%d_%d" % (i, p_), [128, 8, 256], BF16) for i in range(3)] for p_ in range(2)]
        gt = [P.sbuf("mg_g%d" % i, [128, 8, 256], F32) for i in range(2)]
        mg = P.sbuf("mg_m", [128, 8, 256], F32)
        mgb = P.sbuf("mg_mb", [128, 8, 256], BF16)
        xts = [P.sbuf("mg_x%d" % p_, [128, 8, 256], F32) for p_ in range(2)]
        tmp = [P.sbuf("mg_t%d" % i, [128, 256], F32) for i in range(2)]
        gi = 0

        def mg_loads(ti):
            t0, tsz = TT256[ti]
            xsrc, xsn = (xT, "xT") if l == 0 else (S_X, "S_X")
            P.dma("sp", xts[ti % 2][:], xsrc[:, t0:t0 + tsz].rearrange("(kt p) t -> p kt t", p=128), reads=[xsn], writes=["mg_x%d" % (ti % 2)])
            for b in range(3):
                P.dma("sp", brs[ti % 2][b][:], S_BR[b][:, t0:t0 + tsz].rearrange("(kt p) t -> p kt t", p=128), reads=["S_BR%d" % b], writes=["mg_br%d_%d" % (b, ti % 2)])

        mtiles = TT256[:8] if l == depth - 1 else TT256
        for ti, (t0, tsz) in enumerate(mtiles):
            seg = 0 if t0 < NX else 1
            if ti == 0:
                mg_loads(0)
            xt = xts[ti % 2]; xtn = "mg_x%d" % (ti % 2)
            br = brs[ti % 2]
            brn = ["mg_br%d_%d" % (b, ti % 2) for b in range(3)]
            for b in range(3):
                g_ = gt[gi % 2]; gn = "mg_g%d" % (gi % 2); gi += 1
                r0 = COL_GATE + b * D
                P.dma("sp", g_[:], S_P[r0:r0 + D, t0:t0 + tsz].rearrange("(kt p) t -> p kt t", p=128), reads=["S_P"], writes=[gn])
                P.op("act", lambda e, g_=g_: e.activation(out=g_[:], in_=g_[:], func=AF.Sigmoid), reads=[gn], writes=[gn])
                for ct in range(8):
                    ps, psn = next_ps()

                    def mm(e, ps=ps, b=b, ct=ct, br=br):
                        for kt in range(8):
                            ins = e.matmul(ps[:, :256], lhsT=wb[b][:, kt, ct * 128:(ct + 1) * 128], rhs=br[b][:, kt, :], start=(kt == 0), stop=(kt == 7))
                        return ins
                    P.op("pe", mm, reads=["mg_w%d" % b, brn[b]], writes=[psn])
                    mn = "mg_m%d" % ct
                    if b == 0:
                        P.op("dve", lambda e, ps=ps, g_=g_, ct=ct: e.tensor_tensor(out=mg[:, ct, :], in0=ps[:, :256], in1=g_[:, ct, :], op=ALU.mult), reads=[psn, gn], writes=[mn])
                    else:
                        t_ = tmp[ct % 2]; tn = "mg_t%d" % (ct % 2)
                        P.op("dve", lambda e, ps=ps, g_=g_, ct=ct, t_=t_: e.tensor_tensor(out=t_[:], in0=ps[:, :256], in1=g_[:, ct, :], op=ALU.mult), reads=[psn, gn], writes=[tn])
                        P.op("pool", lambda e, ct=ct, t_=t_: e.tensor_tensor(out=mg[:, ct, :], in0=mg[:, ct, :], in1=t_[:], op=ALU.add), reads=[tn, mn], writes=[mn])
            mall = ["mg_m%d" % ct for ct in range(8)]
            P.op("act", lambda e: e.copy(out=mgb[:], in_=mg[:]), reads=mall, writes=["mg_mb"])
            for ct in range(8):
                ps, psn = next_ps()

                def mm2(e, ps=ps, ct=ct):
                    for kt in range(8):
                        ins = e.matmul(ps[:, :256], lhsT=wb[3][:, kt, ct * 128:(ct + 1) * 128], rhs=mgb[:, kt, :], start=(kt == 0), stop=(kt == 7))
                    return ins
                P.op("pe", mm2, reads=["mg_w3", "mg_mb"], writes=[psn])
                P.op("dve", lambda e, ps=ps, ct=ct, seg=seg, xt=xt: e.scalar_tensor_tensor(out=xt[:, ct, :], in0=ps[:, :256], scalar=modS[:, seg, 16 + ct:17 + ct], in1=xt[:, ct, :], op0=ALU.mult, op1=ALU.add),
                     reads=[psn, "modS", xtn], writes=[xtn])
            if ti + 1 < len(mtiles):
                mg_loads(ti + 1)
            P.dma("sp", S_X[:, t0:t0 + tsz].rearrange("(kt p) t -> p kt t", p=128), xt[:], reads=[xtn], writes=["S_X"])
        P.scope_end(mark)

    def ffn_act_phase(l):
        mark = P.scope_begin()
        wt = P.sbuf("ff_w", [128, FKT, 9], F32)
        bt = P.sbuf("ff_b", [128, FKT], F32)
        idf = P.sbuf("ff_idf", [128, 128], F32)
        P.dma("sp", wt[:], ffn_dw_wT[l], writes=["ff_w"])
        P.dma("sp", bt[:], ffn_dw_bT[l], writes=["ff_b"])
        P.dma("sp", idf[:], ident_in, writes=["ff_idf"])
        PADX = 65
        OX, OC = PADX, PADX + NX + PADX + 1
        XPW = OC + NCX + 1
        a_t = [P.sbuf("ff_a%d" % i, [128, NT], F32) for i in range(2)]
        v_t = [P.sbuf("ff_v%d" % i, [128, NT], F32) for i in range(2)]
        xv = [[P.sbuf("ff_xp%d_%d" % (i, m), [128, XPW], BF16) for m in range(3)] for i in range(2)]
        dgs = [P.sbuf("ff_dg%d" % i, [128, 9, 128], BF16) for i in range(2)]
        acc = [P.sbuf("ff_acc%d" % i, [128, NT], F32) for i in range(2)]
        ob = [P.sbuf("ff_ob%d" % i, [128, NT], BF16) for i in range(2)]
        for i in range(2):
            for m in range(3):
                P.op("pool", lambda e, i=i, m=m: e.memset(xv[i][m][:], 0.0), writes=["ff_xp%d_%d" % (i, m)])
        def ff_loads(c):
            P.dma("sp", a_t[c % 2][:], S_F[128 * c:128 * (c + 1), :], reads=["S_F"], writes=["ff_a%d" % (c % 2)])
            P.dma("sp", v_t[c % 2][:], S_F[FH + 128 * c:FH + 128 * (c + 1), :], reads=["S_F"], writes=["ff_v%d" % (c % 2)])

        for c in range(FKT):
            a_ = a_t[c % 2]; an = "ff_a%d" % (c % 2)
            v_ = v_t[c % 2]; vn = "ff_v%d" % (c % 2)
            ac = acc[c % 2]; acn = "ff_acc%d" % (c % 2)
            o_ = ob[c % 2]; on = "ff_ob%d" % (c % 2)
            X = xv[c % 2]; xn = "ff_xp%d" % (c % 2)
            dg = dgs[c % 2]; dgn = "ff_dg%d" % (c % 2)
            if c == 0:
                ff_loads(0)
            xn0, xn1, xn2 = xn + "_0", xn + "_1", xn + "_2"
            P.op("act", lambda e, a_=a_, X=X: e.copy(out=X[0][:, OX:OX + NX], in_=a_[:, 0:NX]), reads=[an], writes=[xn0])
            P.op("act", lambda e, a_=a_, X=X: e.copy(out=X[0][:, OC:OC + NCX], in_=a_[:, NX:NT]), reads=[an], writes=[xn0])
            P.op("act", lambda e, a_=a_, X=X: e.copy(out=X[1][:, OX:OX + NX], in_=a_[:, 0:NX]), reads=[an], writes=[xn1])
            P.op("dve", lambda e, a_=a_, X=X: e.tensor_copy(out=X[2][:, OX:OX + NX], in_=a_[:, 0:NX]), reads=[an], writes=[xn2])
            P.op("pool", lambda e, X=X: e.memset(sub(X[1][:], 0, 128, [(64, 32)], off=OX + 63), 0.0), reads=[], writes=[xn1])
            P.op("dve", lambda e, X=X: e.memset(sub(X[2][:], 0, 128, [(64, 32)], off=OX), 0.0), reads=[], writes=[xn2])

            def mkdiag(e, dg=dg, c=c):
                for kk in range(9):
                    ins = e.activation(out=dg[:, kk, :], in_=idf[:], func=AF.Copy, scale=wt[:, c, kk:kk + 1])
                return ins
            P.op("act", mkdiag, reads=["ff_idf", "ff_w"], writes=[dgn])
            for (t0, tsz) in TT:
                ps, psn = next_ps()
                if t0 < NX:
                    def cmm(e, ps=ps, dg=dg, X=X, t0=t0, tsz=tsz):
                        i_ = 0
                        for dr in (-1, 0, 1):
                            for dw in (-1, 0, 1):
                                src = X[0] if dw == 0 else (X[1] if dw == -1 else X[2])
                                o0 = OX + t0 + 64 * dr + dw
                                ins = e.matmul(ps[:, :tsz], lhsT=dg[:, (dr + 1) * 3 + (dw + 1), :], rhs=src[:, o0:o0 + tsz], start=(i_ == 0), stop=(i_ == 8))
                                i_ += 1
                        return ins
                else:
                    def cmm(e, ps=ps, dg=dg, X=X, t0=t0, tsz=tsz):
                        for i_, dw in enumerate((-1, 0, 1)):
                            ins = e.matmul(ps[:, :tsz], lhsT=dg[:, 3 + (dw + 1), :], rhs=X[0][:, OC + dw:OC + dw + tsz], start=(i_ == 0), stop=(i_ == 2))
                        return ins
                P.op("pe", cmm, reads=[dgn, xn0, xn1, xn2], writes=[psn])
                P.op("act", lambda e, ps=ps, ac=ac, c=c, t0=t0, tsz=tsz: e.activation(out=ac[:, t0:t0 + tsz], in_=ps[:, :tsz], func=AF.Silu, bias=bt[:, c:c + 1]), reads=[psn, "ff_b"], writes=[acn])
                P.op("dve", lambda e, ac=ac, v_=v_, o_=o_, t0=t0, tsz=tsz: e.tensor_tensor(out=o_[:, t0:t0 + tsz], in0=ac[:, t0:t0 + tsz], in1=v_[:, t0:t0 + tsz], op=ALU.mult), reads=[acn, vn], writes=[on])
            if c + 1 < FKT:
                ff_loads(c + 1)
            P.dma("sp", S_A[128 * c:128 * (c + 1), :], o_[:], reads=[on], writes=["S_A"])
        P.scope_end(mark)

    def ffn_down_phase(l, wd):
        mark = P.scope_begin()
        at = [P.sbuf("fd_a%d" % i, [128, FKT, 256], BF16) for i in range(2)]
        xt = [P.sbuf("fd_x%d" % i, [128, 8, 256], F32) for i in range(2)]
        def fd_loads(ti):
            t0, tsz = TT256[ti]
            P.dma("sp", xt[ti % 2][:], S_X[:, t0:t0 + tsz].rearrange("(kt p) t -> p kt t", p=128), reads=["S_X"], writes=["fd_x%d" % (ti % 2)])
            P.dma("sp", at[ti % 2][:], S_A[:, t0:t0 + tsz].rearrange("(kt p) t -> p kt t", p=128), reads=["S_A"], writes=["fd_a%d" % (ti % 2)])

        dtiles = TT256[:8] if l == depth - 1 else TT256
        for ti, (t0, tsz) in enumerate(dtiles):
            seg = 0 if t0 < NX else 1
            a_ = at[ti % 2]; an = "fd_a%d" % (ti % 2)
            x_ = xt[ti % 2]; xn = "fd_x%d" % (ti % 2)
            if ti == 0:
                fd_loads(0)
            for ct in range(8):
                ps, psn = next_ps()

                def mm(e, ps=ps, ct=ct, a_=a_):
                    for kt in range(FKT):
                        ins = e.matmul(ps[:, :256], lhsT=wd[:, kt, ct * 128:(ct + 1) * 128], rhs=a_[:, kt, :], start=(kt == 0), stop=(kt == FKT - 1))
                    return ins
                P.op("pe", mm, reads=["fd_w", an], writes=[psn])
                P.op("dve", lambda e, ps=ps, ct=ct, seg=seg, x_=x_: e.scalar_tensor_tensor(out=x_[:, ct, :], in0=ps[:, :256], scalar=modS[:, seg, 40 + ct:41 + ct], in1=x_[:, ct, :], op0=ALU.mult, op1=ALU.add),
                     reads=[psn, "modS", xn], writes=[xn])
            if ti + 1 < len(dtiles):
                fd_loads(ti + 1)
            P.dma("sp", S_X[:, t0:t0 + tsz].rearrange("(kt p) t -> p kt t", p=128), x_[:], reads=[xn], writes=["S_X"])
        P.scope_end(mark)

    def s5_phase(l):
        mark0 = P.scope_begin()
        Y = P.sbuf("s5_Y", [128, 8, NT], BF16)
        mark = P.scope_begin()
        INV2PI = 1.0 / (2.0 * np.pi)
        PI_LO = TWO_PI_LO / 2.0
        iot = P.sbuf("s5_iota", [128, NT], F32)
        P.dma("sp", iot[:], iota_in, writes=["s5_iota"])
        gm = P.sbuf("s5_gm", [128, 8], F32)
        P.dma("sp", gm[:], gmask_in, writes=["s5_gm"])
        one_t = P.sbuf("s5_one", [128, 1], F32)
        P.op("pool", lambda e: e.memset(one_t[:], 1.0), writes=["s5_one"])
        hpi_t = P.sbuf("s5_hpi", [128, 1], F32)
        P.op("pool", lambda e: e.memset(hpi_t[:], float(np.pi / 2.0)), writes=["s5_hpi"])
        aT = P.sbuf("s5_aT", [128, 3, 64], F32)
        for i in range(3):
            P.dma("sp", aT[:, i, :], s5_aT[l, i], writes=["s5_aTt"])
        rho = P.sbuf("s5_rho", [128, 64], F32)
        thn = P.sbuf("s5_thn", [128, 64], F32)
        dtt = P.sbuf("s5_dtt", [128, 64], F32)
        P.op("act", lambda e: e.activation(out=dtt[:], in_=aT[:, 2, :], func=AF.Exp), reads=["s5_aTt"], writes=["s5_dtt"])
        P.op("dve", lambda e: e.tensor_tensor(out=rho[:], in0=aT[:, 0, :], in1=dtt[:], op=ALU.mult), reads=["s5_aTt", "s5_dtt"], writes=["s5_rho"])
        P.op("act", lambda e: e.activation(out=rho[:], in_=rho[:], func=AF.Exp), reads=["s5_rho"], writes=["s5_rho"])
        P.op("dve", lambda e: e.scalar_tensor_tensor(out=thn[:], in0=aT[:, 1, :], scalar=INV2PI, in1=dtt[:], op0=ALU.mult, op1=ALU.mult), reads=["s5_aTt", "s5_dtt"], writes=["s5_thn"])
        cT = P.sbuf("s5_cT", [128, 2, D], F32)
        for i in range(2):
            P.dma("sp", cT[:, i, :], s5_cT[l, i], writes=["s5_cTt"])
        BL = P.sbuf("s5_BL", [128, 8, 4, 128], BF16)
        CZ = P.sbuf("s5_CZ", [128, 8, 6, 128], BF16)
        P.op("pool", lambda e: e.memset(BL[:], 0.0), writes=["s5_BL"])
        P.op("pool", lambda e: e.memset(CZ[:], 0.0), writes=["s5_CZ"])
        kp = P.sbuf("s5_kp", [128, 3, 128], F32)
        bT = P.sbuf("s5_bT", [128, 2, 128], F32)
        sm = [P.sbuf("s5_sm%d" % i, [128, 128], F32) for i in range(10)]
        smi = P.sbuf("s5_smi", [128, 128], I32)
        f_t = P.sbuf("s5_f", [128, NT], F32)
        fi_t = P.sbuf("s5_fi", [128, NT], F32)
        sn_b = [P.sbuf("s5_sn%d" % i, [128, NT], BF16) for i in range(2)]
        cs_b = [P.sbuf("s5_cs%d" % i, [128, NT], BF16) for i in range(2)]
        bsb2 = [P.sbuf("s5_bsb%d" % i, [128, 2, NT], BF16) for i in range(2)]
        tfull = [P.sbuf("s5_tf%d" % i, [128, NT], BF16) for i in range(4)]
        negC = P.sbuf("s5_negC", [128, 1], F32)
        posC = P.sbuf("s5_posC", [128, 1], F32)
        P.op("pool", lambda e: e.memset(negC[:], -12582912.0), writes=["s5_C"])
        P.op("pool", lambda e: e.memset(posC[:], 12582912.0), writes=["s5_C"])
        wr_t = P.sbuf("s5_wr", [128, NT], BF16)
        wi_t = P.sbuf("s5_wi", [128, NT], BF16)
        tpf = [P.sbuf("s5_tpf%d" % i, [128, 512], F32) for i in range(2)]
        pr_t = [P.sbuf("s5_pr%d" % i, [128, NT], BF16) for i in range(4)]
        u_t = P.sbuf("s5_u", [128, NT], F32)
        ub_t = P.sbuf("s5_ub", [128, NT], BF16)
        yps = [P.psum("s5_yps%d" % i, [128, 512], F32) for i in range(5)]
        ypn = ["s5_yps%d" % i for i in range(5)]

        def sincos(eng, f_ap, fi_ap, sn_ap, cs_ap, fn, fin, snn, csn):
            P.op(eng, lambda e: e.tensor_copy(out=fi_ap, in_=f_ap), reads=[fn], writes=[fin])
            P.op(eng, lambda e: e.tensor_copy(out=cs_ap, in_=fi_ap), reads=[fin], writes=[csn])
            P.op(eng, lambda e: e.tensor_tensor(out=f_ap, in0=f_ap, in1=cs_ap, op=ALU.subtract), reads=[fn, csn], writes=[fn])
            P.op("act", lambda e: e.activation(out=sn_ap, in_=f_ap, func=AF.Sin, scale=TWO_PI_LO), reads=[fn], writes=[snn])
            P.op("act", lambda e: e.activation(out=cs_ap, in_=f_ap, func=AF.Sin, scale=PI_LO), reads=[fn], writes=[csn])
            P.op("act", lambda e: e.activation(out=cs_ap, in_=cs_ap, func=AF.Square, scale=float(np.sqrt(2.0))), reads=[csn], writes=[csn])
            P.op("act", lambda e: e.activation(out=cs_ap, in_=cs_ap, func=AF.Identity, scale=-1.0, bias=one_t[:, 0:1]), reads=[csn, "s5_one"], writes=[csn])

        def tt(eng, out, a, b, op, r, w):
            P.op(eng, lambda e: e.tensor_tensor(out=out, in0=a, in1=b, op=op), reads=r, writes=w)

        REG = [(0, 256, 2048, 2303)] + [(256 + 256 * i, 256, 256 * i, 2047 - 256 * i) for i in range(8)]

        def gen_tables(g):
            sn_t = sn_b[g % 2]; cs_t = cs_b[g % 2]
            snn = "s5_sn%d" % (g % 2); csn = "s5_cs%d" % (g % 2)
            P.op("act", lambda e, g=g: e.activation(out=f_t[:], in_=iot[:], func=AF.Copy, scale=thn[:, g:g + 1]), reads=["s5_iota", "s5_thn"], writes=["s5_f"])
            P.op("act", lambda e: e.activation(out=fi_t[:], in_=f_t[:], func=AF.Identity, bias=posC[:, 0:1]), reads=["s5_f", "s5_C"], writes=["s5_fi"])
            P.op("act", lambda e: e.activation(out=fi_t[:], in_=fi_t[:], func=AF.Identity, bias=negC[:, 0:1]), reads=["s5_fi", "s5_C"], writes=["s5_fi"])
            P.op("pool", lambda e: e.tensor_tensor(out=f_t[:], in0=f_t[:], in1=fi_t[:], op=ALU.subtract), reads=["s5_f", "s5_fi"], writes=["s5_f"])
            P.op("act", lambda e, sn_t=sn_t: e.activation(out=sn_t[:], in_=f_t[:], func=AF.Sin, scale=TWO_PI_LO), reads=["s5_f"], writes=[snn])
            P.op("act", lambda e: e.activation(out=fi_t[:], in_=f_t[:], func=AF.Abs), reads=["s5_f"], writes=["s5_fi"])
            P.op("act", lambda e, cs_t=cs_t: e.activation(out=cs_t[:], in_=fi_t[:], func=AF.Sin, scale=-TWO_PI_LO, bias=hpi_t[:, 0:1]), reads=["s5_fi", "s5_hpi"], writes=[csn])

        k.s5_pending = None
        for j in range(8):
            for i in range(3):
                P.dma("sp", kp[:, i, :], s5_kp[l, j, i], writes=["s5_kp"])
            for i in range(2):
                P.dma("sp", bT[:, i, :], s5_bT[l, j, i], writes=["s5_bTt"])
            are, aim, ldt = kp[:, 0, :], kp[:, 1, :], kp[:, 2, :]
            n = lambda i: "s5_sm%d" % i
            S = [t[:] for t in sm]
            P.op("act", lambda e: e.activation(out=S[0], in_=ldt, func=AF.Exp), reads=["s5_kp"], writes=[n(0)])
            tt("dve", S[1], are, S[0], ALU.mult, ["s5_kp", n(0)], [n(1)])
            P.op("act", lambda e: e.activation(out=S[1], in_=S[1], func=AF.Exp), reads=[n(1)], writes=[n(1)])
            P.op("dve", lambda e: e.scalar_tensor_tensor(out=S[2], in0=aim, scalar=INV2PI, in1=S[0], op0=ALU.mult, op1=ALU.mult), reads=["s5_kp", n(0)], writes=[n(2)])
            sincos("dve", S[2], smi[:], S[3], S[4], n(2), "s5_smi", n(3), n(4))
            tt("dve", S[5], S[1], S[4], ALU.mult, [n(1), n(4)], [n(5)])
            P.op("dve", lambda e: e.tensor_scalar_add(out=S[5], in0=S[5], scalar1=-1.0), reads=[n(5)], writes=[n(5)])
            tt("dve", S[6], S[1], S[3], ALU.mult, [n(1), n(3)], [n(6)])
            tt("dve", S[7], are, are, ALU.mult, ["s5_kp"], [n(7)])
            tt("dve", S[8], aim, aim, ALU.mult, ["s5_kp"], [n(8)])
            tt("dve", S[7], S[7], S[8], ALU.add, [n(7), n(8)], [n(7)])
            P.op("dve", lambda e: e.reciprocal(out=S[7], in_=S[7]), reads=[n(7)], writes=[n(7)])
            tt("dve", S[8], S[5], are, ALU.mult, [n(5), "s5_kp"], [n(8)])
            tt("dve", S[9], S[6], aim, ALU.mult, [n(6), "s5_kp"], [n(9)])
            tt("dve", S[8], S[8], S[9], ALU.add, [n(8), n(9)], [n(8)])
            tt("dve", S[8], S[8], S[7], ALU.mult, [n(8), n(7)], [n(8)])
            tt("dve", S[9], S[6], are, ALU.mult, [n(6), "s5_kp"], [n(9)])
            tt("dve", S[0], S[5], aim, ALU.mult, [n(5), "s5_kp"], [n(0)])
            tt("dve", S[9], S[9], S[0], ALU.subtract, [n(9), n(0)], [n(9)])
            tt("dve", S[9], S[9], S[7], ALU.mult, [n(9), n(7)], [n(9)])
            br_, bi_ = bT[:, 0, :], bT[:, 1, :]
            tt("dve", S[0], S[8], br_, ALU.mult, [n(8), "s5_bTt"], [n(0)])
            tt("dve", S[1], S[9], bi_, ALU.mult, [n(9), "s5_bTt"], [n(1)])
            tt("dve", S[0], S[0], S[1], ALU.subtract, [n(0), n(1)], [n(0)])
            tt("dve", S[2], S[8], bi_, ALU.mult, [n(8), "s5_bTt"], [n(2)])
            tt("dve", S[3], S[9], br_, ALU.mult, [n(9), "s5_bTt"], [n(3)])
            tt("dve", S[2], S[2], S[3], ALU.add, [n(2), n(3)], [n(2)])
            for gl in range(8):
                g = 8 * j + gl
                for ri, src in ((0, sm[0]), (1, sm[2])):
                    P.op("dve", lambda e, gl=gl, ri=ri, src=src: e.tensor_scalar(out=BL[:, gl, 2 * ri, 0:64], in0=src[:, 0:64], scalar1=gm[:, gl:gl + 1], scalar2=None, op0=ALU.mult),
                         reads=[n(0), n(2), "s5_gm"], writes=["s5_BL"])
                    P.op("dve", lambda e, gl=gl, ri=ri, src=src: e.tensor_scalar(out=BL[:, gl, 2 * ri + 1, 64:128], in0=src[:, 64:128], scalar1=gm[:, gl:gl + 1], scalar2=None, op0=ALU.mult),
                         reads=[n(0), n(2), "s5_gm"], writes=["s5_BL"])
                for hh in range(2):
                    pp = slice(64 * hh, 64 * hh + 64)
                    P.op("pool", lambda e, gl=gl, g=g, hh=hh, pp=pp: e.tensor_copy(out=CZ[pp, gl, 3 * hh, 16 * gl:16 * gl + 16], in_=cT[pp, 0, 16 * g:16 * g + 16]), reads=["s5_cTt"], writes=["s5_CZ"])
                    P.op("pool", lambda e, gl=gl, g=g, hh=hh, pp=pp: e.tensor_scalar(out=CZ[pp, gl, 3 * hh + 1, 16 * gl:16 * gl + 16], in0=cT[pp, 0, 16 * g:16 * g + 16], scalar1=-1.0, scalar2=None, op0=ALU.mult), reads=["s5_cTt"], writes=["s5_CZ"])
                    P.op("pool", lambda e, gl=gl, g=g, hh=hh, pp=pp: e.tensor_scalar(out=CZ[pp, gl, 3 * hh + 2, 16 * gl:16 * gl + 16], in0=cT[pp, 1, 16 * g:16 * g + 16], scalar1=-1.0, scalar2=None, op0=ALU.mult), reads=["s5_cTt"], writes=["s5_CZ"])
            P.dma("sp", u_t[:], S_P[128 * j:128 * (j + 1), :], reads=["S_P"], writes=["s5_u"])
            P.op("act", lambda e: e.copy(out=ub_t[:], in_=u_t[:]), reads=["s5_u"], writes=["s5_ub"])

            def bu_evac(gl_, j=j):
                g_ = 8 * j + gl_
                bs_ = bsb2[g_ % 2]; bsn = "s5_bsb%d" % (g_ % 2)
                for (tau0, nn, fc, bc) in REG:
                    k.s5_rr = getattr(k, "s5_rr", 0) + 1
                    bk = ps_lin[k.s5_rr % 3]; bkn = "ps_lin%d" % (k.s5_rr % 3)
                    bre = bk[:, 0:256]; bim = bk[:, 256:512]

                    def mm(e, gl_=gl_, nn=nn, fc=fc, bc=bc, bre=bre, bim=bim):
                        rev = sub(ub_t[:], 0, 128, [(-1, nn)], off=bc)
                        e.matmul(bre[:, :nn], lhsT=BL[:, gl_, 0, :], rhs=ub_t[:, fc:fc + nn], start=True, stop=False)
                        e.matmul(bre[:, :nn], lhsT=BL[:, gl_, 1, :], rhs=rev, start=False, stop=True)
                        e.matmul(bim[:, :nn], lhsT=BL[:, gl_, 2, :], rhs=ub_t[:, fc:fc + nn], start=True, stop=False)
                        return e.matmul(bim[:, :nn], lhsT=BL[:, gl_, 3, :], rhs=rev, start=False, stop=True)
                    P.op("pe", mm, reads=["s5_BL", "s5_ub"], writes=[bkn])
                    P.op("act", lambda e, bk=bk, bs_=bs_, tau0=tau0: e.copy(out=bs_[:, :, tau0:tau0 + 256], in_=bk[:].rearrange("p (c n) -> p c n", c=2)), reads=[bkn], writes=[bsn])

            for gl in range(8):
                g = 8 * j + gl
                gi_ = 8 * j + gl
                sn_t = sn_b[gi_ % 2]; cs_t = cs_b[gi_ % 2]
                snn = "s5_sn%d" % (gi_ % 2); csn = "s5_cs%d" % (gi_ % 2)
                if gi_ == 0:
                    gen_tables(0)
                if gi_ + 1 < 64:
                    gen_tables(gi_ + 1)
                if gl == 0:
                    bu_evac(gl)
                if gl < 7:
                    bu_evac(gl + 1)
                bs_ = bsb2[gi_ % 2]; bsn = "s5_bsb%d" % (gi_ % 2)
                tt("dve", tfull[0][:], bs_[:, 0, :], cs_t[:], ALU.mult, [bsn, csn], ["s5_tf0"])
                tt("dve", tfull[1][:], bs_[:, 1, :], sn_t[:], ALU.mult, [bsn, snn], ["s5_tf1"])
                tt("dve", tfull[2][:], bs_[:, 1, :], cs_t[:], ALU.mult, [bsn, csn], ["s5_tf2"])
                tt("dve", tfull[3][:], bs_[:, 0, :], sn_t[:], ALU.mult, [bsn, snn], ["s5_tf3"])
                tt("dve", wr_t[:], tfull[0][:], tfull[1][:], ALU.add, ["s5_tf0", "s5_tf1"], ["s5_wr"])
                tt("dve", wi_t[:], tfull[2][:], tfull[3][:], ALU.subtract, ["s5_tf2", "s5_tf3"], ["s5_wi"])
                rho_b = sub(rho[:], 0, 128, [(0, NT)], off=g)
                P.op("dve", lambda e, rho_b=rho_b: e.tensor_tensor_scan(out=wr_t[:], data0=rho_b, data1=wr_t[:], initial=0.0, op0=ALU.mult, op1=ALU.add), reads=["s5_rho", "s5_wr"], writes=["s5_wr"])
                P.op("dve", lambda e, rho_b=rho_b: e.tensor_tensor_scan(out=wi_t[:], data0=rho_b, data1=wi_t[:], initial=0.0, op0=ALU.mult, op1=ALU.add), reads=["s5_rho", "s5_wi"], writes=["s5_wi"])
                if k.s5_pending is not None:
                    k.s5_pending()
                    k.s5_pending = None
                tt("dve", pr_t[0][:], wr_t[:], cs_t[:], ALU.mult, ["s5_wr", csn], ["s5_pr0"])
                tt("dve", pr_t[1][:], wi_t[:], sn_t[:], ALU.mult, ["s5_wi", snn], ["s5_pr1"])
                tt("dve", pr_t[2][:], wr_t[:], sn_t[:], ALU.mult, ["s5_wr", snn], ["s5_pr2"])
                tt("dve", pr_t[3][:], wi_t[:], cs_t[:], ALU.mult, ["s5_wi", csn], ["s5_pr3"])
                def emit_readout(gl=gl):
                    for ti, (t0, tsz) in enumerate(TT):
                        if t0 < NX:
                            ftau = 256 + t0
                            btau = 2303 - t0
                        else:
                            ftau = 0
                            btau = 255

                        def rd(e, gl=gl, ti=ti, tsz=tsz, ftau=ftau, btau=btau):
                            first = (gl == 0)
                            last = (gl == 7)
                            lt = (0, 1, 2, 2)
                            for pi in range(4):
                                e.matmul(yps[ti][:, :tsz], lhsT=CZ[:, gl, lt[pi], :], rhs=pr_t[pi][:, ftau:ftau + tsz], start=(first and pi == 0), stop=False)
                            for pi in range(4):
                                ins = e.matmul(yps[ti][:, :tsz], lhsT=CZ[:, gl, 3 + lt[pi], :], rhs=sub(pr_t[pi][:], 0, 128, [(-1, tsz)], off=btau), start=False, stop=(last and pi == 3))
                            return ins
                        P.op("pe", rd, reads=["s5_CZ", "s5_pr0", "s5_pr1", "s5_pr2", "s5_pr3"], writes=[ypn[ti]])

                k.s5_pending = emit_readout
            k.s5_pending()
            k.s5_pending = None
            for ti, (t0, tsz) in enumerate(TT):
                t_ = tpf[ti % 2]; tn = "s5_tpf%d" % (ti % 2)
                P.op("dve", lambda e, t_=t_, ti=ti, t0=t0, tsz=tsz, j=j: e.scalar_tensor_tensor(out=t_[:, :tsz], in0=u_t[:, t0:t0 + tsz], scalar=v8[:, 2, j:j + 1], in1=yps[ti][:, :tsz], op0=ALU.mult, op1=ALU.add),
                     reads=["s5_u", "v8", ypn[ti]], writes=[tn])
                P.op("act", lambda e, t_=t_, t0=t0, tsz=tsz, j=j: e.activation(out=Y[:, j, t0:t0 + tsz], in_=t_[:, :tsz], func=AF.Gelu_apprx_tanh), reads=[tn], writes=["s5_Y"])
        P.scope_end(mark)
        mark = P.scope_begin()
        wg = P.sbuf("s5_wg", [128, 8, D], BF16)
        load_w_bf16(wg, "s5_wg", s5_w_glu[l], 8, D)
        sg = [P.sbuf("s5_sg%d" % i, [128, 512], BF16) for i in range(2)]
        ob = [P.sbuf("s5_ob%d" % i, [128, 8, 512], BF16) for i in range(2)]
        for ti, (t0, tsz) in enumerate(TT[:4] if l == depth - 1 else TT):
            o_ = ob[ti % 2]; on = "s5_ob%d" % (ti % 2)
            for ct in range(8):
                ps, psn = next_ps()

                def mm(e, ps=ps, ct=ct, t0=t0, tsz=tsz):
                    for kt in range(8):
                        ins = e.matmul(ps[:, :tsz], lhsT=wg[:, kt, ct * 128:(ct + 1) * 128], rhs=Y[:, kt, t0:t0 + tsz], start=(kt == 0), stop=(kt == 7))
                    return ins
                P.op("pe", mm, reads=["s5_wg", "s5_Y"], writes=[psn])
                s_ = sg[ct % 2]; sn_ = "s5_sg%d" % (ct % 2)
                P.op("act", lambda e, ps=ps, s_=s_, tsz=tsz: e.activation(out=s_[:, :tsz], in_=ps[:, :tsz], func=AF.Sigmoid), reads=[psn], writes=[sn_])
                P.op("dve", lambda e, s_=s_, o_=o_, ct=ct, t0=t0, tsz=tsz: e.tensor_tensor(out=o_[:, ct, :tsz], in0=Y[:, ct, t0:t0 + tsz], in1=s_[:, :tsz], op=ALU.mult), reads=[sn_, "s5_Y"], writes=[on])
            P.dma("sp", S_BR[0][:, t0:t0 + tsz].rearrange("(kt p) t -> p kt t", p=128), o_[:, :, :tsz], reads=[on], writes=["S_BR0"])
        P.scope_end(mark)
        P.scope_end(mark0)

    def dn_phase(l):
        mark = P.scope_begin()
        CH = 128
        NCH = NT // CH
        BQ = 4
        act = lambda out, in_, func, r, w, **kw: P.op("act", lambda e: e.activation(out=out, in_=in_, func=func, **kw), reads=r, writes=w)
        tt = lambda eng, out, a, b, op, r, w: P.op(eng, lambda e: e.tensor_tensor(out=out, in0=a, in1=b, op=op), reads=r, writes=w)
        sel = P.sbuf("dn_sel", [128, 16, 128], F32)
        nsel = P.sbuf("dn_nsel", [128, 16, 128], F32)
        tri = P.sbuf("dn_tri", [128, 8, 128], BF16)
        negm = P.sbuf("dn_negm", [128, 2, 128], F32)
        identf = P.sbuf("dn_identf", [128, 128], F32)
        P.dma("sp", negm[:], negm_in, writes=["dn_negm"])
        P.dma("sp", identf[:], ident_in, writes=["dn_identf"])
        P.op("pool", lambda e: e.memset(sel[:], 0.0), writes=["dn_sel"])
        identb = P.sbuf("dn_ident", [128, 128], BF16)
        cw = P.sbuf("dn_cw", [128, 24, 3], F32)
        pp = P.sbuf("dn_pp", [40, 2], F32)
        nA = P.sbuf("dn_nA", [40, 1], F32)
        one40 = P.sbuf("dn_one", [128, 1], F32)
        nw = P.sbuf("dn_nw", [128, 1], F32)
        P.dma("sp", sel[0:40], sel2_in, writes=["dn_sel"])
        P.dma("pool", tri[:], tri_in, writes=["dn_tri"])
        P.dma("pool", identb[:], ident_in, writes=["dn_ident"])
        P.dma("sp", cw[:], dn_conv_wT[l], writes=["dn_cw"])
        P.dma("sp", pp[:], dnp40[l], writes=["dn_pp"])
        P.dma("sp", nw[:], dn_nw[l], writes=["dn_nw"])
        P.op("pool", lambda e: e.memset(one40[:], 1.0), writes=["dn_one"])
        P.op("dve", lambda e: e.tensor_scalar(out=nsel[:], in0=sel[:], scalar1=-1.0, scalar2=None, op0=ALU.mult), reads=["dn_sel"], writes=["dn_nsel"])
        act(nA[:], pp[:, 0:1], AF.Exp, ["dn_pp"], ["dn_nA"])
        P.op("dve", lambda e: e.tensor_scalar(out=nA[:], in0=nA[:], scalar1=-1.0, scalar2=None, op0=ALU.mult), reads=["dn_nA"], writes=["dn_nA"])
        beta = P.sbuf("dn_beta", [128, NT], F32)
        G = P.sbuf("dn_G", [128, NT], F32)
        EG = P.sbuf("dn_EG", [128, NT], F32)
        BEG = P.sbuf("dn_BEG", [128, NT], F32)
        ET = P.sbuf("dn_ET", [128, NT], F32)
        mk_g = P.scope_begin()
        cm = P.sbuf("dn_cm", [40, NT], F32)
        P.dma("sp", cm[:], cmask_in, writes=["dn_cm"])
        P.op("pool", lambda e: e.memset(beta[:], 0.0), writes=["dn_beta"])
        P.op("pool", lambda e: e.memset(ET[:], 0.0), writes=["dn_ET"])
        P.op("pool", lambda e: e.memset(G[:], 0.0), writes=["dn_G"])
        P.op("pool", lambda e: e.memset(EG[:], 0.0), writes=["dn_EG"])
        P.op("pool", lambda e: e.memset(BEG[:], 0.0), writes=["dn_BEG"])
        for d in range(2):
            P.dma("sp", beta[32 * d:32 * d + 8, :], S_P[COL_BETA + 8 * d:COL_BETA + 8 * d + 8, :], reads=["S_P"], writes=["dn_beta"])
            P.dma("sp", ET[32 * d:32 * d + 8, :], S_P[COL_DECAY + 8 * d:COL_DECAY + 8 * d + 8, :], reads=["S_P"], writes=["dn_ET"])
        act(beta[0:40], beta[0:40], AF.Sigmoid, ["dn_beta"], ["dn_beta"])
        act(ET[0:40], ET[0:40], AF.Exp, ["dn_ET", "dn_pp"], ["dn_ET"], bias=pp[:, 1:2])
        act(ET[0:40], ET[0:40], AF.Ln, ["dn_ET", "dn_one"], ["dn_ET"], bias=one40[0:40, 0:1])
        P.op("dve", lambda e: e.tensor_scalar(out=ET[0:40], in0=ET[0:40], scalar1=nA[:, 0:1], scalar2=None, op0=ALU.mult), reads=["dn_ET", "dn_nA"], writes=["dn_ET"])
        P.op("dve", lambda e: e.tensor_tensor_scan(out=G[0:8, :], data0=cm[0:8, :], data1=ET[0:8, :], initial=0.0, op0=ALU.mult, op1=ALU.add), reads=["dn_cm", "dn_ET"], writes=["dn_G"])
        rv = lambda t: sub(t[:], 32, 8, [(-1, NT)], off=NT - 1)
        P.op("dve", lambda e: e.tensor_tensor_scan(out=rv(G), data0=rv(cm), data1=rv(ET), initial=0.0, op0=ALU.mult, op1=ALU.add), reads=["dn_cm", "dn_ET"], writes=["dn_G"])
        act(EG[0:40], G[0:40], AF.Exp, ["dn_G"], ["dn_EG"])
        tt("dve", BEG[0:40], beta[0:40], EG[0:40], ALU.mult, ["dn_beta", "dn_EG"], ["dn_BEG"])
        for d in range(2):
            lastoff = CH - 1 if d == 0 else 0
            P.op("dve", lambda e, d=d, lastoff=lastoff: e.tensor_tensor(out=sub(ET[:], 32 * d, 8, [(CH, NCH), (1, CH)]), in0=sub(G[:], 32 * d, 8, [(CH, NCH), (0, CH)], off=lastoff),
                                                                      in1=sub(G[:], 32 * d, 8, [(CH, NCH), (1, CH)]), op=ALU.subtract), reads=["dn_G"], writes=["dn_ET"])
        act(ET[0:40], ET[0:40], AF.Exp, ["dn_ET"], ["dn_ET"])
        P.scope_end(mk_g)
        qn = P.sbuf("dn_qn", [128, NT], BF16)
        kn = P.sbuf("dn_kn", [128, NT], BF16)
        vn = P.sbuf("dn_vn", [128, NT], BF16)
        rin = P.sbuf("dn_rin", [128, 512], F32)
        sqt = P.sbuf("dn_sq", [128, 512], F32)
        O = P.sbuf("dn_O", [128, NT], F32)
        ob = P.sbuf("dn_ob", [128, NT], BF16)
        pd = [P.psum("dn_pd%d" % i, [128, 512], F32) for i in range(5)]
        pdn = ["dn_pd%d" % i for i in range(5)]

        def tri_b(m, nb):
            return sub(tri[:], 0, 128, [(0, nb), (1, CH)], off=m * CH)

        for h in range(8):
            mk1 = P.scope_begin()
            raw = [P.sbuf("dn_raw%d" % i, [128, NT], F32) for i in range(2)]
            cv_ = [P.sbuf("dn_cv%d" % i, [128, NT], F32) for i in range(3)]
            HOX, HOC, HXW = 1, 1 + NX + 2, 1 + NX + 2 + NCX + 1
            hxp = [P.sbuf("dn_hxp%d" % i, [128, HXW], BF16) for i in range(2)]
            hdg = [P.sbuf("dn_hdg%d" % i, [128, 3, 128], BF16) for i in range(2)]
            for i in range(2):
                P.op("pool", lambda e, i=i: e.memset(hxp[i][:], 0.0), writes=["dn_hxp%d" % i])
            for wi_ in range(3):
                r_ = raw[wi_ % 2]; rn = "dn_raw%d" % (wi_ % 2)
                c_ = cv_[wi_]; cn = "dn_cv%d" % wi_
                xp = hxp[wi_ % 2]; xpn = "dn_hxp%d" % (wi_ % 2)
                dg = hdg[wi_ % 2]; dgn = "dn_hdg%d" % (wi_ % 2)
                ch = wi_ * 8 + h
                r0 = COL_QKV + wi_ * D + 128 * h
                P.dma("sp", r_[:], S_P[r0:r0 + 128, :], reads=["S_P"], writes=[rn])
                P.op("act", lambda e, r_=r_, xp=xp: e.copy(out=xp[:, HOX:HOX + NX], in_=r_[:, 0:NX]), reads=[rn], writes=[xpn])
                P.op("dve", lambda e, r_=r_, xp=xp: e.tensor_copy(out=xp[:, HOC:HOC + NCX], in_=r_[:, NX:NT]), reads=[rn], writes=[xpn])

                def mkdiag(e, dg=dg, ch=ch):
                    for kk in range(3):
                        ins = e.activation(out=dg[:, kk, :], in_=identf[:], func=AF.Copy, scale=cw[:, ch, kk:kk + 1])
                    return ins
                P.op("act", mkdiag, reads=["dn_identf", "dn_cw"], writes=[dgn])
                for (t0, tsz) in TT:
                    base = (HOX + t0) if t0 < NX else HOC
                    ps, psn = next_ps()

                    def cmm(e, ps=ps, dg=dg, xp=xp, base=base, tsz=tsz):
                        for kk in range(3):
                            ins = e.matmul(ps[:, :tsz], lhsT=dg[:, kk, :], rhs=xp[:, base + kk - 1:base + kk - 1 + tsz], start=(kk == 0), stop=(kk == 2))
                        return ins
                    P.op("pe", cmm, reads=[dgn, xpn], writes=[psn])
                    act(c_[:, t0:t0 + tsz], ps[:, :tsz], AF.Silu, [psn], [cn])
            for wi_, dst, dn_, scl in ((0, qn, "dn_qn", 128.0 ** -0.5), (1, kn, "dn_kn", 1.0)):
                c_ = cv_[wi_]; cn = "dn_cv%d" % wi_
                for (t0, tsz) in TT:
                    act(sqt[:, :tsz], c_[:, t0:t0 + tsz], AF.Square, [cn], ["dn_sq"])
                    ps, psn = next_ps()
                    P.op("pe", lambda e, ps=ps, tsz=tsz: e.matmul(ps[:, :tsz], lhsT=ones32[:], rhs=sqt[:, :tsz], start=True, stop=True), reads=["dn_sq", "ones32"], writes=[psn])
                    act(rin[:, :tsz], ps[:, :tsz], AF.Sqrt, [psn, "eps_t"], ["dn_rin"], bias=k.eps_t[:, 0:1])
                    P.op("dve", lambda e, tsz=tsz: e.reciprocal(out=rin[:, :tsz], in_=rin[:, :tsz]), reads=["dn_rin"], writes=["dn_rin"])
                    P.op("dve", lambda e, c_=c_, dst=dst, scl=scl, t0=t0, tsz=tsz: e.scalar_tensor_tensor(out=dst[:, t0:t0 + tsz], in0=c_[:, t0:t0 + tsz], scalar=scl, in1=rin[:, :tsz], op0=ALU.mult, op1=ALU.mult),
                         reads=[cn, "dn_rin"], writes=[dn_])
            P.op("act", lambda e: e.copy(out=vn[:], in_=cv_[2][:]), reads=["dn_cv2"], writes=["dn_vn"])
            P.scope_end(mk1)
            mk2 = P.scope_begin()
            tokv = [P.sbuf("dn_tokv%d" % d, [128, NCH, 128], BF16) for d in range(2)]
            tokt = [P.sbuf("dn_tokt%d" % d, [128, NCH, 128], BF16) for d in range(2)]
            tokkb = [P.sbuf("dn_tokk%d" % d, [128, BQ, 128], BF16) for d in range(2)]
            TtAll = [P.sbuf("dn_TtAll%d" % d, [128, NT], BF16) for d in range(2)]
            Aqk = [P.sbuf("dn_Aqk%d" % d, [128, NT], BF16) for d in range(2)]
            wTn = [P.sbuf("dn_wTn%d" % d, [128, NT], BF16) for d in range(2)]
            qg = [P.sbuf("dn_qg%d" % d, [128, NT], BF16) for d in range(2)]
            sdcol = [P.sbuf("dn_sd%d" % d, [128, NCH], F32) for d in range(2)]
            tmpbs = [[P.sbuf("dn_tmpb%d_%d" % (d, i), [128, 512], BF16) for i in range(3)] for d in range(2)]
            W1s = [P.sbuf("dn_W1_%d" % d, [128, 512], F32) for d in range(2)]
            Mxs = [P.sbuf("dn_Mx_%d" % d, [128, 512], F32) for d in range(2)]
            matss = [{nm: [P.sbuf("dn_%s%d_%d" % (nm, i, d), [128, 512], BF16) for i in range(2)] for nm in ("A", "Bt", "T", "Tt")} for d in range(2)]
            A0fs = [P.sbuf("dn_A0f%d" % d, [128, 512], BF16) for d in range(2)]
            Bt0fs = [P.sbuf("dn_Bt0f%d" % d, [128, 512], BF16) for d in range(2)]
            Sf = [P.sbuf("dn_S%d" % d, [128, 128], F32) for d in range(2)]
            Sb = [P.sbuf("dn_Sb%d" % d, [128, 128], BF16) for d in range(2)]
            vnew = [P.sbuf("dn_vnew%d" % d, [128, 128], BF16) for d in range(2)]
            P.op("pool", lambda e: e.memset(O[:], 0.0), writes=["dn_O"])
            banks = [[pd[0], pd[1], pd[2], pd[3]], [pd[4], ps_lin[0], ps_lin[1], ps_lin[2]]]
            bankn = [[pdn[0], pdn[1], pdn[2], pdn[3]], [pdn[4], "ps_lin0", "ps_lin1", "ps_lin2"]]

            def batch_gen(d, t0, tsz):
                dh = 8 * d + h
                m_nstrT = 1 if d == 0 else 3
                lastoff = CH - 1 if d == 0 else 0
                B = banks[d]; Bn = bankn[d]
                nb = tsz // CH
                n0 = t0 // CH
                W = tsz
                S_ = "_%d" % d
                tmpb = tmpbs[d]; W1 = W1s[d]; Mx = Mxs[d]; mats = matss[d]; A0f = A0fs[d]; Bt0f = Bt0fs[d]; tokk = tokkb[d]
                mn = lambda nm, i: "dn_%s%d_%d" % (nm, i, d)
                v3 = lambda t: t[:, :W].rearrange("p (c i) -> p c i", c=nb)
                dsts = (tokv[d], tokk, tokt[d])
                dstn = ("dn_tokv%d" % d, "dn_tokk%d" % d, "dn_tokt%d" % d)
                for xi, (rows, srcT, sname) in enumerate(((beta, vn, "dn_vn"), (BEG, kn, "dn_kn"), (ET, kn, "dn_kn"))):
                    P.op("pe", lambda e, xi=xi, rows=rows: e.matmul(B[xi][:, :tsz], lhsT=sel[:, dh, :], rhs=rows[:, t0:t0 + tsz], start=True, stop=True),
                         reads=["dn_sel", "dn_beta", "dn_BEG", "dn_ET"], writes=[Bn[xi]])
                P.op("pe", lambda e: e.matmul(B[3][:, :tsz], lhsT=sel[:, dh, :], rhs=EG[:, t0:t0 + tsz], start=True, stop=True), reads=["dn_sel", "dn_EG"], writes=[Bn[3]])
                yield
                for xi, (rows, srcT, sname) in enumerate(((beta, vn, "dn_vn"), (BEG, kn, "dn_kn"), (ET, kn, "dn_kn"))):
                    tt("dve", tmpb[xi][:, :tsz], srcT[:, t0:t0 + tsz], B[xi][:, :tsz], ALU.mult, [sname, Bn[xi]], ["dn_tmpb%d_%d" % (d, xi)])
                tt("dve", qg[d][:, t0:t0 + tsz], qn[:, t0:t0 + tsz], B[3][:, :tsz], ALU.mult, ["dn_qn", Bn[3]], ["dn_qg%d" % d])
                P.op("dve", lambda e: e.tensor_copy(out=sdcol[d][:, n0:n0 + nb], in_=sub(B[3][:], 0, 128, [(CH, nb)], off=lastoff)), reads=[Bn[3]], writes=["dn_sd%d" % d])
                yield
                for xi in range(3):
                    def tr(e, xi=xi):
                        for c in range(nb):
                            ins = e.matmul(B[xi][:, c * 128:(c + 1) * 128], lhsT=tmpb[xi][:, c * CH:(c + 1) * CH], rhs=identb[:], start=True, stop=True)
                        return ins
                    P.op("pe", tr, reads=["dn_tmpb%d_%d" % (d, xi), "dn_ident"], writes=[Bn[xi]])
                yield
                for xi in range(3):
                    o_ = dsts[xi][:, n0:n0 + nb, :] if xi != 1 else dsts[xi][:, 0:nb, :]
                    i_ = B[xi][:, :nb * 128].rearrange("p (c d) -> p c d", c=nb)
                    if xi != 1:
                        P.op("act", lambda e, o_=o_, i_=i_: e.copy(out=o_, in_=i_), reads=[Bn[xi]], writes=[dstn[xi]])
                    else:
                        P.op("dve", lambda e, o_=o_, i_=i_: e.tensor_copy(out=o_, in_=i_), reads=[Bn[xi]], writes=[dstn[xi]])
                yield

                def mats_mm(e):
                    for c in range(nb):
                        cs = slice((n0 + c) * CH, (n0 + c + 1) * CH)
                        os_ = slice(c * CH, (c + 1) * CH)
                        e.matmul(B[0][:, os_], lhsT=sel[:, dh, :], rhs=G[:, cs], start=True, stop=False)
                        e.matmul(B[0][:, os_], lhsT=G[:, cs], rhs=nsel[:, dh, :], start=False, stop=False)
                        e.matmul(B[0][:, os_], lhsT=identf[:], rhs=negm[:, d, :], start=False, stop=True)
                        e.matmul(B[1][:, os_], lhsT=kn[:, cs], rhs=kn[:, cs], start=True, stop=True)
                        e.matmul(B[2][:, os_], lhsT=kn[:, cs], rhs=qn[:, cs], start=True, stop=True)
                        ins = e.matmul(B[3][:, os_], lhsT=sel[:, dh, :], rhs=beta[:, cs], start=True, stop=True)
                    return ins
                P.op("pe", mats_mm, reads=["dn_sel", "dn_nsel", "dn_G", "dn_kn", "dn_qn", "dn_beta", "dn_negm", "dn_identf"], writes=Bn)
                yield
                act(W1[:, :W], B[0][:, :W], AF.Exp, [Bn[0]], ["dn_W1" + S_])
                yield
                tt("dve", Mx[:, :W], B[1][:, :W], W1[:, :W], ALU.mult, [Bn[1], "dn_W1" + S_], ["dn_Mx" + S_])
                tt("dve", Mx[:, :W], B[3][:, :W], Mx[:, :W], ALU.mult, [Bn[3], "dn_Mx" + S_], ["dn_Mx" + S_])
                tt("dve", v3(Bt0f), v3(Mx), tri_b(m_nstrT, nb), ALU.mult, ["dn_Mx" + S_, "dn_tri"], ["dn_Bt0f" + S_])
                tt("dve", Aqk[d][:, n0 * CH:n0 * CH + W], B[2][:, :W], W1[:, :W], ALU.mult, [Bn[2], "dn_W1" + S_], ["dn_Aqk%d" % d])
                yield

                def trA(e):
                    for c in range(nb):
                        os_ = slice(c * CH, (c + 1) * CH)
                        ins = e.matmul(B[0][:, os_], lhsT=Bt0f[:, os_], rhs=identb[:], start=True, stop=True)
                    return ins
                P.op("pe", trA, reads=["dn_Bt0f" + S_, "dn_ident"], writes=[Bn[0]])
                tt("dve", v3(mats["Bt"][0]), v3(Bt0f), tri_b(5, nb), ALU.mult, ["dn_Bt0f" + S_, "dn_tri"], [mn("Bt", 0)])
                yield
                P.op("act", lambda e: e.copy(out=A0f[:, :W], in_=B[0][:, :W]), reads=[Bn[0]], writes=["dn_A0f" + S_])
                tt("pool", v3(mats["Tt"][0]), v3(mats["Bt"][0]), tri_b(4, nb), ALU.add, [mn("Bt", 0), "dn_tri"], [mn("Tt", 0)])
                yield
                tt("dve", v3(mats["A"][0]), v3(A0f), tri_b(5, nb), ALU.mult, ["dn_A0f" + S_, "dn_tri"], [mn("A", 0)])
                yield
                tt("pool", v3(mats["T"][0]), v3(mats["A"][0]), tri_b(4, nb), ALU.add, [mn("A", 0), "dn_tri"], [mn("T", 0)])
                yield
                NLV = 4
                for lv in range(NLV):
                    a_, b_ = lv % 2, (lv + 1) % 2
                    A_, Bt_, T_, Tt_ = mats["A"][a_], mats["Bt"][a_], mats["T"][a_], mats["Tt"][a_]
                    An, Btn, Tn, Ttn = mats["A"][b_], mats["Bt"][b_], mats["T"][b_], mats["Tt"][b_]

                    def sq(e, A_=A_, Bt_=Bt_):
                        for c in range(nb):
                            os_ = slice(c * CH, (c + 1) * CH)
                            e.matmul(B[0][:, os_], lhsT=Bt_[:, os_], rhs=A_[:, os_], start=True, stop=True)
                            ins = e.matmul(B[1][:, os_], lhsT=A_[:, os_], rhs=Bt_[:, os_], start=True, stop=True)
                        return ins
                    P.op("pe", sq, reads=[mn("A", a_), mn("Bt", a_)], writes=[Bn[0], Bn[1]])
                    yield
                    P.op("act", lambda e, An=An: e.copy(out=An[:, :W], in_=B[0][:, :W]), reads=[Bn[0]], writes=[mn("A", b_)])
                    P.op("dve", lambda e, Btn=Btn: e.tensor_copy(out=Btn[:, :W], in_=B[1][:, :W]), reads=[Bn[1]], writes=[mn("Bt", b_)])
                    yield

                    def pr(e, T_=T_, Tt_=Tt_, An=An, Btn=Btn):
                        for c in range(nb):
                            os_ = slice(c * CH, (c + 1) * CH)
                            e.matmul(B[2][:, os_], lhsT=T_[:, os_], rhs=Btn[:, os_], start=True, stop=False)
                            e.matmul(B[2][:, os_], lhsT=identb[:], rhs=Tt_[:, os_], start=False, stop=True)
                            e.matmul(B[3][:, os_], lhsT=Tt_[:, os_], rhs=An[:, os_], start=True, stop=False)
                            ins = e.matmul(B[3][:, os_], lhsT=identb[:], rhs=T_[:, os_], start=False, stop=True)
                        return ins
                    P.op("pe", pr, reads=[mn("T", a_), mn("Tt", a_), mn("A", b_), mn("Bt", b_), "dn_ident"], writes=[Bn[2], Bn[3]])
                    yield
                    P.op("act", lambda e, Ttn=Ttn: e.copy(out=Ttn[:, :W], in_=B[2][:, :W]), reads=[Bn[2]], writes=[mn("Tt", b_)])
                    P.op("act", lambda e, Tn=Tn: e.copy(out=Tn[:, :W], in_=B[3][:, :W]), reads=[Bn[3]], writes=[mn("T", b_)])
                    yield
                cur = NLV % 2
                for bi, mk_ in enumerate((6, 7)):
                    oth = 1 - cur
                    T_, Tt_ = mats["T"][cur], mats["Tt"][cur]
                    Tn, Ttn = mats["T"][oth], mats["Tt"][oth]
                    Ao, Bo, Pm, Rm = mats["A"][0], mats["Bt"][0], mats["A"][1], mats["Bt"][1]
                    tt("dve", v3(Ao), v3(A0f), tri_b(mk_, nb), ALU.mult, ["dn_A0f" + S_, "dn_tri"], [mn("A", 0)])
                    tt("pool", v3(Bo), v3(Bt0f), tri_b(mk_, nb), ALU.mult, ["dn_Bt0f" + S_, "dn_tri"], [mn("Bt", 0)])
                    yield

                    def b1(e, T_=T_, Tt_=Tt_, Ao=Ao, Bo=Bo):
                        for c in range(nb):
                            os_ = slice(c * CH, (c + 1) * CH)
                            e.matmul(B[0][:, os_], lhsT=Bo[:, os_], rhs=T_[:, os_], start=True, stop=True)
                            ins = e.matmul(B[1][:, os_], lhsT=Ao[:, os_], rhs=Tt_[:, os_], start=True, stop=True)
                        return ins
                    P.op("pe", b1, reads=[mn("A", 0), mn("Bt", 0), mn("T", cur), mn("Tt", cur)], writes=[Bn[0], Bn[1]])
                    yield
                    P.op("act", lambda e, Pm=Pm: e.copy(out=Pm[:, :W], in_=B[0][:, :W]), reads=[Bn[0]], writes=[mn("A", 1)])
                    P.op("dve", lambda e, Rm=Rm: e.tensor_copy(out=Rm[:, :W], in_=B[1][:, :W]), reads=[Bn[1]], writes=[mn("Bt", 1)])
                    yield

                    def b2(e, T_=T_, Tt_=Tt_, Pm=Pm, Rm=Rm, bi=bi):
                        for c in range(nb):
                            os_ = slice(c * CH, (c + 1) * CH)
                            e.matmul(B[3][:, os_], lhsT=T_[:, os_], rhs=Rm[:, os_], start=True, stop=False)
                            ins = e.matmul(B[3][:, os_], lhsT=identb[:], rhs=Tt_[:, os_], start=False, stop=True)
                            if bi == 0:
                                e.matmul(B[2][:, os_], lhsT=Tt_[:, os_], rhs=Pm[:, os_], start=True, stop=False)
                                ins = e.matmul(B[2][:, os_], lhsT=identb[:], rhs=T_[:, os_], start=False, stop=True)
                        return ins
                    P.op("pe", b2, reads=[mn("A", 1), mn("Bt", 1), mn("T", cur), mn("Tt", cur), "dn_ident"], writes=[Bn[2], Bn[3]])
                    yield
                    if bi == 0:
                        P.op("act", lambda e, Ttn=Ttn: e.copy(out=Ttn[:, :W], in_=B[3][:, :W]), reads=[Bn[3]], writes=[mn("Tt", oth)])
                        P.op("act", lambda e, Tn=Tn: e.copy(out=Tn[:, :W], in_=B[2][:, :W]), reads=[Bn[2]], writes=[mn("T", oth)])
                    else:
                        P.op("act", lambda e: e.copy(out=TtAll[d][:, n0 * CH:n0 * CH + W], in_=B[3][:, :W]), reads=[Bn[3]], writes=["dn_TtAll%d" % d])
                    yield
                    cur = oth

                def wmm(e):
                    for c in range(nb):
                        cs = slice((n0 + c) * CH, (n0 + c + 1) * CH)
                        ins = e.matmul(B[0][:, c * CH:(c + 1) * CH], lhsT=tokk[:, c, :], rhs=TtAll[d][:, cs], start=True, stop=True)
                    return ins
                P.op("pe", wmm, reads=["dn_tokk%d" % d, "dn_TtAll%d" % d], writes=[Bn[0]])
                yield
                P.op("act", lambda e: e.activation(out=wTn[d][:, n0 * CH:n0 * CH + W], in_=B[0][:, :W], func=AF.Copy, scale=-1.0), reads=[Bn[0]], writes=["dn_wTn%d" % d])
                yield

            for (t0, tsz) in TT:
                gens = [batch_gen(0, t0, tsz), batch_gen(1, t0, tsz)]
                alive = [True, True]
                while any(alive):
                    for gi_ in range(2):
                        if alive[gi_]:
                            try:
                                next(gens[gi_])
                            except StopIteration:
                                alive[gi_] = False
            orders = [[16, 17] + list(range(16)), [17, 16] + list(range(15, -1, -1))]
            for d in range(2):
                P.op("pool", lambda e, d=d: e.memset(Sf[d][:], 0.0), writes=["dn_S%d" % d])
                P.op("pool", lambda e, d=d: e.memset(Sb[d][:], 0.0), writes=["dn_Sb%d" % d])
            for si in range(NCH):
                for d in range(2):
                    n = orders[d][si]
                    cs = slice(n * CH, (n + 1) * CH)
                    bank = pd[d * 2 + si % 2]
                    bn = pdn[d * 2 + si % 2]
                    p1, p2, p3 = bank[:, 0:128], bank[:, 128:256], bank[:, 256:384]
                    rS, rSb, rv_ = "dn_S%d" % d, "dn_Sb%d" % d, "dn_vnew%d" % d

                    def m1(e, p1=p1, n=n, cs=cs, d=d):
                        e.matmul(p1, lhsT=TtAll[d][:, cs], rhs=tokv[d][:, n, :], start=True, stop=False)
                        return e.matmul(p1, lhsT=wTn[d][:, cs], rhs=Sb[d][:], start=False, stop=True)
                    P.op("pe", m1, reads=["dn_TtAll%d" % d, "dn_tokv%d" % d, "dn_wTn%d" % d, rSb], writes=[bn + "a"])
                    P.op("act", lambda e, p1=p1, d=d: e.copy(out=vnew[d][:], in_=p1), reads=[bn + "a"], writes=[rv_])

                    def m2(e, p2=p2, p3=p3, n=n, cs=cs, d=d):
                        e.matmul(p2, lhsT=tokt[d][:, n, :], rhs=vnew[d][:], start=True, stop=True)
                        e.matmul(p3, lhsT=Sb[d][:], rhs=qg[d][:, cs], start=True, stop=False)
                        return e.matmul(p3, lhsT=vnew[d][:], rhs=Aqk[d][:, cs], start=False, stop=True)
                    P.op("pe", m2, reads=["dn_tokt%d" % d, rv_, rSb, "dn_qg%d" % d, "dn_Aqk%d" % d], writes=[bn + "b"])
                    P.op("dve", lambda e, p2=p2, n=n, d=d: e.scalar_tensor_tensor(out=Sb[d][:], in0=Sf[d][:], scalar=sdcol[d][:, n:n + 1], in1=p2, op0=ALU.mult, op1=ALU.add), reads=[bn + "b", "dn_sd%d" % d, rS], writes=[rSb])
                    P.op("dve", lambda e, p2=p2, n=n, d=d: e.scalar_tensor_tensor(out=Sf[d][:], in0=Sf[d][:], scalar=sdcol[d][:, n:n + 1], in1=p2, op0=ALU.mult, op1=ALU.add), reads=[bn + "b", "dn_sd%d" % d, rS], writes=[rS])
                    tt("dve", O[:, cs], p3, O[:, cs], ALU.add, [bn + "b", "dn_O"], ["dn_O"])
            P.scope_end(mk2)
            mk3 = P.scope_begin()
            zt = P.sbuf("dn_z", [128, NT], F32)
            P.dma("sp", zt[:], S_P[COL_Z + 128 * h:COL_Z + 128 * (h + 1), :], reads=["S_P"], writes=["dn_z"])
            act(zt[:], zt[:], AF.Silu, ["dn_z"], ["dn_z"])
            if "S_O" in debug:
                P.dma("sp", S_O[128 * h:128 * (h + 1), :], O[:], reads=["dn_O"], writes=["S_O"])
            for (t0, tsz) in TT:
                act(sqt[:, :tsz], O[:, t0:t0 + tsz], AF.Square, ["dn_O"], ["dn_sq"])
                ps, psn = next_ps()
                P.op("pe", lambda e, ps=ps, tsz=tsz: e.matmul(ps[:, :tsz], lhsT=ones32[:], rhs=sqt[:, :tsz], start=True, stop=True), reads=["dn_sq", "ones32"], writes=[psn])
                act(rin[:, :tsz], ps[:, :tsz], AF.Sqrt, [psn, "eps_t"], ["dn_rin"], scale=1.0 / 128.0, bias=k.eps_t[:, 0:1])
                P.op("dve", lambda e, tsz=tsz: e.reciprocal(out=rin[:, :tsz], in_=rin[:, :tsz]), reads=["dn_rin"], writes=["dn_rin"])
                P.op("dve", lambda e, t0=t0, tsz=tsz: e.scalar_tensor_tensor(out=rin[:, :tsz], in0=O[:, t0:t0 + tsz], scalar=nw[:, 0:1], in1=rin[:, :tsz], op0=ALU.mult, op1=ALU.mult), reads=["dn_O", "dn_nw", "dn_rin"], writes=["dn_rin"])
                tt("pool", ob[:, t0:t0 + tsz], rin[:, :tsz], zt[:, t0:t0 + tsz], ALU.mult, ["dn_rin", "dn_z"], ["dn_ob"])
            P.dma("sp", S_BR[1][128 * h:128 * (h + 1), :], ob[:], reads=["dn_ob"], writes=["S_BR1"])
            P.scope_end(mk3)
        P.scope_end(mark)

    k.eps_t = P.sbuf("eps_t", [128, 1], F32)
    P.op("pool", lambda e: e.memset(k.eps_t[:], EPS), writes=["eps_t"])

    final = []
    for l in range(depth):
        modulation(l)
        if stop_after == "mod":
            break
        norm_phase("n1", xT if l == 0 else S_X, "xT" if l == 0 else "S_X", scl1, "scl1", lambda seg, kt: modS[:, seg, kt:kt + 1], S_H, "S_H", True, TT)
        if stop_after == "norm1":
            break
        if "S_P" not in inject:
            linear_to_dram("pin", w_in[l], N_IN, KT, S_H, "S_H", S_P, "S_P", ctx_skip_from=(4224 if l == depth - 1 else None))
        if stop_after == "proj":
            break
        if "S_BR0" not in inject:
            s5_phase(l)
        if stop_after == "s5":
            break
        if "S_BR1" not in inject:
            dn_phase(l)
        if stop_after == "dn":
            break
        if "S_BR2" not in inject:
            cv_phase(l)
        if stop_after == "cv":
            break
        merge_phase(l)
        if stop_after == "merge":
            break
        norm_phase("n2", S_X, "S_X", scl2, "scl2", lambda seg, kt: modS[:, seg, 24 + kt:25 + kt], S_H, "S_H", True, TT[:4] if l == depth - 1 else TT)
        linear_to_dram("fup", ffn_w_up[l], 2 * FH, KT, S_H, "S_H", S_F, "S_F", tts=(TT[:4] if l == depth - 1 else TT))
        mk_wd = P.scope_begin()
        wd_t = P.sbuf("fd_w", [128, FKT, D], BF16)
        load_w_bf16(wd_t, "fd_w", ffn_w_down[l], FKT, D)
        ffn_act_phase(l)
        ffn_down_phase(l, wd_t)
        P.scope_end(mk_wd)
        if stop_after == "ffn":
            break
    else:
        final += norm_phase("nf", S_X, "S_X", fnw_t, "fnw_t", None, outT, "outT", False, TT[:4])

    if "modS" in debug:
        o = nc.dram_tensor("modS_o", [128, 96], F32, kind="ExternalOutput").ap()
        final.append(P.dma("sp", o, modS[:].rearrange("p s c -> p (s c)"), reads=["modS"]))
    for nm in ("S_X", "S_H", "S_P", "S_BR0", "S_BR1", "S_BR2", "S_F", "S_A", "S_O", "outT"):
        t = P.last_w.get(nm)
        if t is not None:
            final.append(t)
    P.finish(final)
    P.emit()
    P.close()
    return nc


def prep_inputs(inputs):
    f = lambda a: np.ascontiguousarray(a, dtype=np.float32)
    x = inputs["x"]
    ctx = inputs["ctx"]
    shared = {}
    shared["ada_w"] = f(inputs["ada_w"])
    shared["ada_bT"] = f(inputs["ada_b"].reshape(DEPTH, 48, 128).transpose(0, 2, 1))
    v = np.stack([inputs[n] for n in ("norm1_w", "norm2_w", "s5_d", "cv_dw_b", "cv_ln_w", "cv_ln_b")], axis=1)
    shared["vec8"] = f(v.reshape(DEPTH, 6, 8, 128).transpose(0, 3, 1, 2))
    shared["fnw"] = f(inputs["final_norm_w"].reshape(8, 128).T)
    shared["w_in"] = f(inputs["w_in"])
    shared["cv_dw_wT"] = f(inputs["cv_dw_w"].reshape(DEPTH, 31, 8, 128).transpose(0, 3, 2, 1))
    for n in ("w_br_s5", "w_br_dn", "w_br_cv", "w_out", "ffn_w_up", "ffn_w_down"):
        shared[n] = f(inputs[n])
    shared["ffn_dw_wT"] = f(inputs["ffn_dw_w"].reshape(DEPTH, 9, FKT, 128).transpose(0, 3, 2, 1))
    shared["ffn_dw_bT"] = f(inputs["ffn_dw_b"].reshape(DEPTH, FKT, 128).transpose(0, 2, 1))
    L = DEPTH
    are, aim, ldt = inputs["s5_a_re"], inputs["s5_a_im"], inputs["s5_log_dt"]
    ldt_b = np.broadcast_to(ldt[:, :, :, None], are.shape)
    aT = np.stack([a.transpose(0, 1, 3, 2).reshape(L, 128, 64) for a in (are, aim, ldt_b)], axis=1)
    shared["s5_aT"] = f(aT)
    def kp_layout(a):
        a5 = a.reshape(L, 2, 8, 8, 64).transpose(0, 2, 3, 1, 4)
        a6 = np.broadcast_to(a5[:, :, :, None, :, :], (L, 8, 8, 16, 2, 64))
        return a6.reshape(L, 8, 128, 128)
    shared["s5_kp"] = f(np.stack([kp_layout(a) for a in (are, aim, ldt_b)], axis=2))
    def b_layout(b):
        b6 = b.reshape(L, 2, 8, 8, 64, 16).transpose(0, 2, 3, 5, 1, 4)
        return b6.reshape(L, 8, 128, 128)
    shared["s5_bT"] = f(np.stack([b_layout(inputs["s5_b_re"]), b_layout(inputs["s5_b_im"])], axis=2))
    def c_layout(c):
        c3 = c.transpose(0, 3, 1, 2).reshape(L, 64, 1024)
        return np.concatenate([c3, c3], axis=1)
    shared["s5_cT"] = f(np.stack([c_layout(inputs["s5_c_re"]), c_layout(inputs["s5_c_im"])], axis=1))
    shared["s5_w_glu"] = f(inputs["s5_w_glu"])
    shared["gmask"] = f((np.arange(128)[:, None] // 16 == np.arange(8)[None, :]).astype(np.float32))
    shared["dn_conv_wT"] = f(inputs["dn_conv_w"].reshape(L, 3, 24, 128).transpose(0, 3, 2, 1))
    dnp = np.zeros((L, 40, 2), np.float32)
    for d_ in range(2):
        dnp[:, 32 * d_:32 * d_ + 8, 0] = inputs["dn_a_log"][:, d_, :]
        dnp[:, 32 * d_:32 * d_ + 8, 1] = inputs["dn_dt_bias"][:, d_, :]
    shared["dnp40"] = dnp
    shared["dn_nw"] = f(inputs["dn_norm_w"].reshape(L, 128, 1))
    sel2 = np.zeros((40, 16, 128), np.float32)
    for d_ in range(2):
        for h_ in range(8):
            sel2[32 * d_ + h_, 8 * d_ + h_, :] = 1.0
    shared["sel2"] = sel2
    cmk = np.ones((40, NT), np.float32)
    cmk[0:8, 0::128] = 0.0
    cmk[32:40, 127::128] = 0.0
    shared["cmask"] = cmk
    pi_ = np.arange(128)[:, None]; fi_ = np.arange(128)[None, :]
    tri_ = np.stack([fi_ >= pi_, fi_ > pi_, fi_ <= pi_, fi_ < pi_, fi_ == pi_,
                     pi_ // 32 == fi_ // 32, (pi_ // 64 == fi_ // 64) & (pi_ // 32 != fi_ // 32), pi_ // 64 != fi_ // 64], axis=1).astype(np.float32)
    tri_[:, 1, :] *= -1.0
    tri_[:, 3, :] *= -1.0
    shared["tri"] = f(tri_)
    shared["ident"] = f(np.eye(128, dtype=np.float32))
    shared["negm"] = f(np.stack([np.where(fi_ >= pi_, 0.0, -30000.0), np.where(fi_ <= pi_, 0.0, -30000.0)], axis=1))
    shared["iota"] = f(np.broadcast_to(np.arange(NT, dtype=np.float32), (128, NT)))
    in_maps = []
    for b in range(8):
        m = dict(shared)
        m["xT"] = f(np.concatenate([x[b].T, ctx[b].T], axis=1))
        cv = np.stack([inputs["c"][b].reshape(8, 128).T, inputs["c_ctx"].reshape(8, 128).T], axis=2)
        m["cvec"] = f(cv.reshape(128, 16))
        in_maps.append(m)
    return in_maps


def kernel(**inputs):
    in_maps = prep_inputs(inputs)
    nc = build()
    res = run_bass_kernel_spmd(nc, in_maps, core_ids=list(range(8)))
    out = np.stack([r["outT"].T for r in res.results], axis=0)
    return np.ascontiguousarray(out, dtype=np.float32)
```
